# Optimizing a Trainium2 kernel written in Bass

```python
import math
import jax
import jax.numpy as jnp
from jax import lax
import numpy as np

D_MODEL = 2048
BATCH = 16
SEQ = 2048
DEPTH = 2

GRID_W = 64
CTX_LEN = 256
N_MIXERS = 2
N_A = (DEPTH + 1) // 2
N_B = DEPTH // 2
NORM_EPS = 1e-6
D_FF = 4 * D_MODEL

ML_HEADS = 8
ML_DV = D_MODEL // ML_HEADS
ML_DK = ML_DV // 2
ML_QK = ML_HEADS * ML_DK
ML_V = ML_HEADS * ML_DV
ML_IN = 2 * ML_QK + 2 * ML_V + 4 * ML_HEADS
ML_CHUNK = 64
IGATE_BIAS = -2.0
FGATE_LO = 3.0
FGATE_HI = 6.0

DA_HEADS = 8
DA_DH = D_MODEL // (2 * DA_HEADS)
DA_Q = DA_HEADS * 2 * DA_DH
DA_IN = 3 * DA_Q
Q_BLOCK = 128
ROPE_BASE = 10000.0

kernel_name = 'hybrid_mlstm_diffattn_dit_block'

F32 = jnp.float32


def rmsnorm(x, g):
    xf = x.astype(F32)
    y = xf * lax.rsqrt(jnp.mean(xf * xf, axis=-1, keepdims=True) + NORM_EPS)
    return (y * g.astype(F32)).astype(x.dtype)


def ada_mod(cond, w, b):
    m = jax.nn.silu(cond) @ w + b
    return jnp.split(m[..., None, :], 6, axis=-1)


def modulate(x, g, shift, scale):
    return rmsnorm(x, g) * (1 + scale) + shift


def sq_relu_mlp(h, w1, w2):
    return jnp.square(jax.nn.relu(h @ w1)) @ w2


def axial_rope(t, n_lat):
    rows = n_lat // GRID_W
    row = jnp.repeat(jnp.arange(rows, dtype=F32), GRID_W)
    col = jnp.tile(jnp.arange(GRID_W, dtype=F32), rows)
    nf = DA_DH // 4
    freq = ROPE_BASE ** (-jnp.arange(nf, dtype=F32) / nf)
    ang = jnp.stack([row[:, None] * freq, col[:, None] * freq], axis=1)
    cos = jnp.cos(ang)[:, None, None]
    sin = jnp.sin(ang)[:, None, None]
    tf = t.astype(F32).reshape(*t.shape[:-1], 2, 2, nf)
    t1, t2 = tf[..., 0, :], tf[..., 1, :]
    out = jnp.stack([t1 * cos - t2 * sin, t2 * cos + t1 * sin], axis=-2)
    return out.reshape(t.shape).astype(t.dtype)


def mlstm_zero_state(B):
    return (jnp.zeros((B, ML_HEADS, ML_DK, ML_DV), F32),
            jnp.zeros((B, ML_HEADS, ML_DK), F32),
            jnp.zeros((B, ML_HEADS), F32))


def mlstm_chunk_scan(q, k, v, ig, lf, state):
    B, H, T, _ = q.shape
    nc = T // ML_CHUNK

    def to_chunks(a):
        return jnp.moveaxis(a.reshape(B, H, nc, ML_CHUNK, *a.shape[3:]), 2, 0)

    xs = (to_chunks(q), to_chunks(k), to_chunks(v), to_chunks(ig), to_chunks(lf))
    lower = jnp.tril(jnp.ones((ML_CHUNK, ML_CHUNK), dtype=bool))

    def step(carry, inp):
        C, n, m = carry
        qc, kc, vc, igc, lfc = inp
        b = jnp.cumsum(lfc, axis=-1)
        d = b[..., :, None] - b[..., None, :] + igc[..., None, :]
        d = jnp.where(lower, d, -jnp.inf)
        inter = b + m[..., None]
        m_t = jnp.maximum(inter, jnp.max(d, axis=-1))
        w_inter = jnp.exp(inter - m_t)
        s = jnp.einsum('bhtd,bhsd->bhts', qc, kc) * jnp.exp(d - m_t[..., None])
        num = w_inter[..., None] * jnp.einsum('bhtd,bhde->bhte', qc, C) + jnp.einsum('bhts,bhse->bhte', s, vc)
        den = w_inter * jnp.einsum('bhtd,bhd->bht', qc, n) + jnp.sum(s, axis=-1)
        h = num / jnp.maximum(jnp.abs(den), jnp.exp(-m_t))[..., None]
        b_end = b[..., -1]
        g = b_end[..., None] - b + igc
        m_new = jnp.maximum(b_end + m, jnp.max(g, axis=-1))
        a = jnp.exp(b_end + m - m_new)
        w = jnp.exp(g - m_new[..., None])
        C_new = a[..., None, None] * C + jnp.einsum('bhs,bhsd,bhse->bhde', w, kc, vc)
        n_new = a[..., None] * n + jnp.einsum('bhs,bhsd->bhd', w, kc)
        return (C_new, n_new, m_new), h

    final, h = lax.scan(step, state, xs)
    return jnp.moveaxis(h, 0, 2).reshape(B, H, T, -1), final


def mlstm_project(h, w_in, b_gates):
    B, T, _ = h.shape
    p = (h @ w_in).astype(F32)
    q, k, v, og, g = jnp.split(p, [ML_QK, 2 * ML_QK, 2 * ML_QK + ML_V, 2 * ML_QK + 2 * ML_V], axis=-1)

    def heads(a, dh):
        return a.reshape(B, T, ML_HEADS, dh).transpose(0, 2, 1, 3)

    q = heads(q, ML_DK) * (ML_DK ** -0.5)
    k = heads(k, ML_DK)
    v = heads(v, ML_DV)
    g = (g + b_gates.astype(F32)).reshape(B, T, 4, ML_HEADS).transpose(2, 0, 3, 1)
    return (q, k, v, og, g[0], jax.nn.log_sigmoid(g[1]), g[2], jax.nn.log_sigmoid(g[3]))


def mlstm_bidir(p, state_f, state_b):
    q, k, v, _, ig_f, lf_f, ig_b, lf_b = p
    hf, sf = mlstm_chunk_scan(q, k, v, ig_f, lf_f, state_f)

    def flip(a):
        return jnp.flip(a, axis=2)

    hb, sb = mlstm_chunk_scan(flip(q), flip(k), flip(v), flip(ig_b), flip(lf_b), state_b)
    return hf + flip(hb), sf, sb


def mlstm_out(hs, og, head_g, w_out, dtype):
    B, H, T, _ = hs.shape
    hn = hs * lax.rsqrt(jnp.mean(hs * hs, axis=-1, keepdims=True) + NORM_EPS)
    hn = hn.transpose(0, 2, 1, 3).reshape(B, T, ML_V) * head_g.astype(F32)
    return (hn * jax.nn.sigmoid(og)).astype(dtype) @ w_out


def mlstm_mixer(hl, hc, w_in, b_gates, head_g, w_out, need_ctx):
    pc = mlstm_project(hc, w_in, b_gates)
    pl = mlstm_project(hl, w_in, b_gates)
    z = mlstm_zero_state(hc.shape[0])
    hsc, sf, sb = mlstm_bidir(pc, z, z)
    hsl, _, _ = mlstm_bidir(pl, sf, sb)
    yl = mlstm_out(hsl, pl[3], head_g, w_out, hl.dtype)
    yc = mlstm_out(hsc, pc[3], head_g, w_out, hc.dtype) if need_ctx else None
    return yl, yc


def diff_attention_mixer(hl, hc, w_in, lam_qk, g_sub, w_out, layer_idx, need_ctx):
    lam_init = 0.8 - 0.6 * math.exp(-0.3 * layer_idx)
    lq = lam_qk.astype(F32)
    lam = jnp.exp(jnp.sum(lq[0] * lq[1])) - jnp.exp(jnp.sum(lq[2] * lq[3])) + lam_init
    scale = DA_DH ** -0.5

    def proj(h):
        B, T, _ = h.shape
        q, k, v = jnp.split(h @ w_in, 3, axis=-1)
        return (q.reshape(B, T, DA_HEADS, 2, DA_DH), k.reshape(B, T, DA_HEADS, 2, DA_DH),
                v.reshape(B, T, DA_HEADS, 2 * DA_DH))

    def attend(qb, kk, vv):
        s = jnp.einsum('bqhcd,bkhcd->bhcqk', qb.astype(F32), kk) * scale
        p = jax.nn.softmax(s, axis=-1)
        a = p[:, :, 0] - lam * p[:, :, 1]
        return jnp.einsum('bhqk,bkhe->bqhe', a, vv)

    def out(o, dtype):
        B, T = o.shape[:2]
        on = o * lax.rsqrt(jnp.mean(o * o, axis=-1, keepdims=True) + NORM_EPS) * g_sub.astype(F32) * (1 - lam_init)
        return on.reshape(B, T, DA_Q).astype(dtype) @ w_out

    B, S, _ = hl.shape
    ql, kl, vl = proj(hl)
    ql = axial_rope(ql, S)
    kl = axial_rope(kl, S)
    qc, kc, vc = proj(hc)
    k_all = jnp.concatenate([kl, kc], axis=1).astype(F32)
    v_all = jnp.concatenate([vl, vc], axis=1).astype(F32)
    nb = S // Q_BLOCK
    qblocks = jnp.moveaxis(ql.reshape(B, nb, Q_BLOCK, DA_HEADS, 2, DA_DH), 1, 0)
    ol = lax.map(lambda qb: attend(qb, k_all, v_all), qblocks)
    ol = jnp.moveaxis(ol, 0, 1).reshape(B, S, DA_HEADS, 2 * DA_DH)
    yl = out(ol, hl.dtype)
    yc = out(attend(qc, kc.astype(F32), vc.astype(F32)), hc.dtype) if need_ctx else None
    return yl, yc


def setup_inputs(seed: int = 0) -> dict:
    key = jax.random.key(seed)
    ks = jax.random.split(key, 20)
    D = D_MODEL

    def nrm(k, shape, s):
        return jax.random.normal(k, shape, F32) * s

    gate_base = jnp.concatenate([jnp.full((ML_HEADS,), IGATE_BIAS, dtype=F32),
                                 jnp.linspace(FGATE_LO, FGATE_HI, ML_HEADS, dtype=F32)] * 2)
    return {
        'x': nrm(ks[0], (BATCH, SEQ, D), 1.0),
        'c': nrm(ks[1], (BATCH, D), 1.0),
        'ctx': nrm(ks[2], (BATCH, CTX_LEN, D), 1.0),
        'c_ctx': nrm(ks[3], (D,), 1.0),
        'ada_w': nrm(ks[4], (DEPTH, D, 6 * D), 0.5 * D ** -0.5),
        'ada_b': nrm(ks[5], (DEPTH, 6 * D), 0.02),
        'norm_g': 1.0 + nrm(ks[6], (DEPTH, 4, D), 0.02),
        'mlp_w1': nrm(ks[7], (DEPTH, D, D_FF), D ** -0.5),
        'mlp_w2': nrm(ks[8], (DEPTH, D_FF, D), D_FF ** -0.5),
        'ml_w_in': nrm(ks[9], (N_A, D, ML_IN), D ** -0.5),
        'ml_b_gates': gate_base + nrm(ks[10], (N_A, 4 * ML_HEADS), 0.1),
        'ml_head_g': 1.0 + nrm(ks[11], (N_A, ML_V), 0.02),
        'ml_w_out': nrm(ks[12], (N_A, ML_V, D), ML_V ** -0.5),
        'da_w_in': nrm(ks[13], (N_B, D, DA_IN), D ** -0.5),
        'da_lambda': nrm(ks[14], (N_B, 4, DA_DH), 0.1),
        'da_sub_g': 1.0 + nrm(ks[15], (N_B, 2 * DA_DH), 0.02),
        'da_w_out': nrm(ks[16], (N_B, DA_Q, D), DA_Q ** -0.5),
    }


def reference(x, c, ctx, c_ctx, ada_w, ada_b, norm_g, mlp_w1, mlp_w2, ml_w_in, ml_b_gates, ml_head_g,
              ml_w_out, da_w_in, da_lambda, da_sub_g, da_w_out):
    xl, xc = x, ctx
    for i in range(DEPTH):
        last = i == DEPTH - 1
        j = i // N_MIXERS
        ml = ada_mod(c, ada_w[i], ada_b[i])
        mc = ada_mod(c_ctx, ada_w[i], ada_b[i])
        hl = modulate(xl, norm_g[i, 0], ml[0], ml[1])
        hc = modulate(xc, norm_g[i, 0], mc[0], mc[1])
        if i % N_MIXERS == 0:
            yl, yc = mlstm_mixer(hl, hc, ml_w_in[j], ml_b_gates[j], ml_head_g[j], ml_w_out[j], not last)
        else:
            yl, yc = diff_attention_mixer(hl, hc, da_w_in[j], da_lambda[j], da_sub_g[j], da_w_out[j], i, not last)
        xl = xl + ml[2] * rmsnorm(yl, norm_g[i, 1])
        fl = sq_relu_mlp(modulate(xl, norm_g[i, 2], ml[3], ml[4]), mlp_w1[i], mlp_w2[i])
        xl = xl + ml[5] * rmsnorm(fl, norm_g[i, 3])
        if not last:
            xc = xc + mc[2] * rmsnorm(yc, norm_g[i, 1])
            fc = sq_relu_mlp(modulate(xc, norm_g[i, 2], mc[3], mc[4]), mlp_w1[i], mlp_w2[i])
            xc = xc + mc[5] * rmsnorm(fc, norm_g[i, 3])
    return xl
```

```python
import math
import numpy as np
import concourse.bass as bass
import concourse.mybir as mybir
from concourse.bass_utils import run_bass_kernel_spmd

F32 = mybir.dt.float32
BF16 = mybir.dt.bfloat16
I32 = mybir.dt.int32
AF = mybir.ActivationFunctionType
ALU = mybir.AluOpType
AX = mybir.AxisListType

D = 2048
DFF = 8192
SEQ = 2048
CTX = 256
NTOK = SEQ + CTX
NT = NTOK // 128
NB = 2
H = 8
EPS = 1e-6
ML_IN = 6176
EPOCH = 12000
POST_STOP = 0

ENGS = ("pe", "act", "dve", "pool", "sp")


class Buf:
    __slots__ = ("name", "writer", "readers", "sem", "ndma", "excl")

    def __init__(self, name, excl=False):
        self.name = name
        self.excl = excl
        self.writer = None
        self.readers = {}
        self.sem = None
        self.ndma = 0


class Op:
    __slots__ = ("eng", "fn", "deps", "signal", "sem", "tick", "dma", "idx", "vsem")


class VSem:
    __slots__ = ("count", "handle")

    def __init__(self):
        self.count = 0
        self.handle = None


class Prog:
    def __init__(self, nc):
        self.nc = nc
        self.q = {e: [] for e in ENGS}
        self.fence = {e: None for e in ENGS}
        self.dma_last = {}
        self.nsem = 0
        self.free_vsems = []
        self.live_dma_bufs = []

    def op(self, eng, fn, r=(), w=(), dma=None):
        o = Op()
        o.eng = eng; o.fn = fn; o.signal = False; o.sem = None; o.tick = 0; o.dma = dma
        deps = []
        f = self.fence[eng]
        if f is not None:
            deps.extend(f)
            self.fence[eng] = None
        for b in r:
            d = b.writer
            if d is not None:
                if d.dma is not None or d.eng != eng or eng != "pe":
                    deps.append(d)
            if b.excl:
                for d in b.readers.values():
                    if d.eng != eng:
                        deps.append(d)
        for b in w:
            d = b.writer
            if d is not None and (d.dma is not None or d.eng != eng):
                deps.append(d)
            for d in b.readers.values():
                if d.dma is not None or d.eng != eng:
                    deps.append(d)
        for b in r:
            b.readers[id(dma) if dma is not None else eng] = o
        for b in w:
            b.writer = o
            b.readers = {}
        seen = set()
        dd = []
        for d in deps:
            if id(d) not in seen:
                seen.add(id(d)); dd.append(d); d.signal = True
        o.deps = dd
        o.idx = len(self.q[eng])
        self.q[eng].append(o)
        if dma is not None:
            o.signal = True
            self.dma_last[id(dma)] = o
            if dma.sem is None:
                dma.sem = self.free_vsems.pop() if self.free_vsems else VSem()
                self.live_dma_bufs.append(dma)
            dma.sem.count += 1
            o.vsem = dma.sem
            o.tick = 16 * dma.sem.count
            assert o.tick < 60000, dma.name
        return o

    def release_dma_sems(self):
        for b in self.live_dma_bufs:
            self.free_vsems.append(b.sem)
            self.dma_last.pop(id(b), None)
            b.sem = None
        self.live_dma_bufs = []

    def barrier(self):
        ops = []
        for e in ENGS:
            for o in reversed(self.q[e]):
                if o.dma is None:
                    o.signal = True
                    ops.append(o)
                    break
        ops.extend(self.dma_last.values())
        for e in ENGS:
            self.fence[e] = list(ops) + (self.fence[e] or [])

    def emit(self):
        nc = self.nc
        self.barrier()
        for e in ENGS:
            cnt = 0
            sem = None
            for o in self.q[e]:
                if o.dma is not None:
                    v = o.vsem
                    if v.handle is None:
                        v.handle = nc.alloc_semaphore("d%d" % self.nsem); self.nsem += 1
                    o.sem = v.handle
                elif o.signal:
                    if sem is None or cnt >= EPOCH:
                        sem = nc.alloc_semaphore("e%d" % self.nsem); self.nsem += 1
                        cnt = 0
                    cnt += 1
                    o.sem = sem
                    o.tick = cnt
        q = self.q

        def run(ename):
            def f(e):
                known = {}
                for o in q[ename]:
                    for d in o.deps:
                        k = known.get(d.sem, 0)
                        if d.tick > k:
                            e.wait_ge(d.sem, d.tick)
                            known[d.sem] = d.tick
                    ins = o.fn(e)
                    if o.signal:
                        ins.then_inc(o.sem, 16 if o.dma is not None else 1)
                fin = self.fence[ename]
                if fin:
                    for d in fin:
                        k = known.get(d.sem, 0)
                        if d.tick > k:
                            e.wait_ge(d.sem, d.tick)
                            known[d.sem] = d.tick
            return f

        with nc.Block() as block:
            block.sync(run("sp"))
            block.tensor(run("pe"))
            block.scalar(run("act"))
            block.vector(run("dve"))
            block.gpsimd(run("pool"))


class Pool:
    def __init__(self, K, name, shape, dtype, n=1, psum=False):
        self.tiles = []
        for i in range(n):
            nm = "%s_%d_%d" % (name, K.uid(), i)
            if psum:
                t = K.nc.alloc_psum_tensor(nm, shape, dtype)
            else:
                t = K.nc.alloc_sbuf_tensor(nm, shape, dtype)
            self.tiles.append((t, Buf(nm)))
        self.i = 0

    def next(self):
        t = self.tiles[self.i % len(self.tiles)]
        self.i += 1
        return t


class Tile:
    __slots__ = ("t", "b")

    def __init__(self, t, b):
        self.t = t
        self.b = b


class Phase:
    def __init__(self, K, name):
        self.K = K
        self.name = name
        self.stack = []

    def __enter__(self):
        return self

    def __exit__(self, *a):
        self.K.P.barrier()
        self.K.P.release_dma_sems()
        for g in reversed(self.stack):
            g.__exit__(None, None, None)
        return False

    def sb(self, name, shape, dtype, n=None):
        K = self.K
        res = []
        for i in range(n or 1):
            nm = "%s_%s_%d_%d" % (self.name, name, K.uid(), i)
            g = K.nc.sbuf_tensor(nm, shape, dtype)
            t = g.__enter__()
            self.stack.append(g)
            res.append(Tile(t, Buf(nm)))
        return res if n is not None else res[0]

    def ps(self, name, shape, dtype, n=None):
        K = self.K
        res = []
        for i in range(n or 1):
            nm = "%s_%s_%d_%d" % (self.name, name, K.uid(), i)
            g = K.nc.psum_tensor(nm, shape, dtype)
            t = g.__enter__()
            self.stack.append(g)
            res.append(Tile(t, Buf(nm, excl=True)))
        return res if n is not None else res[0]


class Rot:
    def __init__(self, tiles):
        self.tiles = tiles
        self.i = 0

    def next(self):
        t = self.tiles[self.i % len(self.tiles)]
        self.i += 1
        return t


class K:
    def __init__(self, nc, dbg=(), ext_in=()):
        self.ext_in = set(ext_in)
        self.nc = nc
        self.P = Prog(nc)
        self._uid = 0
        self.dbg = set(dbg)
        self.d = {}
        self.db = {}
        self.evac_i = 0

    def uid(self):
        self._uid += 1
        return self._uid

    def phase(self, name):
        return Phase(self, name)

    def dram(self, name, shape, dtype, kind=None):
        if kind is None:
            kind = "ExternalOutput" if name in self.dbg else ("ExternalInput" if name in self.ext_in else "Internal")
        self.d[name] = self.nc.dram_tensor(name, list(shape), dtype, kind=kind).ap()
        return self.d[name]

    def dbuf(self, key):
        b = self.db.get(key)
        if b is None:
            b = Buf(str(key))
            self.db[key] = b
        return b

    def dma(self, q, out, in_, slot, r=(), w=(), slow=False):
        if slow:
            fn = lambda e: e.dma_start(out=out, in_=in_, allow_slow_non_contiguous=True)
        else:
            fn = lambda e: e.dma_start(out=out, in_=in_)
        return self.P.op(q, fn, r=r, w=w, dma=slot)

    def load(self, q, tile, out, in_, r=(), slow=False):
        return self.dma(q, out, in_, tile.b, r=r, w=[tile.b], slow=slow)

    def store(self, q, tile, out, in_, w=(), slow=False):
        return self.dma(q, out, in_, tile.b, r=[tile.b], w=w, slow=slow)

    def mm(self, out, lhsT, rhs, start, stop, r, w):
        return self.P.op("pe", lambda e: e.matmul(out, lhsT=lhsT, rhs=rhs, start=start, stop=stop), r=r, w=w)

    def tr(self, out, in_, ident, r, w):
        return self.P.op("pe", lambda e: e.transpose(out, in_, ident), r=r, w=w)

    def act(self, out, in_, func, r, w, bias=None, scale=None, accum=None, eng="act"):
        kw = {}
        if bias is not None:
            kw["bias"] = bias
        if scale is not None:
            kw["scale"] = scale
        if accum is not None:
            kw["accum_out"] = accum
        return self.P.op("act", lambda e: e.activation(out=out, in_=in_, func=func, **kw), r=r, w=w)

    def ts(self, eng, out, in0, s1, s2, op0, op1, r, w):
        if op1 is None:
            return self.P.op(eng, lambda e: e.tensor_scalar(out=out, in0=in0, scalar1=s1, scalar2=None, op0=op0), r=r, w=w)
        return self.P.op(eng, lambda e: e.tensor_scalar(out=out, in0=in0, scalar1=s1, scalar2=s2, op0=op0, op1=op1), r=r, w=w)

    def tt(self, eng, out, in0, in1, op, r, w):
        return self.P.op(eng, lambda e: e.tensor_tensor(out=out, in0=in0, in1=in1, op=op), r=r, w=w)

    def stt(self, eng, out, in0, scalar, in1, op0, op1, r, w):
        return self.P.op(eng, lambda e: e.scalar_tensor_tensor(out=out, in0=in0, scalar=scalar, in1=in1, op0=op0, op1=op1), r=r, w=w)

    def copy(self, eng, out, in_, r, w):
        if eng == "act":
            return self.P.op("act", lambda e: e.activation(out=out, in_=in_, func=AF.Copy), r=r, w=w)
        return self.P.op(eng, lambda e: e.tensor_copy(out=out, in_=in_), r=r, w=w)

    def recip(self, out, in_, r, w):
        return self.P.op("dve", lambda e: e.reciprocal(out=out, in_=in_), r=r, w=w)

    def memset(self, eng, out, val, w):
        return self.P.op(eng, lambda e: e.memset(out, val), r=(), w=w)

    def evac_eng(self):
        self.evac_i += 1
        return "act" if self.evac_i % 2 else "dve"


def phase_ada(k):
    P = k.P
    d = k.d
    with k.phase("ada") as ph:
        cT = ph.sb("cT", [128, 16, 3], F32)
        for r in range(3):
            src = d["c"][r] if r < 2 else d["c_ctx"]
            src = src.rearrange("(kc p) -> p kc", p=128)
            k.load("sp", cT, cT.t[:, :, r], src, slow=True)
        k.act(cT.t[:], cT.t[:], AF.Silu, r=[cT.b], w=[cT.b])
        cTb = ph.sb("cTb", [128, 16, 3], BF16)
        k.copy("dve", cTb.t[:], cT.t[:], r=[cT.b], w=[cTb.b])
        wrot = Rot(ph.sb("w", [128, 16, 512], BF16, n=4))
        brot = Rot(ph.sb("b", [3, 512], F32, n=2))
        orot = Rot(ph.sb("o", [3, 512], F32, n=2))
        prot = Rot(ph.ps("ps", [128, 512], F32, n=2))
        for l in range(2):
            for j in range(24):
                wt = wrot.next(); bt = brot.next(); ot = orot.next(); ps = prot.next()
                cs = slice(j * 512, (j + 1) * 512)
                k.load("pool", wt, wt.t[:], d["ada_w"][l][:, cs].rearrange("(kc p) n -> p kc n", p=128))
                k.load("sp", bt, bt.t[:], d["ada_b"][l][cs].partition_broadcast(3))
                for kc in range(16):
                    k.mm(ps.t[0:3, :], cTb.t[:, kc, :], wt.t[:, kc, :], kc == 0, kc == 15, r=[cTb.b, wt.b], w=[ps.b])
                k.tt("dve", ot.t[:], ps.t[0:3, :], bt.t[:], ALU.add, r=[ps.b, bt.b], w=[ot.b])
                k.store("sp", ot, d["mod"][l][:, cs], ot.t[:])


def cast_thunks(k, l):
    d = k.d
    s1 = Buf("cast1_%d" % l)
    s2 = Buf("cast2_%d" % l)
    th = []
    for i in range(16):
        r0 = i * 128
        th.append(lambda r0=r0, i=i: k.dma("pool", d["w1b"][l][r0:r0 + 128, :], d["mlp_w1"][l][r0:r0 + 128, :], s1,
                                           w=[k.dbuf(("w1b", l, i))]))
    for i in range(16):
        r0 = i * 512
        th.append(lambda r0=r0, i=i: k.dma("pool", d["w2b"][l][r0:r0 + 512, :], d["mlp_w2"][l][r0:r0 + 512, :], s2,
                                           w=[k.dbuf(("w2b", l, i))]))
    out = []
    for i in range(16):
        out.append(th[i]); out.append(th[16 + i])
    return out


def load_cols(k, ph, name, src_vec):
    t = ph.sb(name, [128, 16], F32)
    k.load("sp", t, t.t[:], src_vec.rearrange("(kc p) -> p kc", p=128), slow=True)
    return t


def mod_cols(k, ph, l, r, g_idx, shift_i, scale_i, tag):
    d = k.d
    g = load_cols(k, ph, "g" + tag, d["norm_g"][l][g_idx])
    sc = load_cols(k, ph, "sc" + tag, d["mod"][l][r][scale_i * D:(scale_i + 1) * D])
    sh = load_cols(k, ph, "sh" + tag, d["mod"][l][r][shift_i * D:(shift_i + 1) * D])
    k.stt("dve", sc.t[:], sc.t[:], 1.0, g.t[:], ALU.add, ALU.mult, r=[sc.b, g.b], w=[sc.b])
    return sc, sh


def token_src(k, l, bl, tt):
    d = k.d
    if l == 0:
        if tt < 2:
            return d["ctx"][bl][tt * 128:(tt + 1) * 128, :]
        return d["x"][bl][(tt - 2) * 128:(tt - 1) * 128, :]
    return d["xl1"][bl][tt * 128:(tt + 1) * 128, :]


def norm_part(k, xt, pools):
    junk, ssr, xnr, trr = pools
    jk = junk.next(); ss = ssr.next(); xn = xnr.next()
    k.act(jk.t[:], xt.t[:], AF.Square, r=[xt.b], w=[jk.b, ss.b], accum=ss.t[:, 0:1])
    k.act(ss.t[:, 1:2], ss.t[:, 0:1], AF.Sqrt, r=[ss.b], w=[ss.b], bias=EPS_T(k), scale=1.0 / D)
    k.recip(ss.t[:, 2:3], ss.t[:, 1:2], r=[ss.b], w=[ss.b])
    k.ts("dve", xn.t[:], xt.t[:], ss.t[:, 2:3], None, ALU.mult, None, r=[xt.b, ss.b], w=[xn.b])
    return xn


def trans_part(k, xn, hT_dst, A, Bm, ident, pools):
    trr = pools[3]
    for g4 in range(4):
        pt = trr.next()
        for j in range(4):
            kc = g4 * 4 + j
            k.tr(pt.t[:, j, :], xn.t[:, kc * 128:(kc + 1) * 128], ident.t[:], r=[xn.b, ident.b], w=[pt.b])
        for j in range(4):
            kc = g4 * 4 + j
            dst, dbuf = hT_dst(kc)
            if (kc % 2) == 0:
                k.act(dst, pt.t[:, j, :], AF.Identity, r=[pt.b, A.b, Bm.b], w=[dbuf],
                      bias=Bm.t[:, kc:kc + 1], scale=A.t[:, kc:kc + 1])
            else:
                k.ts("dve", dst, pt.t[:, j, :], A.t[:, kc:kc + 1], Bm.t[:, kc:kc + 1], ALU.mult, ALU.add,
                     r=[pt.b, A.b, Bm.b], w=[dbuf])


def norm_mod_T(k, ph, xt, hT_dst, A, Bm, ident, pools):
    xn = norm_part(k, xt, pools)
    trans_part(k, xn, hT_dst, A, Bm, ident, pools)


def EPS_T(k):
    return k.eps.t[:, 0:1]


def setup_consts(k):
    nc = k.nc
    P = k.P

    def sbt(name, shape, dtype):
        return Tile(nc.alloc_sbuf_tensor(name, shape, dtype), Buf(name))

    k.ident = sbt("ident", [128, 128], BF16)
    k.eps = sbt("epsc", [128, 1], F32)
    k.memset("pool", k.ident.t[:], 0.0, w=[k.ident.b])
    P.op("pool", lambda e: e.affine_select(out=k.ident.t[:], in_=k.ident.t[:], pattern=[[-1, 128]],
                                           compare_op=ALU.not_equal, fill=1.0, base=0, channel_multiplier=1),
         r=[k.ident.b], w=[k.ident.b])
    k.memset("pool", k.eps.t[:], EPS, w=[k.eps.b])


def phase_inproj0(k, bl):
    d = k.d
    with k.phase("ip0") as ph:
        hT = ph.sb("hT", [128, 16, NTOK], BF16)
        hTb = [Buf("hT%d" % i) for i in range(NT)]
        build_hT(k, ph, 0, bl, hT, hTb, ntr=4)
        wrot = Rot(ph.sb("w", [128, 16, 512], BF16, n=2))
        prot = Rot(ph.ps("ps", [128, 512], F32, n=3))
        orot = Rot(ph.sb("o", [128, 512], BF16, n=3))
        grot = Rot(ph.sb("g", [128, 32], F32, n=2))
        gtmp = Rot(ph.sb("gtmp", [128, 16], F32, n=2))
        bg = ph.sb("bg", [128, 32], F32)
        k.load("sp", bg, bg.t[:], d["ml_b_gates"][0].partition_broadcast(128))
        blocks = []
        for i in range(2):
            blocks.append(("q", i * 512, 512, "qk0", i * 512))
        for i in range(2):
            blocks.append(("k", 1024 + i * 512, 512, "qk0", 1024 + i * 512))
        for i in range(4):
            blocks.append(("v", 2048 + i * 512, 512, "v0", i * 512))
        for i in range(4):
            blocks.append(("og", 4096 + i * 512, 512, "og0", i * 512))
        blocks.append(("g", 6144, 32, "gt0", 0))
        for (kind, c0, ncols, dst, dc0) in blocks:
            wt = wrot.next()
            k.load("pool", wt, wt.t[:, :, 0:ncols], d["ml_w_in"][0][:, c0:c0 + ncols].rearrange("(kc p) n -> p kc n", p=128))
            for tt in range(NT):
                ps = prot.next()
                for kc in range(16):
                    k.mm(ps.t[:, 0:ncols], hT.t[:, kc, tt * 128:(tt + 1) * 128], wt.t[:, kc, 0:ncols], kc == 0, kc == 15,
                         r=[hTb[tt], wt.b], w=[ps.b])
                rows = slice(tt * 128, (tt + 1) * 128)
                if kind == "g":
                    gt = grot.next(); tmp = gtmp.next()
                    k.tt("dve", gt.t[:], ps.t[:, 0:32], bg.t[:], ALU.add, r=[ps.b, bg.b], w=[gt.b])
                    gv = gt.t[:].rearrange("p (a b c) -> p a b c", a=2, b=2, c=8)[:, :, 1, :]
                    tv = tmp.t[:].rearrange("p (a c) -> p a c", a=2, c=8)
                    k.act(tv, gv, AF.Exp, r=[gt.b], w=[tmp.b], scale=-1.0)
                    k.act(tv, tv, AF.Ln, r=[tmp.b], w=[tmp.b], bias=1.0)
                    k.ts("dve", gv, tv, -1.0, None, ALU.mult, None, r=[tmp.b, gt.b], w=[gt.b])
                    k.store("sp", gt, d["gt0"][bl][rows, :], gt.t[:])
                else:
                    ot = orot.next()
                    eng = k.evac_eng()
                    if kind == "q":
                        if eng == "act":
                            k.act(ot.t[:], ps.t[:], AF.Copy, r=[ps.b], w=[ot.b], scale=128.0 ** -0.5)
                        else:
                            k.ts("dve", ot.t[:], ps.t[:], 128.0 ** -0.5, None, ALU.mult, None, r=[ps.b], w=[ot.b])
                    else:
                        k.copy(eng, ot.t[:], ps.t[:], r=[ps.b], w=[ot.b])
                    k.store("sp", ot, d[dst][bl][rows, dc0:dc0 + 512], ot.t[:])


def bcast_last(ap2, n):
    a = ap2.ap
    return bass.AP(ap2.tensor, ap2.offset, [list(a[0]), list(a[1]), [0, n]])


def bcast_mid(ap2, n):
    a = ap2.ap
    return bass.AP(ap2.tensor, ap2.offset, [list(a[0]), [0, n], list(a[1])])


def phase_scan0(k, bl, thunks=()):
    d = k.d
    P = k.P
    with k.phase("scan") as ph:
        ones = ph.sb("ones", [128, 128], F32)
        triF = ph.sb("triF", [128, 128], F32)
        triB = ph.sb("triB", [128, 128], F32)
        mskF = ph.sb("mskF", [128, 128], BF16)
        mskB = ph.sb("mskB", [128, 128], BF16)
        k.memset("pool", ones.t[:], 1.0, w=[ones.b])
        for tri, msk, sgn in ((triF, mskF, -1), (triB, mskB, 1)):
            k.memset("pool", tri.t[:], 1.0, w=[tri.b])
            P.op("pool", lambda e, tri=tri, sgn=sgn: e.affine_select(out=tri.t[:], in_=tri.t[:], pattern=[[-sgn, 128]],
                 compare_op=ALU.is_ge, fill=0.0, base=0, channel_multiplier=sgn), r=[tri.b], w=[tri.b])
            k.copy("pool", msk.t[:], tri.t[:], r=[tri.b], w=[msk.b])
        HG = ph.sb("HG", [128, D], F32)
        k.load("sp", HG, HG.t[:], d["ml_head_g"][0].partition_broadcast(128))
        U = ph.sb("U", [128, 8, 257], F32)
        Cb = ph.sb("Cb", [128, 8, 257], BF16)
        Ub = [Buf("U%d" % h) for h in range(8)]
        Cbb = [Buf("Cb%d" % h) for h in range(8)]
        qkr = Rot(ph.sb("qk", [128, D], BF16, n=4))
        var = ph.sb("va", [128, 8, 257], BF16, n=4)
        for v in var:
            k.memset("pool", v.t[:, :, 256:257], 1.0, w=[v.b])
        var = Rot(var)
        gtr = Rot(ph.sb("gt", [128, 32], F32, n=4))
        smr = Rot(ph.sb("sm", [128, 40], F32, n=6))
        Qer = Rot(ph.sb("Qe", [128, 8, 128], BF16, n=2))
        Kfr = Rot(ph.sb("Kf", [128, 8, 128], BF16, n=2))
        QeTr = Rot(ph.sb("QeT", [128, 8, 128], BF16, n=2))
        KfTr = Rot(ph.sb("KfT", [128, 8, 128], BF16, n=2))
        ATr = Rot(ph.sb("AT", [128, 8, 128], BF16, n=2))
        hfr = Rot(ph.sb("hf", [128, D], F32, n=4))
        hsr = Rot(ph.sb("hs", [128, D], F32, n=2))
        ogr = Rot(ph.sb("og", [128, D], BF16, n=2))
        sgr = Rot(ph.sb("sg", [128, D], BF16, n=2))
        w1r = Rot(ph.sb("w1", [128, D], BF16, n=2))
        yir = Rot(ph.sb("yi", [128, D], BF16, n=2))
        rdr = Rot(ph.sb("rd", [128, 2], F32, n=4))
        msr = Rot(ph.sb("ms", [128, 24], F32, n=2))
        jkr = Rot(ph.sb("jk", [128, 256], BF16, n=2))
        cs_ps = Rot(ph.ps("cs", [128, 16], F32, n=1))
        tr_ps = Rot(ph.ps("tr", [128, 4, 128], BF16, n=2))
        st_ps = Rot(ph.ps("st", [128, 4, 128], F32, n=2))
        nm_ps = Rot(ph.ps("nm", [128, 257], F32, n=2))
        pp_ps = Rot(ph.ps("pp", [128, 257], F32, n=1))
        orders = (list(range(NT)), [1, 0] + list(range(NT - 1, 1, -1)))
        Us = (U, ph.sb("U2", [128, 8, 257], F32))
        Cbs = (Cb, ph.sb("Cb2", [128, 8, 257], BF16))
        Ubs = (Ub, [Buf("U2_%d" % h) for h in range(8)])
        Cbbs = (Cbb, [Buf("Cb2_%d" % h) for h in range(8)])
        smps = [None, None]
        for dirn in range(2):
            k.memset("pool", Us[dirn].t[:], 0.0, w=Ubs[dirn])
            k.memset("pool", Cbs[dirn].t[:], 0.0, w=Cbbs[dirn])

        def step_loads(dirn, tt):
            rows = slice(tt * 128, (tt + 1) * 128)
            qk = qkr.next(); va = var.next(); gt = gtr.next()
            k.load("sp", qk, qk.t[:], d["qk0"][bl][rows, :])
            k.load("sp", va, va.t[:, :, 0:256], d["v0"][bl][rows, :].rearrange("p (h e) -> p h e", h=8))
            k.load("sp", gt, gt.t[:], d["gt0"][bl][rows, :])
            return (qk, va, gt)

        def step(dirn, tt, tiles3):
            tri = triF if dirn == 0 else triB
            msk = mskF if dirn == 0 else mskB
            U_, Cb_, Ub_, Cbb_ = Us[dirn], Cbs[dirn], Ubs[dirn], Cbbs[dirn]
            smp = smps[dirn]
            rows = slice(tt * 128, (tt + 1) * 128)
            qk, va, gt = tiles3
            sm = smr.next()
            ig = gt.t[:, dirn * 16:dirn * 16 + 8]
            lf = gt.t[:, dirn * 16 + 8:dirn * 16 + 16]
            cs = cs_ps.next()
            k.mm(cs.t[:, 0:8], tri.t[:], lf, True, True, r=[tri.b, gt.b], w=[cs.b])
            k.mm(cs.t[:, 8:16], ones.t[:], lf, True, True, r=[ones.b, gt.b], w=[cs.b])
            E = sm.t[:, 0:8]; Fv = sm.t[:, 8:16]; Tc = sm.t[:, 16:24]; tmp = sm.t[:, 24:32]
            k.act(E, cs.t[:, 0:8], AF.Exp, r=[cs.b], w=[sm.b])
            k.act(Tc, cs.t[:, 8:16], AF.Exp, r=[cs.b], w=[sm.b])
            k.tt("dve", tmp, ig, cs.t[:, 0:8], ALU.subtract, r=[gt.b, cs.b], w=[sm.b])
            k.act(Fv, tmp, AF.Exp, r=[sm.b], w=[sm.b])
            yield
            Qe = Qer.next(); Kf = Kfr.next(); QeT = QeTr.next(); KfT = KfTr.next(); AT = ATr.next()
            k.tt("dve", Qe.t[:], qk.t[:, 0:1024].rearrange("p (h e) -> p h e", h=8), bcast_last(E, 128), ALU.mult,
                 r=[qk.b, sm.b], w=[Qe.b])
            k.tt("pool", Kf.t[:], qk.t[:, 1024:2048].rearrange("p (h e) -> p h e", h=8), bcast_last(Fv, 128), ALU.mult,
                 r=[qk.b, sm.b], w=[Kf.b])
            yield
            for src, dstT in ((Qe, QeT), (Kf, KfT)):
                for g in range(2):
                    pt = tr_ps.next()
                    for j in range(4):
                        k.tr(pt.t[:, j, :], src.t[:, g * 4 + j, :], k.ident.t[:], r=[src.b, k.ident.b], w=[pt.b])
                    k.copy(k.evac_eng(), dstT.t[:, g * 4:(g + 1) * 4, :], pt.t[:], r=[pt.b], w=[dstT.b])
            yield
            for g in range(2):
                st = st_ps.next()
                for j in range(4):
                    h = g * 4 + j
                    k.mm(st.t[:, j, :], KfT.t[:, h, :], QeT.t[:, h, :], True, True, r=[KfT.b, QeT.b], w=[st.b])
                k.tt("dve", AT.t[:, g * 4:(g + 1) * 4, :], st.t[:], bcast_mid(msk.t[:], 4), ALU.mult,
                     r=[st.b, msk.b], w=[AT.b])
            yield
            hf = hfr.next()
            for h in range(8):
                nm = nm_ps.next(); rd = rdr.next(); pp = pp_ps.next()
                k.mm(nm.t[:], AT.t[:, h, :], va.t[:, h, :], True, False, r=[AT.b, va.b], w=[nm.b])
                k.mm(nm.t[:], QeT.t[:, h, :], Cb_.t[:, h, :], False, True, r=[QeT.b, Cbb_[h]], w=[nm.b])
                k.act(rd.t[:, 0:1], nm.t[:, 256:257], AF.Abs, r=[nm.b], w=[rd.b])
                k.ts("dve", rd.t[:, 0:1], rd.t[:, 0:1], 1.0, None, ALU.max, None, r=[rd.b], w=[rd.b])
                k.recip(rd.t[:, 1:2], rd.t[:, 0:1], r=[rd.b], w=[rd.b])
                hsl = slice(h * 256, (h + 1) * 256)
                if h % 2 == 0:
                    k.act(hf.t[:, hsl], nm.t[:, 0:256], AF.Copy, r=[nm.b, rd.b], w=[hf.b], scale=rd.t[:, 1:2])
                else:
                    k.ts("dve", hf.t[:, hsl], nm.t[:, 0:256], rd.t[:, 1:2], None, ALU.mult, None, r=[nm.b, rd.b], w=[hf.b])
                yield
                k.mm(pp.t[:], Kf.t[:, h, :], va.t[:, h, :], True, True, r=[Kf.b, va.b], w=[pp.b])
                if smp is None:
                    k.copy("dve", U_.t[:, h, :], pp.t[:], r=[pp.b], w=[Ub_[h]])
                else:
                    k.stt("dve", U_.t[:, h, :], U_.t[:, h, :], smp.t[:, 16 + h:17 + h], pp.t[:], ALU.mult, ALU.add,
                          r=[Ub_[h], smp.b, pp.b], w=[Ub_[h]])
                k.act(Cb_.t[:, h, :], U_.t[:, h, :], AF.Copy, r=[Ub_[h], sm.b], w=[Cbb_[h]], scale=sm.t[:, 16 + h:17 + h])
            yield
            smps[dirn] = sm
            dst = "hf0" if dirn == 0 else "hb0"
            k.store("sp", hf, d[dst][bl][rows, :], hf.t[:], w=[k.dbuf((dst, bl, tt))])

        thunks = list(thunks)
        done_at = {}
        for i in range(NT):
            done_at.setdefault(orders[0][i], []).append(i)
            done_at.setdefault(orders[1][i], []).append(i)
        ready = {}
        for tt, v in done_at.items():
            ready.setdefault(max(v), []).append(tt)

        def combine(tt):
            rows = slice(tt * 128, (tt + 1) * 128)
            hf = hfr.next(); hb = hfr.next(); hs = hsr.next()
            og = ogr.next(); sg = sgr.next(); w1 = w1r.next(); yi = yir.next(); ms = msr.next(); jk = jkr.next()
            k.load("sp", hf, hf.t[:], d["hf0"][bl][rows, :], r=[k.dbuf(("hf0", bl, tt))])
            k.load("sp", hb, hb.t[:], d["hb0"][bl][rows, :], r=[k.dbuf(("hb0", bl, tt))])
            k.load("sp", og, og.t[:], d["og0"][bl][rows, :])
            k.tt("dve", hs.t[:], hf.t[:], hb.t[:], ALU.add, r=[hf.b, hb.b], w=[hs.b])
            for h in range(8):
                k.act(jk.t[:], hs.t[:, h * 256:(h + 1) * 256], AF.Square, r=[hs.b], w=[jk.b, ms.b], accum=ms.t[:, h:h + 1])
            k.act(ms.t[:, 8:16], ms.t[:, 0:8], AF.Sqrt, r=[ms.b], w=[ms.b], bias=EPS_T(k), scale=1.0 / 256)
            k.recip(ms.t[:, 16:24], ms.t[:, 8:16], r=[ms.b], w=[ms.b])
            k.act(sg.t[:], og.t[:], AF.Sigmoid, r=[og.b], w=[sg.b])
            k.tt("pool", w1.t[:], sg.t[:], HG.t[:], ALU.mult, r=[sg.b, HG.b], w=[w1.b])
            for h in range(8):
                hsl = slice(h * 256, (h + 1) * 256)
                k.stt("dve", yi.t[:, hsl], hs.t[:, hsl], ms.t[:, 16 + h:17 + h], w1.t[:, hsl], ALU.mult, ALU.mult,
                      r=[hs.b, ms.b, w1.b], w=[yi.b])
            k.store("sp", yi, d["yin"][bl][rows, :], yi.t[:])

        pre = [step_loads(0, orders[0][0]), step_loads(1, orders[1][0])]
        for i in range(NT):
            gens = []
            for dirn in range(2):
                cur3 = pre[dirn]
                if i + 1 < NT:
                    pre[dirn] = step_loads(dirn, orders[dirn][i + 1])
                gens.append(step(dirn, orders[dirn][i], cur3))
            while gens:
                for g_ in list(gens):
                    try:
                        next(g_)
                    except StopIteration:
                        gens.remove(g_)
            for _ in range(2):
                if thunks:
                    thunks.pop(0)()
            for tt in ready.get(i - 1, []):
                combine(tt)
        for tt in ready.get(NT - 1, []):
            combine(tt)
        while thunks:
            thunks.pop(0)()


def bcast_row_tile(k, ph, name, vec_ap):
    t = ph.sb(name, [128, D], F32)
    k.load("sp", t, t.t[:], vec_ap.partition_broadcast(128))
    return t


def phase_post(k, l, blocks, w_out_ap):
    for blk in blocks:
        phase_post_block(k, l, [blk], w_out_ap)


def phase_post_block(k, l, blocks, w_out_ap):
    d = k.d
    P = k.P
    with k.phase("post%d" % l) as ph:
        NTB = 4
        uT = ph.sb("uT", [128, 64, NTB * 128], BF16)
        hT = ph.sb("hT", [128, 16, NTB * 128], BF16)
        fg = ph.sb("fg", [128, NTB, D], F32)
        fgb = [Buf("fg%d" % i) for i in range(NTB)]
        hTb = [Buf("hTb%d" % i) for i in range(NTB)]
        xrot = Rot(ph.sb("x", [128, D], F32, n=2))
        yrot = Rot(ph.sb("y", [128, D], BF16, n=2))
        pools = (Rot(ph.sb("junk", [128, D], BF16, n=1)), Rot(ph.sb("ss", [128, 4], F32, n=2)),
                 Rot(ph.sb("xn", [128, D], BF16, n=1)), Rot(ph.ps("tr", [128, 4, 128], BF16, n=2)))
        wrot = Rot(ph.sb("w", [128, 16, 256], BF16, n=2))
        w2rot = Rot(ph.sb("w2", [128, 4, 512], BF16, n=2))
        G1 = ph.sb("G1", [128, D], F32)
        G3 = ph.sb("G3", [128, D], F32)
        gtmp = ph.sb("gtmp", [128, D], F32)
        rlr = Rot(ph.sb("rl", [128, NTB * 128], BF16, n=2))
        sqr = Rot(ph.sb("sq", [128, 16], F32, n=NTB * 2))
        jk2 = Rot(ph.sb("jk2", [128, 512], BF16, n=1))
        prot = Rot(ph.ps("ps", [128, 512], F32, n=6))
        ubuf = [Buf("u%d" % i) for i in range(64)]
        cols = {}
        cur_cond = None
        for blk in blocks:
            nt = len(blk)
            N = nt * 128
            cond = blk[0][0]
            if cond != cur_cond:
                cur_cond = cond
                for (G, gi, mi) in ((G1, 1, 2), (G3, 3, 5)):
                    k.load("sp", G, G.t[:], d["mod"][l][cond][mi * D:(mi + 1) * D].partition_broadcast(128))
                    k.load("sp", gtmp, gtmp.t[:], d["norm_g"][l][gi].partition_broadcast(128))
                    k.tt("pool", G.t[:], G.t[:], gtmp.t[:], ALU.mult, r=[G.b, gtmp.b], w=[G.b])
                if cond not in cols:
                    cols[cond] = mod_cols(k, ph, l, cond, 2, 3, 4, "m%d" % cond)
                A2, B2 = cols[cond]
            for t, (_, yin_ap, xin_ap, xmid_ap, xout_ap, key) in enumerate(blk):
                yt = yrot.next()
                k.load("sp", yt, yt.t[:], yin_ap)
                for g4 in range(4):
                    pt = pools[3].next()
                    for j in range(4):
                        kc = g4 * 4 + j
                        k.tr(pt.t[:, j, :], yt.t[:, kc * 128:(kc + 1) * 128], k.ident.t[:], r=[yt.b, k.ident.b], w=[pt.b])
                    k.copy(k.evac_eng(), hT.t[:, g4 * 4:(g4 + 1) * 4, t * 128:(t + 1) * 128], pt.t[:], r=[pt.b], w=[hTb[t]])
            sqs = [sqr.next() for _ in range(nt)]
            for cb in range(8):
                wt = wrot.next()
                cs = slice(cb * 256, (cb + 1) * 256)
                k.load("pool", wt, wt.t[:], w_out_ap[:, cs].rearrange("(kc p) n -> p kc n", p=128))
                for t in range(nt):
                    ps = prot.next()
                    for kc in range(16):
                        k.mm(ps.t[:, 0:256], hT.t[:, kc, t * 128:(t + 1) * 128], wt.t[:, kc, :], kc == 0, kc == 15,
                             r=[hTb[t], wt.b], w=[ps.b])
                    k.tt("dve", fg.t[:, t, cs], ps.t[:, 0:256], G1.t[:, cs], ALU.mult, r=[ps.b, G1.b], w=[fgb[t]])
                    jk = jk2.next()
                    k.act(jk.t[:, 0:256], ps.t[:, 0:256], AF.Square, r=[ps.b], w=[jk.b, sqs[t].b], accum=sqs[t].t[:, cb:cb + 1])
            for t, (_, yin_ap, xin_ap, xmid_ap, xout_ap, key) in enumerate(blk):
                sq = sqs[t]
                xt = xrot.next()
                k.load("sp", xt, xt.t[:], xin_ap)
                P.op("dve", lambda e, sq=sq: e.tensor_reduce(out=sq.t[:, 8:9], in_=sq.t[:, 0:8], axis=AX.X, op=ALU.add),
                     r=[sq.b], w=[sq.b])
                k.act(sq.t[:, 9:10], sq.t[:, 8:9], AF.Sqrt, r=[sq.b], w=[sq.b], bias=EPS_T(k), scale=1.0 / D)
                k.recip(sq.t[:, 10:11], sq.t[:, 9:10], r=[sq.b], w=[sq.b])
                k.stt("dve", xt.t[:], fg.t[:, t, :], sq.t[:, 10:11], xt.t[:], ALU.mult, ALU.add, r=[fgb[t], sq.b, xt.b], w=[xt.b])
                k.store("sp", xt, xmid_ap, xt.t[:], w=[k.dbuf(("xmid", key))])
                norm_mod_T(k, ph, xt, lambda kc, t=t: (hT.t[:, kc, t * 128:(t + 1) * 128], hTb[t]), A2, B2, k.ident, pools)
            if POST_STOP == 1:
                continue
            for mb in range(32):
                wt = wrot.next()
                k.load("sp", wt, wt.t[:], d["w1b"][l][:, mb * 256:(mb + 1) * 256].rearrange("(kc p) n -> p kc n", p=128),
                       r=[k.dbuf(("w1b", l, i)) for i in range(16)])
                for mc in range(2):
                    ps = prot.next()
                    for kc in range(16):
                        k.mm(ps.t[:, 0:N], wt.t[:, kc, mc * 128:(mc + 1) * 128], hT.t[:, kc, 0:N], kc == 0, kc == 15,
                             r=[wt.b] + hTb[:nt], w=[ps.b])
                    rl = rlr.next()
                    k.act(rl.t[:, 0:N], ps.t[:, 0:N], AF.Relu, r=[ps.b], w=[rl.b])
                    m = mb * 2 + mc
                    k.tt("pool", uT.t[:, m, 0:N], rl.t[:, 0:N], rl.t[:, 0:N], ALU.mult, r=[rl.b], w=[ubuf[m]])
            if POST_STOP == 2:
                continue
            sqs = [sqr.next() for _ in range(nt)]
            for cb in range(4):
                cs = slice(cb * 512, (cb + 1) * 512)
                pss = [prot.next() for _ in range(nt)]
                for g in range(16):
                    w2 = w2rot.next()
                    k.load("sp", w2, w2.t[:], d["w2b"][l][g * 512:(g + 1) * 512, cs].rearrange("(kc p) n -> p kc n", p=128),
                           r=[k.dbuf(("w2b", l, g))])
                    for t in range(nt):
                        for j in range(4):
                            m = g * 4 + j
                            k.mm(pss[t].t[:], uT.t[:, m, t * 128:(t + 1) * 128], w2.t[:, j, :], m == 0, m == 63,
                                 r=[ubuf[m], w2.b], w=[pss[t].b])
                for t in range(nt):
                    k.tt("dve", fg.t[:, t, cs], pss[t].t[:], G3.t[:, cs], ALU.mult, r=[pss[t].b, G3.b], w=[fgb[t]])
                    jk = jk2.next()
                    k.act(jk.t[:], pss[t].t[:], AF.Square, r=[pss[t].b], w=[jk.b, sqs[t].b], accum=sqs[t].t[:, cb:cb + 1])
            for t, (_, yin_ap, xin_ap, xmid_ap, xout_ap, key) in enumerate(blk):
                sq = sqs[t]
                xt = xrot.next()
                k.load("sp", xt, xt.t[:], xmid_ap, r=[k.dbuf(("xmid", key))])
                P.op("dve", lambda e, sq=sq: e.tensor_reduce(out=sq.t[:, 8:9], in_=sq.t[:, 0:4], axis=AX.X, op=ALU.add),
                     r=[sq.b], w=[sq.b])
                k.act(sq.t[:, 9:10], sq.t[:, 8:9], AF.Sqrt, r=[sq.b], w=[sq.b], bias=EPS_T(k), scale=1.0 / D)
                k.recip(sq.t[:, 10:11], sq.t[:, 9:10], r=[sq.b], w=[sq.b])
                k.stt("dve", xt.t[:], fg.t[:, t, :], sq.t[:, 10:11], xt.t[:], ALU.mult, ALU.add, r=[fgb[t], sq.b, xt.b], w=[xt.b])
                k.store("sp", xt, xout_ap, xt.t[:])


def phase_P(k, l, tiles, w_out_ap):
    d = k.d
    P = k.P
    with k.phase("P%d" % l) as ph:
        Wo = ph.sb("Wo", [128, 16, D], BF16)
        Wob = [Buf("Wo%d" % i) for i in range(4)]
        for cb in range(4):
            cs = slice(cb * 512, (cb + 1) * 512)
            k.dma("pool", Wo.t[:, :, cs], w_out_ap[:, cs].rearrange("(kc p) n -> p kc n", p=128), Wo.b, w=[Wob[cb]])
        G1 = ph.sb("G1", [128, D], F32)
        gtmp = ph.sb("gtmp", [128, D], F32)
        yrot = Rot(ph.sb("y", [128, D], BF16, n=3))
        yTr = Rot(ph.sb("yT", [128, 16, 128], BF16, n=2))
        xrot = Rot(ph.sb("x", [128, D], F32, n=4))
        fgr = Rot(ph.sb("fg", [128, D], F32, n=2))
        hTr = Rot(ph.sb("hTt", [128, 16, 128], BF16, n=2))
        pools = (Rot(ph.sb("junk", [128, D], BF16, n=1)), Rot(ph.sb("ss", [128, 4], F32, n=3)),
                 Rot(ph.sb("xn", [128, D], BF16, n=3)), Rot(ph.ps("tr", [128, 4, 128], BF16, n=3)))
        sqr = Rot(ph.sb("sq", [128, 16], F32, n=4))
        jk2 = Rot(ph.sb("jk2", [128, 512], BF16, n=2))
        prot = Rot(ph.ps("ps", [128, 512], F32, n=5))
        cols = {}
        cur_cond = None
        prev = None
        tl = list(tiles)

        def finish_prev(pv):
            xn_p, A_p, B_p, key_p = pv
            hTt = hTr.next()
            trans_part(k, xn_p, lambda kc, hTt=hTt: (hTt.t[:, kc, :], hTt.b), A_p, B_p, k.ident, pools)
            k.store("sp", hTt, d["hTd"][key_p[0]][key_p[1]], hTt.t[:].rearrange("p a b -> p (a b)"),
                    w=[k.dbuf(("hTd", key_p))])

        def tile_loads(tile):
            yt = yrot.next(); xt = xrot.next()
            k.load("sp", yt, yt.t[:], tile[1])
            k.load("sp", xt, xt.t[:], tile[2])
            return yt, xt
        nxt_ld = tile_loads(tl[0])
        for ti, tile in enumerate(tl + [None]):
            cur = None
            if tile is not None:
                (cond, yin_ap, xin_ap, xmid_ap, xout_ap, key) = tile
                yt, xt = nxt_ld
                if ti + 1 < len(tl):
                    nxt_ld = tile_loads(tl[ti + 1])
                if cond != cur_cond:
                    cur_cond = cond
                    k.load("sp", G1, G1.t[:], d["mod"][l][cond][2 * D:3 * D].partition_broadcast(128))
                    k.load("sp", gtmp, gtmp.t[:], d["norm_g"][l][1].partition_broadcast(128))
                    k.tt("pool", G1.t[:], G1.t[:], gtmp.t[:], ALU.mult, r=[G1.b, gtmp.b], w=[G1.b])
                    if cond not in cols:
                        cols[cond] = mod_cols(k, ph, l, cond, 2, 3, 4, "m%d" % cond)
                A2, B2 = cols[cond]
                yT = yTr.next(); fg = fgr.next(); sq = sqr.next()
                for g4 in range(4):
                    pt = pools[3].next()
                    for j in range(4):
                        kc = g4 * 4 + j
                        k.tr(pt.t[:, j, :], yt.t[:, kc * 128:(kc + 1) * 128], k.ident.t[:], r=[yt.b, k.ident.b], w=[pt.b])
                    k.copy(k.evac_eng(), yT.t[:, g4 * 4:(g4 + 1) * 4, :], pt.t[:], r=[pt.b], w=[yT.b])
                pss4 = []
                for cb in range(4):
                    cs = slice(cb * 512, (cb + 1) * 512)
                    ps = prot.next()
                    pss4.append(ps)
                    for kc in range(16):
                        k.mm(ps.t[:], yT.t[:, kc, :], Wo.t[:, kc, cs], kc == 0, kc == 15, r=[yT.b, Wob[cb]], w=[ps.b])
                if prev is not None:
                    finish_prev(prev)
                    prev = None
                for cb in range(4):
                    cs = slice(cb * 512, (cb + 1) * 512)
                    ps = pss4[cb]
                    k.tt("dve", fg.t[:, cs], ps.t[:], G1.t[:, cs], ALU.mult, r=[ps.b, G1.b], w=[fg.b])
                    jk = jk2.next()
                    k.act(jk.t[:], ps.t[:], AF.Square, r=[ps.b], w=[jk.b, sq.b], accum=sq.t[:, cb:cb + 1])
                P.op("dve", lambda e, sq=sq: e.tensor_reduce(out=sq.t[:, 8:9], in_=sq.t[:, 0:4], axis=AX.X, op=ALU.add),
                     r=[sq.b], w=[sq.b])
                k.act(sq.t[:, 9:10], sq.t[:, 8:9], AF.Sqrt, r=[sq.b], w=[sq.b], bias=EPS_T(k), scale=1.0 / D)
                k.recip(sq.t[:, 10:11], sq.t[:, 9:10], r=[sq.b], w=[sq.b])
                k.stt("dve", xt.t[:], fg.t[:], sq.t[:, 10:11], xt.t[:], ALU.mult, ALU.add, r=[fg.b, sq.b, xt.b], w=[xt.b])
                k.store("sp", xt, xmid_ap, xt.t[:], w=[k.dbuf(("xmid", key))])
                xn = norm_part(k, xt, pools)
                cur = (xn, A2, B2, key)
            if prev is not None:
                finish_prev(prev)
            prev = cur


def phase_M(k, l, blocks):
    d = k.d
    P = k.P
    with k.phase("M%d" % l) as ph:
        NTB = 4
        uT = ph.sb("uT", [128, 64, NTB * 128], BF16)
        hT = ph.sb("hT", [128, NTB, 16, 128], BF16)
        fg = ph.sb("fg", [128, NTB, D], F32)
        fgb = [Buf("fg%d" % i) for i in range(NTB)]
        hTb = [Buf("hTb%d" % i) for i in range(NTB)]
        xrot = Rot(ph.sb("x", [128, D], F32, n=2))
        wrot = Rot(ph.sb("w", [128, 16, 256], BF16, n=3))
        w2rot = Rot(ph.sb("w2", [128, 4, 512], BF16, n=4))
        G3 = ph.sb("G3", [128, D], F32)
        gtmp = ph.sb("gtmp", [128, D], F32)
        rlr = Rot(ph.sb("rl", [128, NTB * 128], BF16, n=2))
        sqr = Rot(ph.sb("sq", [128, 16], F32, n=NTB * 2))
        jk2 = Rot(ph.sb("jk2", [128, 512], BF16, n=1))
        prot = Rot(ph.ps("ps", [128, 512], F32, n=8))
        ubuf = [Buf("u%d" % i) for i in range(64)]
        cur_cond = None

        def w1_load(mb):
            wt = wrot.next()
            k.load("sp", wt, wt.t[:], d["w1b"][l][:, mb * 256:(mb + 1) * 256].rearrange("(kc p) n -> p kc n", p=128),
                   r=[k.dbuf(("w1b", l, i)) for i in range(16)])
            return wt

        def block_prefetch(blk):
            for t, (_, yin_ap, xin_ap, xmid_ap, xout_ap, key) in enumerate(blk):
                k.dma("sp", hT.t[:, t, :, :].rearrange("p a b -> p (a b)"), d["hTd"][key[0]][key[1]], hT.b,
                      r=[k.dbuf(("hTd", key))], w=[hTb[t]])
            return [w1_load(mb) for mb in range(3)]
        pre_w = None
        for bi, blk in enumerate(blocks):
            nt = len(blk)
            N = nt * 128
            cond = blk[0][0]
            if cond != cur_cond:
                cur_cond = cond
                k.load("sp", G3, G3.t[:], d["mod"][l][cond][5 * D:6 * D].partition_broadcast(128))
                k.load("sp", gtmp, gtmp.t[:], d["norm_g"][l][3].partition_broadcast(128))
                k.tt("pool", G3.t[:], G3.t[:], gtmp.t[:], ALU.mult, r=[G3.b, gtmp.b], w=[G3.b])
            if pre_w is None:
                pre_w = block_prefetch(blk)
            my_w = pre_w
            pre_w = None
            for mb in range(32):
                if mb < len(my_w):
                    wt = my_w[mb]
                else:
                    wt = w1_load(mb)
                for mc in range(2):
                    ps = prot.next()
                    for kc in range(16):
                        k.mm(ps.t[:, 0:N].rearrange("p (a b) -> p a b", a=nt), wt.t[:, kc, mc * 128:(mc + 1) * 128],
                             hT.t[:, 0:nt, kc, :], kc == 0, kc == 15, r=[wt.b] + hTb[:nt], w=[ps.b])
                    rl = rlr.next()
                    k.act(rl.t[:, 0:N], ps.t[:, 0:N], AF.Relu, r=[ps.b], w=[rl.b])
                    m = mb * 2 + mc
                    k.tt("pool", uT.t[:, m, 0:N], rl.t[:, 0:N], rl.t[:, 0:N], ALU.mult, r=[rl.b], w=[ubuf[m]])
            sqs = [sqr.next() for _ in range(nt)]
            for cb in range(4):
                cs = slice(cb * 512, (cb + 1) * 512)
                pss = [prot.next() for _ in range(nt)]
                for g in range(16):
                    w2 = w2rot.next()
                    k.load("sp", w2, w2.t[:], d["w2b"][l][g * 512:(g + 1) * 512, cs].rearrange("(kc p) n -> p kc n", p=128),
                           r=[k.dbuf(("w2b", l, g))])
                    if cb == 3 and g == 12 and bi + 1 < len(blocks):
                        pre_w = block_prefetch(blocks[bi + 1])
                    for t in range(nt):
                        for j in range(4):
                            m = g * 4 + j
                            k.mm(pss[t].t[:], uT.t[:, m, t * 128:(t + 1) * 128], w2.t[:, j, :], m == 0, m == 63,
                                 r=[ubuf[m], w2.b], w=[pss[t].b])
                for t in range(nt):
                    k.tt("dve", fg.t[:, t, cs], pss[t].t[:], G3.t[:, cs], ALU.mult, r=[pss[t].b, G3.b], w=[fgb[t]])
                    jk = jk2.next()
                    k.act(jk.t[:], pss[t].t[:], AF.Square, r=[pss[t].b], w=[jk.b, sqs[t].b], accum=sqs[t].t[:, cb:cb + 1])
            for t, (_, yin_ap, xin_ap, xmid_ap, xout_ap, key) in enumerate(blk):
                sq = sqs[t]
                xt = xrot.next()
                k.load("sp", xt, xt.t[:], xmid_ap, r=[k.dbuf(("xmid", key))])
                P.op("dve", lambda e, sq=sq: e.tensor_reduce(out=sq.t[:, 8:9], in_=sq.t[:, 0:4], axis=AX.X, op=ALU.add),
                     r=[sq.b], w=[sq.b])
                k.act(sq.t[:, 9:10], sq.t[:, 8:9], AF.Sqrt, r=[sq.b], w=[sq.b], bias=EPS_T(k), scale=1.0 / D)
                k.recip(sq.t[:, 10:11], sq.t[:, 9:10], r=[sq.b], w=[sq.b])
                k.stt("dve", xt.t[:], fg.t[:, t, :], sq.t[:, 10:11], xt.t[:], ALU.mult, ALU.add, r=[fgb[t], sq.b, xt.b], w=[xt.b])
                k.store("sp", xt, xout_ap, xt.t[:])


def phase_post2(k, l, blocks, w_out_ap):
    tiles = [t for blk in blocks for t in blk]
    phase_P(k, l, tiles, w_out_ap)
    phase_M(k, l, blocks)


def post_blocks0(k):
    d = k.d
    blocks = []
    for bl in range(NB):
        def tile(tt, bl=bl):
            rows = slice(tt * 128, (tt + 1) * 128)
            return (2 if tt < 2 else bl, d["yin"][bl][rows, :], token_src(k, 0, bl, tt), d["xmid"][bl][rows, :],
                    d["xl1"][bl][rows, :], (bl, tt))
        blocks.append([tile(0), tile(1)])
        for b4 in range(4):
            blocks.append([tile(2 + b4 * 4 + i) for i in range(4)])
    return blocks


def build_hT(k, ph, l, bl, hT, hTb, ntr=2):
    A_c, B_c = mod_cols(k, ph, l, 2, 0, 0, 1, "c")
    A_l, B_l = mod_cols(k, ph, l, bl, 0, 0, 1, "l")
    xrot = Rot(ph.sb("x", [128, D], F32, n=3))
    pools = (Rot(ph.sb("junk", [128, D], BF16, n=1)), Rot(ph.sb("ss", [128, 4], F32, n=3)),
             Rot(ph.sb("xn", [128, D], BF16, n=3)), Rot(ph.ps("tr", [128, 4, 128], BF16, n=ntr)))
    prev = None
    for tt in range(NT + 1):
        cur = None
        if tt < NT:
            xt = xrot.next()
            k.load("sp", xt, xt.t[:], token_src(k, l, bl, tt))
            cur = (tt, norm_part(k, xt, pools))
        if prev is not None:
            pt_, xn = prev
            A, Bm = (A_c, B_c) if pt_ < 2 else (A_l, B_l)
            trans_part(k, xn, lambda kc, pt_=pt_: (hT.t[:, kc, pt_ * 128:(pt_ + 1) * 128], hTb[pt_]), A, Bm, k.ident, pools)
        prev = cur


def phase_inproj1(k, bl):
    d = k.d
    P = k.P
    with k.phase("ip1") as ph:
        hT = ph.sb("hT", [128, 16, NTOK], BF16)
        hTb = [Buf("hT%d" % i) for i in range(NT)]
        build_hT(k, ph, 1, bl, hT, hTb)
        cosT = ph.sb("cosT", [128, SEQ], F32)
        sinT = ph.sb("sinT", [128, SEQ], F32)
        ang = ph.sb("ang", [128, SEQ], F32)
        tmpT = ph.sb("tmpT", [128, SEQ], F32)
        pcol = ph.sb("pcol", [128, 4], F32)
        Lm = ph.sb("Lm", [128, 128], BF16)
        Rm = ph.sb("Rm", [128, 32], F32)
        Jm = ph.sb("Jm", [128, 32], F32)
        k.memset("pool", Rm.t[:], 0.0, w=[Rm.b])
        for i in range(4):
            P.op("pool", lambda e, i=i: e.affine_select(out=Rm.t[:], in_=Rm.t[:], pattern=[[-1, 32]], compare_op=ALU.not_equal,
                                                         fill=1.0, base=-32 * i, channel_multiplier=1), r=[Rm.b], w=[Rm.b])
        P.op("pool", lambda e: e.iota(Jm.t[:], [[1, 32]], base=0, channel_multiplier=0,
                                      allow_small_or_imprecise_dtypes=True), w=[Jm.b])
        k.tt("dve", Rm.t[:], Rm.t[:], Jm.t[:], ALU.mult, r=[Rm.b, Jm.b], w=[Rm.b])
        P.op("dve", lambda e: e.tensor_reduce(out=pcol.t[:, 1:2], in_=Rm.t[:], axis=AX.X, op=ALU.add), r=[Rm.b], w=[pcol.b])
        k.act(pcol.t[:, 2:3], pcol.t[:, 1:2], AF.Exp, r=[pcol.b], w=[pcol.b], scale=-math.log(10000.0) / 32.0)
        P.op("pool", lambda e: e.iota(ang.t[0:64, :].rearrange("p (a b) -> p a b", a=32), [[1, 32], [0, 64]], base=0,
                                      channel_multiplier=0, allow_small_or_imprecise_dtypes=True), w=[ang.b])
        P.op("pool", lambda e: e.iota(ang.t[64:128, :].rearrange("p (a b) -> p a b", a=32), [[0, 32], [1, 64]], base=0,
                                      channel_multiplier=0, allow_small_or_imprecise_dtypes=True), w=[ang.b])
        k.ts("dve", ang.t[:], ang.t[:], pcol.t[:, 2:3], None, ALU.mult, None, r=[ang.b, pcol.b], w=[ang.b])
        k.act(tmpT.t[:], ang.t[:], AF.Sin, r=[ang.b], w=[tmpT.b], scale=1.0 / 64)
        k.act(sinT.t[:], ang.t[:], AF.Sin, r=[ang.b], w=[sinT.b], scale=1.0 / 32)
        k.tt("dve", tmpT.t[:], tmpT.t[:], tmpT.t[:], ALU.mult, r=[tmpT.b], w=[tmpT.b])
        k.ts("dve", cosT.t[:], tmpT.t[:], -2.0, 1.0, ALU.mult, ALU.add, r=[tmpT.b], w=[cosT.b])
        for it in range(5):
            k.tt("dve", tmpT.t[:], sinT.t[:], sinT.t[:], ALU.mult, r=[sinT.b], w=[tmpT.b])
            k.stt("dve", sinT.t[:], sinT.t[:], 2.0, cosT.t[:], ALU.mult, ALU.mult, r=[sinT.b, cosT.b], w=[sinT.b])
            k.ts("dve", cosT.t[:], tmpT.t[:], -2.0, 1.0, ALU.mult, ALU.add, r=[tmpT.b], w=[cosT.b])
        k.memset("pool", Lm.t[:], 0.0, w=[Lm.b])
        for a in range(2):
            for half, val in ((0, -1.0), (1, 1.0)):
                c0 = a * 64 + half * 32
                off = 32 if half == 0 else -32
                P.op("pool", lambda e, c0=c0, off=off, val=val: e.affine_select(
                    out=Lm.t[:, c0:c0 + 32], in_=Lm.t[:, c0:c0 + 32], pattern=[[-1, 32]], compare_op=ALU.not_equal,
                    fill=val, base=-c0 - off, channel_multiplier=1), r=[Lm.b], w=[Lm.b])
        wrot = Rot(ph.sb("w", [128, 16, 512], BF16, n=2))
        prot = Rot(ph.ps("ps", [128, 512], F32, n=4))
        rrot = Rot(ph.ps("rp", [128, 512], F32, n=2))
        qrr = Rot(ph.sb("qr", [128, 512], BF16, n=3))
        t1r = Rot(ph.sb("t1", [128, 512], F32, n=2))
        t2r = Rot(ph.sb("t2", [128, 512], F32, n=2))
        orot = Rot(ph.sb("o", [128, 512], BF16, n=4))
        pending = [None]
        for which in ("q", "k"):
            for wb in range(4):
                wt = wrot.next()
                c0 = (0 if which == "q" else D) + wb * 512
                k.load("pool", wt, wt.t[:], d["da_w_in"][0][:, c0:c0 + 512].rearrange("(kc p) n -> p kc n", p=128))
                for j in range(4):
                    mc = wb * 4 + j
                    groups = [(256 + g * 512, 512, True) for g in range(4)]
                    if which == "k":
                        groups = [(0, 256, False)] + groups
                    for (t0, n, rope) in groups:
                        ps = prot.next()
                        for kc in range(16):
                            k.mm(ps.t[:, 0:n], wt.t[:, kc, j * 128:(j + 1) * 128], hT.t[:, kc, t0:t0 + n], kc == 0, kc == 15,
                                 r=[wt.b] + hTb, w=[ps.b])
                        ot = orot.next()
                        if not rope:
                            k.copy("act", ot.t[:, 0:n], ps.t[:, 0:n], r=[ps.b], w=[ot.b])
                            k.store("sp", ot, d["kT1"][bl][mc][:, t0:t0 + n], ot.t[:, 0:n])
                            continue
                        qr = qrr.next()
                        l0 = t0 - 256
                        k.copy("act", qr.t[:], ps.t[:], r=[ps.b], w=[qr.b])
                        if pending[0] is not None:
                            pending[0]()

                        def fin(ps=ps, qr=qr, ot=ot, l0=l0, t0=t0, mc=mc, which=which):
                            t1 = t1r.next(); t2 = t2r.next(); rp = rrot.next()
                            k.mm(rp.t[:], Lm.t[:], qr.t[:], True, True, r=[Lm.b, qr.b], w=[rp.b])
                            k.tt("dve", t1.t[:], ps.t[:], cosT.t[:, l0:l0 + 512], ALU.mult, r=[ps.b, cosT.b], w=[t1.b])
                            k.tt("dve", t2.t[:], rp.t[:], sinT.t[:, l0:l0 + 512], ALU.mult, r=[rp.b, sinT.b], w=[t2.b])
                            k.tt("pool", ot.t[:], t1.t[:], t2.t[:], ALU.add, r=[t1.b, t2.b], w=[ot.b])
                            if which == "q":
                                k.store("sp", ot, d["qT1"][bl][mc][:, l0:l0 + 512], ot.t[:])
                            else:
                                k.store("sp", ot, d["kT1"][bl][mc][:, t0:t0 + 512], ot.t[:])
                        pending[0] = fin
        if pending[0] is not None:
            pending[0]()
            pending[0] = None
        for wb in range(4):
            wt = wrot.next()
            c0 = 2 * D + wb * 512
            k.load("pool", wt, wt.t[:], d["da_w_in"][0][:, c0:c0 + 512].rearrange("(kc p) n -> p kc n", p=128))
            for tt in range(NT):
                ps = prot.next()
                for kc in range(16):
                    k.mm(ps.t[:], hT.t[:, kc, tt * 128:(tt + 1) * 128], wt.t[:, kc, :], kc == 0, kc == 15,
                         r=[hTb[tt], wt.b], w=[ps.b])
                ot = orot.next()
                k.copy(k.evac_eng(), ot.t[:], ps.t[:], r=[ps.b], w=[ot.b])
                k.store("sp", ot, d["v1"][bl][tt * 128:(tt + 1) * 128, wb * 512:(wb + 1) * 512], ot.t[:])


LAM_INIT1 = 0.8 - 0.6 * math.exp(-0.3 * 1)


def phase_attn1(k, bl):
    d = k.d
    P = k.P
    with k.phase("attn") as ph:
        lam = ph.sb("lam", [128, 8], F32)
        lq = ph.sb("lq", [128, 4, 128], F32)
        lj = ph.sb("lj", [128, 128], F32)
        k.load("sp", lq, lq.t[:], d["da_lambda"][0].rearrange("a b -> (a b)").partition_broadcast(128).rearrange("p (a b) -> p a b", a=4))
        for i in range(2):
            k.tt("dve", lj.t[:], lq.t[:, 2 * i, :], lq.t[:, 2 * i + 1, :], ALU.mult, r=[lq.b], w=[lj.b])
            P.op("dve", lambda e, i=i: e.tensor_reduce(out=lam.t[:, i:i + 1], in_=lj.t[:], axis=AX.X, op=ALU.add),
                 r=[lj.b], w=[lam.b])
        k.act(lam.t[:, 2:4], lam.t[:, 0:2], AF.Exp, r=[lam.b], w=[lam.b])
        k.tt("dve", lam.t[:, 4:5], lam.t[:, 2:3], lam.t[:, 3:4], ALU.subtract, r=[lam.b], w=[lam.b])
        k.ts("dve", lam.t[:, 5:6], lam.t[:, 4:5], LAM_INIT1, None, ALU.add, None, r=[lam.b], w=[lam.b])
        gs = ph.sb("gs", [128, 256], F32)
        k.load("sp", gs, gs.t[:], d["da_sub_g"][0].partition_broadcast(128))
        k.ts("dve", gs.t[:], gs.t[:], 1.0 - LAM_INIT1, None, ALU.mult, None, r=[gs.b], w=[gs.b])
        qtr = Rot(ph.sb("QT", [128, 2, SEQ], BF16, n=2))
        ktr = Rot(ph.sb("KT", [128, 2, NTOK], BF16, n=2))
        var = ph.sb("VA", [128, NT, 257], BF16, n=2)
        for v in var:
            k.memset("pool", v.t[:, :, 256:257], 1.0, w=[v.b])
        var = Rot(var)
        ptr_ = Rot(ph.sb("PT", [128, 512], BF16, n=4))
        t0r = Rot(ph.sb("t0", [128, 4, 256], F32, n=2))
        orr = Rot(ph.sb("oo", [128, 256], F32, n=2))
        yir = Rot(ph.sb("yi", [128, 256], BF16, n=3))
        rlr = Rot(ph.sb("rl", [128, 8], F32, n=4))
        jkr = Rot(ph.sb("jk", [128, 256], BF16, n=1))
        st_ps = Rot(ph.ps("st", [128, 512], F32, n=4))
        o_ps = Rot(ph.ps("o", [128, 257], F32, n=4))
        scale = 128.0 ** -0.5
        def head_loads(h):
            QT = qtr.next(); KT = ktr.next(); VA = var.next()
            for c in range(2):
                k.load("sp", QT, QT.t[:, c, :], d["qT1"][bl][2 * h + c])
                k.load("sp", KT, KT.t[:, c, :], d["kT1"][bl][2 * h + c])
            k.load("sp", VA, VA.t[:, :, 0:256], d["v1"][bl][:, h * 256:(h + 1) * 256].rearrange("(kc p) e -> p kc e", p=128))
            return QT, KT, VA
        nxt_h = head_loads(0)
        for h in range(H):
            QT, KT, VA = nxt_h
            if h + 1 < H:
                nxt_h = head_loads(h + 1)
            for qb in range(4):
                t0 = t0r.next()
                for c in range(2):
                    os_ = [o_ps.next() for _ in range(4)]
                    sts = {}

                    def issue_s(kc, c=c, qb=qb):
                        st = st_ps.next()
                        k.mm(st.t[:], KT.t[:, c, kc * 128:(kc + 1) * 128], QT.t[:, c, qb * 512:(qb + 1) * 512], True, True,
                             r=[KT.b, QT.b], w=[st.b])
                        sts[kc] = st
                    issue_s(0)
                    issue_s(1)
                    for kc in range(NT):
                        if kc + 2 < NT:
                            issue_s(kc + 2)
                        st = sts.pop(kc)
                        pt = ptr_.next()
                        k.act(pt.t[:], st.t[:], AF.Exp, r=[st.b], w=[pt.b], scale=scale)
                        for qt in range(4):
                            k.mm(os_[qt].t[:], pt.t[:, qt * 128:(qt + 1) * 128], VA.t[:, kc, :], kc == 0, kc == NT - 1,
                                 r=[pt.b, VA.b], w=[os_[qt].b])
                    for qt in range(4):
                        o = os_[qt]
                        rl = rlr.next()
                        k.recip(rl.t[:, 0:1], o.t[:, 256:257], r=[o.b], w=[rl.b])
                        if c == 0:
                            k.ts("dve", t0.t[:, qt, :], o.t[:, 0:256], rl.t[:, 0:1], None, ALU.mult, None, r=[o.b, rl.b], w=[t0.b])
                        else:
                            oo = orr.next(); yi = yir.next(); jk = jkr.next()
                            k.ts("dve", rl.t[:, 1:2], rl.t[:, 0:1], lam.t[:, 5:6], -1.0, ALU.mult, ALU.mult, r=[rl.b, lam.b], w=[rl.b])
                            k.stt("dve", oo.t[:], o.t[:, 0:256], rl.t[:, 1:2], t0.t[:, qt, :], ALU.mult, ALU.add,
                                  r=[o.b, rl.b, t0.b], w=[oo.b])
                            k.act(jk.t[:], oo.t[:], AF.Square, r=[oo.b], w=[jk.b, rl.b], accum=rl.t[:, 2:3])
                            k.act(rl.t[:, 3:4], rl.t[:, 2:3], AF.Sqrt, r=[rl.b], w=[rl.b], bias=EPS_T(k), scale=1.0 / 256)
                            k.recip(rl.t[:, 4:5], rl.t[:, 3:4], r=[rl.b], w=[rl.b])
                            k.stt("dve", yi.t[:], oo.t[:], rl.t[:, 4:5], gs.t[:], ALU.mult, ALU.mult, r=[oo.b, rl.b, gs.b], w=[yi.b])
                            r0 = 256 + qb * 512 + qt * 128
                            k.store("sp", yi, d["yin"][bl][r0:r0 + 128, h * 256:(h + 1) * 256], yi.t[:])


def post_blocks1(k):
    d = k.d
    blocks = []
    for bl in range(NB):
        def tile(tt, bl=bl):
            rows = slice(tt * 128, (tt + 1) * 128)
            orow = slice((tt - 2) * 128, (tt - 1) * 128)
            return (bl, d["yin"][bl][rows, :], d["xl1"][bl][rows, :], d["xmid"][bl][rows, :],
                    d["out"][bl][orow, :], (bl, tt))
        for b4 in range(4):
            blocks.append([tile(2 + b4 * 4 + i) for i in range(4)])
    return blocks


def declare_dram(k):
    nc = k.nc
    need = getattr(k, "need_inputs", None)
    inp = [("x", [NB, SEQ, D]), ("ctx", [NB, CTX, D]), ("c", [NB, D]), ("c_ctx", [D]),
           ("ada_w", [2, D, 6 * D]), ("ada_b", [2, 6 * D]), ("norm_g", [2, 4, D]),
           ("mlp_w1", [2, D, DFF]), ("mlp_w2", [2, DFF, D]), ("ml_w_in", [1, D, ML_IN]),
           ("ml_b_gates", [1, 32]), ("ml_head_g", [1, D]), ("ml_w_out", [1, D, D]),
           ("da_w_in", [1, D, 3 * D]), ("da_lambda", [1, 4, 128]), ("da_sub_g", [1, 256]), ("da_w_out", [1, D, D])]
    for name, shape in inp:
        k.dram(name, shape, F32, kind="ExternalInput" if (need is None or name in need) else "Internal")
    k.dram("out", [NB, SEQ, D], F32, kind="ExternalOutput")
    k.dram("mod", [2, 3, 6 * D], F32)
    k.dram("w1b", [2, D, DFF], BF16)
    k.dram("w2b", [2, DFF, D], BF16)
    k.dram("qk0", [NB, NTOK, D], BF16)
    k.dram("v0", [NB, NTOK, D], BF16)
    k.dram("og0", [NB, NTOK, D], BF16)
    k.dram("gt0", [NB, NTOK, 32], F32)
    k.dram("hf0", [NB, NTOK, D], F32)
    k.dram("hb0", [NB, NTOK, D], F32)
    k.dram("yin", [NB, NTOK, D], BF16)
    k.dram("xmid", [NB, NTOK, D], F32)
    k.dram("hTd", [NB, NT, 128, 16 * 128], BF16)
    k.dram("xl1", [NB, NTOK, D], F32)
    k.dram("qT1", [NB, 16, 128, SEQ], BF16)
    k.dram("kT1", [NB, 16, 128, NTOK], BF16)
    k.dram("v1", [NB, NTOK, D], BF16)


def build(phases, dbg=(), ext_in=(), need_inputs=None):
    nc = bass.Bass("TRN2", target_bir_lowering=False)
    k = K(nc, dbg, ext_in)
    k.need_inputs = need_inputs
    declare_dram(k)
    setup_consts(k)
    for ph in phases:
        ph(k)
    k.P.emit()
    return nc, k


def all_phases(k):
    phase_ada(k)
    for bl in range(NB):
        phase_inproj0(k, bl)
        phase_scan0(k, bl, cast_thunks(k, bl))
    phase_post2(k, 0, post_blocks0(k), k.d["ml_w_out"][0])
    for bl in range(NB):
        phase_inproj1(k, bl)
        phase_attn1(k, bl)
    phase_post2(k, 1, post_blocks1(k), k.d["da_w_out"][0])


_NC_CACHE = {}


def kernel(**inputs):
    n = 8
    if "nc" not in _NC_CACHE:
        _NC_CACHE["nc"] = build([all_phases])[0]
    nc = _NC_CACHE["nc"]
    shared = {kk: np.ascontiguousarray(v, dtype=np.float32) for kk, v in inputs.items() if kk not in ("x", "c", "ctx")}
    in_maps = []
    for i in range(n):
        m = dict(shared)
        for kk in ("x", "c", "ctx"):
            m[kk] = np.ascontiguousarray(inputs[kk][i * NB:(i + 1) * NB], dtype=np.float32)
        in_maps.append(m)
    res = run_bass_kernel_spmd(nc, in_maps, core_ids=list(range(n)))
    return np.concatenate([r["out"] for r in res.results], axis=0).astype(np.float32)
```

```python
import math
import numpy as np
import concourse.bass as bass
import concourse.mybir as mybir
from concourse.bass_utils import run_bass_kernel_spmd

F32 = mybir.dt.float32
BF16 = mybir.dt.bfloat16
I32 = mybir.dt.int32
AF = mybir.ActivationFunctionType
ALU = mybir.AluOpType
AX = mybir.AxisListType

D = 2048
DFF = 8192
SEQ = 2048
CTX = 256
NTOK = SEQ + CTX
NT = NTOK // 128
NB = 2
H = 8
EPS = 1e-6
ML_IN = 6176
EPOCH = 12000
POST_STOP = 0

ENGS = ("pe", "act", "dve", "pool", "sp")


class Buf:
    __slots__ = ("name", "writer", "readers", "sem", "ndma", "excl")

    def __init__(self, name, excl=False):
        self.name = name
        self.excl = excl
        self.writer = None
        self.readers = {}
        self.sem = None
        self.ndma = 0


class Op:
    __slots__ = ("eng", "fn", "deps", "signal", "sem", "tick", "dma", "idx", "vsem")


class VSem:
    __slots__ = ("count", "handle")

    def __init__(self):
        self.count = 0
        self.handle = None


class Prog:
    def __init__(self, nc):
        self.nc = nc
        self.q = {e: [] for e in ENGS}
        self.fence = {e: None for e in ENGS}
        self.dma_last = {}
        self.nsem = 0
        self.free_vsems = []
        self.live_dma_bufs = []

    def op(self, eng, fn, r=(), w=(), dma=None):
        o = Op()
        o.eng = eng; o.fn = fn; o.signal = False; o.sem = None; o.tick = 0; o.dma = dma
        deps = []
        f = self.fence[eng]
        if f is not None:
            deps.extend(f)
            self.fence[eng] = None
        for b in r:
            d = b.writer
            if d is not None:
                if d.dma is not None or d.eng != eng or eng != "pe":
                    deps.append(d)
            if b.excl:
                for d in b.readers.values():
                    if d.eng != eng:
                        deps.append(d)
        for b in w:
            d = b.writer
            if d is not None and (d.dma is not None or d.eng != eng):
                deps.append(d)
            for d in b.readers.values():
                if d.dma is not None or d.eng != eng:
                    deps.append(d)
        for b in r:
            b.readers[id(dma) if dma is not None else eng] = o
        for b in w:
            b.writer = o
            b.readers = {}
        seen = set()
        dd = []
        for d in deps:
            if id(d) not in seen:
                seen.add(id(d)); dd.append(d); d.signal = True
        o.deps = dd
        o.idx = len(self.q[eng])
        self.q[eng].append(o)
        if dma is not None:
            o.signal = True
            self.dma_last[id(dma)] = o
            if dma.sem is None:
                dma.sem = self.free_vsems.pop() if self.free_vsems else VSem()
                self.live_dma_bufs.append(dma)
            dma.sem.count += 1
            o.vsem = dma.sem
            o.tick = 16 * dma.sem.count
            assert o.tick < 60000, dma.name
        return o

    def release_dma_sems(self):
        for b in self.live_dma_bufs:
            self.free_vsems.append(b.sem)
            self.dma_last.pop(id(b), None)
            b.sem = None
        self.live_dma_bufs = []

    def barrier(self):
        ops = []
        for e in ENGS:
            for o in reversed(self.q[e]):
                if o.dma is None:
                    o.signal = True
                    ops.append(o)
                    break
        ops.extend(self.dma_last.values())
        for e in ENGS:
            self.fence[e] = list(ops) + (self.fence[e] or [])

    def emit(self):
        nc = self.nc
        self.barrier()
        for e in ENGS:
            cnt = 0
            sem = None
            for o in self.q[e]:
                if o.dma is not None:
                    v = o.vsem
                    if v.handle is None:
                        v.handle = nc.alloc_semaphore("d%d" % self.nsem); self.nsem += 1
                    o.sem = v.handle
                elif o.signal:
                    if sem is None or cnt >= EPOCH:
                        sem = nc.alloc_semaphore("e%d" % self.nsem); self.nsem += 1
                        cnt = 0
                    cnt += 1
                    o.sem = sem
                    o.tick = cnt
        q = self.q

        def run(ename):
            def f(e):
                known = {}
                for o in q[ename]:
                    for d in o.deps:
                        k = known.get(d.sem, 0)
                        if d.tick > k:
                            e.wait_ge(d.sem, d.tick)
                            known[d.sem] = d.tick
                    ins = o.fn(e)
                    if o.signal:
                        ins.then_inc(o.sem, 16 if o.dma is not None else 1)
                fin = self.fence[ename]
                if fin:
                    for d in fin:
                        k = known.get(d.sem, 0)
                        if d.tick > k:
                            e.wait_ge(d.sem, d.tick)
                            known[d.sem] = d.tick
            return f

        with nc.Block() as block:
            block.sync(run("sp"))
            block.tensor(run("pe"))
            block.scalar(run("act"))
            block.vector(run("dve"))
            block.gpsimd(run("pool"))


class Pool:
    def __init__(self, K, name, shape, dtype, n=1, psum=False):
        self.tiles = []
        for i in range(n):
            nm = "%s_%d_%d" % (name, K.uid(), i)
            if psum:
                t = K.nc.alloc_psum_tensor(nm, shape, dtype)
            else:
                t = K.nc.alloc_sbuf_tensor(nm, shape, dtype)
            self.tiles.append((t, Buf(nm)))
        self.i = 0

    def next(self):
        t = self.tiles[self.i % len(self.tiles)]
        self.i += 1
        return t


class Tile:
    __slots__ = ("t", "b")

    def __init__(self, t, b):
        self.t = t
        self.b = b


class Phase:
    def __init__(self, K, name):
        self.K = K
        self.name = name
        self.stack = []

    def __enter__(self):
        return self

    def __exit__(self, *a):
        self.K.P.barrier()
        self.K.P.release_dma_sems()
        for g in reversed(self.stack):
            g.__exit__(None, None, None)
        return False

    def sb(self, name, shape, dtype, n=None):
        K = self.K
        res = []
        for i in range(n or 1):
            nm = "%s_%s_%d_%d" % (self.name, name, K.uid(), i)
            g = K.nc.sbuf_tensor(nm, shape, dtype)
            t = g.__enter__()
            self.stack.append(g)
            res.append(Tile(t, Buf(nm)))
        return res if n is not None else res[0]

    def ps(self, name, shape, dtype, n=None):
        K = self.K
        res = []
        for i in range(n or 1):
            nm = "%s_%s_%d_%d" % (self.name, name, K.uid(), i)
            g = K.nc.psum_tensor(nm, shape, dtype)
            t = g.__enter__()
            self.stack.append(g)
            res.append(Tile(t, Buf(nm, excl=True)))
        return res if n is not None else res[0]


class Rot:
    def __init__(self, tiles):
        self.tiles = tiles
        self.i = 0

    def next(self):
        t = self.tiles[self.i % len(self.tiles)]
        self.i += 1
        return t


class K:
    def __init__(self, nc, dbg=(), ext_in=()):
        self.ext_in = set(ext_in)
        self.nc = nc
        self.P = Prog(nc)
        self._uid = 0
        self.dbg = set(dbg)
        self.d = {}
        self.db = {}
        self.evac_i = 0

    def uid(self):
        self._uid += 1
        return self._uid

    def phase(self, name):
        return Phase(self, name)

    def dram(self, name, shape, dtype, kind=None):
        if kind is None:
            kind = "ExternalOutput" if name in self.dbg else ("ExternalInput" if name in self.ext_in else "Internal")
        self.d[name] = self.nc.dram_tensor(name, list(shape), dtype, kind=kind).ap()
        return self.d[name]

    def dbuf(self, key):
        b = self.db.get(key)
        if b is None:
            b = Buf(str(key))
            self.db[key] = b
        return b

    def dma(self, q, out, in_, slot, r=(), w=(), slow=False):
        if slow:
            fn = lambda e: e.dma_start(out=out, in_=in_, allow_slow_non_contiguous=True)
        else:
            fn = lambda e: e.dma_start(out=out, in_=in_)
        return self.P.op(q, fn, r=r, w=w, dma=slot)

    def load(self, q, tile, out, in_, r=(), slow=False):
        return self.dma(q, out, in_, tile.b, r=r, w=[tile.b], slow=slow)

    def store(self, q, tile, out, in_, w=(), slow=False):
        return self.dma(q, out, in_, tile.b, r=[tile.b], w=w, slow=slow)

    def mm(self, out, lhsT, rhs, start, stop, r, w):
        return self.P.op("pe", lambda e: e.matmul(out, lhsT=lhsT, rhs=rhs, start=start, stop=stop), r=r, w=w)

    def tr(self, out, in_, ident, r, w):
        return self.P.op("pe", lambda e: e.transpose(out, in_, ident), r=r, w=w)

    def act(self, out, in_, func, r, w, bias=None, scale=None, accum=None, eng="act"):
        kw = {}
        if bias is not None:
            kw["bias"] = bias
        if scale is not None:
            kw["scale"] = scale
        if accum is not None:
            kw["accum_out"] = accum
        return self.P.op("act", lambda e: e.activation(out=out, in_=in_, func=func, **kw), r=r, w=w)

    def ts(self, eng, out, in0, s1, s2, op0, op1, r, w):
        if op1 is None:
            return self.P.op(eng, lambda e: e.tensor_scalar(out=out, in0=in0, scalar1=s1, scalar2=None, op0=op0), r=r, w=w)
        return self.P.op(eng, lambda e: e.tensor_scalar(out=out, in0=in0, scalar1=s1, scalar2=s2, op0=op0, op1=op1), r=r, w=w)

    def tt(self, eng, out, in0, in1, op, r, w):
        return self.P.op(eng, lambda e: e.tensor_tensor(out=out, in0=in0, in1=in1, op=op), r=r, w=w)

    def stt(self, eng, out, in0, scalar, in1, op0, op1, r, w):
        return self.P.op(eng, lambda e: e.scalar_tensor_tensor(out=out, in0=in0, scalar=scalar, in1=in1, op0=op0, op1=op1), r=r, w=w)

    def copy(self, eng, out, in_, r, w):
        if eng == "act":
            return self.P.op("act", lambda e: e.activation(out=out, in_=in_, func=AF.Copy), r=r, w=w)
        return self.P.op(eng, lambda e: e.tensor_copy(out=out, in_=in_), r=r, w=w)

    def recip(self, out, in_, r, w):
        return self.P.op("dve", lambda e: e.reciprocal(out=out, in_=in_), r=r, w=w)

    def memset(self, eng, out, val, w):
        return self.P.op(eng, lambda e: e.memset(out, val), r=(), w=w)

    def evac_eng(self):
        self.evac_i += 1
        return "act" if self.evac_i % 2 else "dve"


def phase_ada(k):
    P = k.P
    d = k.d
    with k.phase("ada") as ph:
        cT = ph.sb("cT", [128, 16, 3], F32)
        for r in range(3):
            src = d["c"][r] if r < 2 else d["c_ctx"]
            src = src.rearrange("(kc p) -> p kc", p=128)
            k.load("sp", cT, cT.t[:, :, r], src, slow=True)
        k.act(cT.t[:], cT.t[:], AF.Silu, r=[cT.b], w=[cT.b])
        cTb = ph.sb("cTb", [128, 16, 3], BF16)
        k.copy("dve", cTb.t[:], cT.t[:], r=[cT.b], w=[cTb.b])
        wrot = Rot(ph.sb("w", [128, 16, 512], BF16, n=4))
        brot = Rot(ph.sb("b", [3, 512], F32, n=2))
        orot = Rot(ph.sb("o", [3, 512], F32, n=2))
        prot = Rot(ph.ps("ps", [128, 512], F32, n=2))
        for l in range(2):
            for j in range(24):
                wt = wrot.next(); bt = brot.next(); ot = orot.next(); ps = prot.next()
                cs = slice(j * 512, (j + 1) * 512)
                k.load("pool", wt, wt.t[:], d["ada_w"][l][:, cs].rearrange("(kc p) n -> p kc n", p=128))
                k.load("sp", bt, bt.t[:], d["ada_b"][l][cs].partition_broadcast(3))
                for kc in range(16):
                    k.mm(ps.t[0:3, :], cTb.t[:, kc, :], wt.t[:, kc, :], kc == 0, kc == 15, r=[cTb.b, wt.b], w=[ps.b])
                k.tt("dve", ot.t[:], ps.t[0:3, :], bt.t[:], ALU.add, r=[ps.b, bt.b], w=[ot.b])
                k.store("sp", ot, d["mod"][l][:, cs], ot.t[:])


def cast_thunks(k, l):
    d = k.d
    s1 = Buf("cast1_%d" % l)
    s2 = Buf("cast2_%d" % l)
    th = []
    for i in range(16):
        r0 = i * 128
        th.append(lambda r0=r0, i=i: k.dma("pool", d["w1b"][l][r0:r0 + 128, :], d["mlp_w1"][l][r0:r0 + 128, :], s1,
                                           w=[k.dbuf(("w1b", l, i))]))
    for i in range(16):
        r0 = i * 512
        th.append(lambda r0=r0, i=i: k.dma("pool", d["w2b"][l][r0:r0 + 512, :], d["mlp_w2"][l][r0:r0 + 512, :], s2,
                                           w=[k.dbuf(("w2b", l, i))]))
    out = []
    for i in range(16):
        out.append(th[i]); out.append(th[16 + i])
    return out


def load_cols(k, ph, name, src_vec):
    t = ph.sb(name, [128, 16], F32)
    k.load("sp", t, t.t[:], src_vec.rearrange("(kc p) -> p kc", p=128), slow=True)
    return t


def mod_cols(k, ph, l, r, g_idx, shift_i, scale_i, tag):
    d = k.d
    g = load_cols(k, ph, "g" + tag, d["norm_g"][l][g_idx])
    sc = load_cols(k, ph, "sc" + tag, d["mod"][l][r][scale_i * D:(scale_i + 1) * D])
    sh = load_cols(k, ph, "sh" + tag, d["mod"][l][r][shift_i * D:(shift_i + 1) * D])
    k.stt("dve", sc.t[:], sc.t[:], 1.0, g.t[:], ALU.add, ALU.mult, r=[sc.b, g.b], w=[sc.b])
    return sc, sh


def token_src(k, l, bl, tt):
    d = k.d
    if l == 0:
        if tt < 2:
            return d["ctx"][bl][tt * 128:(tt + 1) * 128, :]
        return d["x"][bl][(tt - 2) * 128:(tt - 1) * 128, :]
    return d["xl1"][bl][tt * 128:(tt + 1) * 128, :]


def norm_part(k, xt, pools):
    junk, ssr, xnr, trr = pools
    jk = junk.next(); ss = ssr.next(); xn = xnr.next()
    k.act(jk.t[:], xt.t[:], AF.Square, r=[xt.b], w=[jk.b, ss.b], accum=ss.t[:, 0:1])
    k.act(ss.t[:, 1:2], ss.t[:, 0:1], AF.Sqrt, r=[ss.b], w=[ss.b], bias=EPS_T(k), scale=1.0 / D)
    k.recip(ss.t[:, 2:3], ss.t[:, 1:2], r=[ss.b], w=[ss.b])
    k.ts("dve", xn.t[:], xt.t[:], ss.t[:, 2:3], None, ALU.mult, None, r=[xt.b, ss.b], w=[xn.b])
    return xn


def trans_part(k, xn, hT_dst, A, Bm, ident, pools):
    trr = pools[3]
    for g4 in range(4):
        pt = trr.next()
        for j in range(4):
            kc = g4 * 4 + j
            k.tr(pt.t[:, j, :], xn.t[:, kc * 128:(kc + 1) * 128], ident.t[:], r=[xn.b, ident.b], w=[pt.b])
        for j in range(4):
            kc = g4 * 4 + j
            dst, dbuf = hT_dst(kc)
            if (kc % 2) == 0:
                k.act(dst, pt.t[:, j, :], AF.Identity, r=[pt.b, A.b, Bm.b], w=[dbuf],
                      bias=Bm.t[:, kc:kc + 1], scale=A.t[:, kc:kc + 1])
            else:
                k.ts("dve", dst, pt.t[:, j, :], A.t[:, kc:kc + 1], Bm.t[:, kc:kc + 1], ALU.mult, ALU.add,
                     r=[pt.b, A.b, Bm.b], w=[dbuf])


def norm_mod_T(k, ph, xt, hT_dst, A, Bm, ident, pools):
    xn = norm_part(k, xt, pools)
    trans_part(k, xn, hT_dst, A, Bm, ident, pools)


def EPS_T(k):
    return k.eps.t[:, 0:1]


def setup_consts(k):
    nc = k.nc
    P = k.P

    def sbt(name, shape, dtype):
        return Tile(nc.alloc_sbuf_tensor(name, shape, dtype), Buf(name))

    k.ident = sbt("ident", [128, 128], BF16)
    k.eps = sbt("epsc", [128, 1], F32)
    k.memset("pool", k.ident.t[:], 0.0, w=[k.ident.b])
    P.op("pool", lambda e: e.affine_select(out=k.ident.t[:], in_=k.ident.t[:], pattern=[[-1, 128]],
                                           compare_op=ALU.not_equal, fill=1.0, base=0, channel_multiplier=1),
         r=[k.ident.b], w=[k.ident.b])
    k.memset("pool", k.eps.t[:], EPS, w=[k.eps.b])


def phase_inproj0(k, bl):
    d = k.d
    with k.phase("ip0") as ph:
        hT = ph.sb("hT", [128, 16, NTOK], BF16)
        hTb = [Buf("hT%d" % i) for i in range(NT)]
        build_hT(k, ph, 0, bl, hT, hTb, ntr=4)
        wrot = Rot(ph.sb("w", [128, 16, 512], BF16, n=2))
        prot = Rot(ph.ps("ps", [128, 512], F32, n=3))
        orot = Rot(ph.sb("o", [128, 512], BF16, n=3))
        grot = Rot(ph.sb("g", [128, 32], F32, n=2))
        gtmp = Rot(ph.sb("gtmp", [128, 16], F32, n=2))
        bg = ph.sb("bg", [128, 32], F32)
        k.load("sp", bg, bg.t[:], d["ml_b_gates"][0].partition_broadcast(128))
        blocks = []
        for i in range(2):
            blocks.append(("q", i * 512, 512, "qk0", i * 512))
        for i in range(2):
            blocks.append(("k", 1024 + i * 512, 512, "qk0", 1024 + i * 512))
        for i in range(4):
            blocks.append(("v", 2048 + i * 512, 512, "v0", i * 512))
        for i in range(4):
            blocks.append(("og", 4096 + i * 512, 512, "og0", i * 512))
        blocks.append(("g", 6144, 32, "gt0", 0))
        for (kind, c0, ncols, dst, dc0) in blocks:
            wt = wrot.next()
            k.load("pool", wt, wt.t[:, :, 0:ncols], d["ml_w_in"][0][:, c0:c0 + ncols].rearrange("(kc p) n -> p kc n", p=128))
            for tt in range(NT):
                ps = prot.next()
                for kc in range(16):
                    k.mm(ps.t[:, 0:ncols], hT.t[:, kc, tt * 128:(tt + 1) * 128], wt.t[:, kc, 0:ncols], kc == 0, kc == 15,
                         r=[hTb[tt], wt.b], w=[ps.b])
                rows = slice(tt * 128, (tt + 1) * 128)
                if kind == "g":
                    gt = grot.next(); tmp = gtmp.next()
                    k.tt("dve", gt.t[:], ps.t[:, 0:32], bg.t[:], ALU.add, r=[ps.b, bg.b], w=[gt.b])
                    gv = gt.t[:].rearrange("p (a b c) -> p a b c", a=2, b=2, c=8)[:, :, 1, :]
                    tv = tmp.t[:].rearrange("p (a c) -> p a c", a=2, c=8)
                    k.act(tv, gv, AF.Exp, r=[gt.b], w=[tmp.b], scale=-1.0)
                    k.act(tv, tv, AF.Ln, r=[tmp.b], w=[tmp.b], bias=1.0)
                    k.ts("dve", gv, tv, -1.0, None, ALU.mult, None, r=[tmp.b, gt.b], w=[gt.b])
                    k.store("sp", gt, d["gt0"][bl][rows, :], gt.t[:])
                else:
                    ot = orot.next()
                    eng = k.evac_eng()
                    if kind == "q":
                        if eng == "act":
                            k.act(ot.t[:], ps.t[:], AF.Copy, r=[ps.b], w=[ot.b], scale=128.0 ** -0.5)
                        else:
                            k.ts("dve", ot.t[:], ps.t[:], 128.0 ** -0.5, None, ALU.mult, None, r=[ps.b], w=[ot.b])
                    else:
                        k.copy(eng, ot.t[:], ps.t[:], r=[ps.b], w=[ot.b])
                    k.store("sp", ot, d[dst][bl][rows, dc0:dc0 + 512], ot.t[:])


def bcast_last(ap2, n):
    a = ap2.ap
    return bass.AP(ap2.tensor, ap2.offset, [list(a[0]), list(a[1]), [0, n]])


def bcast_mid(ap2, n):
    a = ap2.ap
    return bass.AP(ap2.tensor, ap2.offset, [list(a[0]), [0, n], list(a[1])])


def phase_scan0(k, bl, thunks=()):
    d = k.d
    P = k.P
    with k.phase("scan") as ph:
        ones = ph.sb("ones", [128, 128], F32)
        triF = ph.sb("triF", [128, 128], F32)
        triB = ph.sb("triB", [128, 128], F32)
        mskF = ph.sb("mskF", [128, 128], BF16)
        mskB = ph.sb("mskB", [128, 128], BF16)
        k.memset("pool", ones.t[:], 1.0, w=[ones.b])
        for tri, msk, sgn in ((triF, mskF, -1), (triB, mskB, 1)):
            k.memset("pool", tri.t[:], 1.0, w=[tri.b])
            P.op("pool", lambda e, tri=tri, sgn=sgn: e.affine_select(out=tri.t[:], in_=tri.t[:], pattern=[[-sgn, 128]],
                 compare_op=ALU.is_ge, fill=0.0, base=0, channel_multiplier=sgn), r=[tri.b], w=[tri.b])
            k.copy("pool", msk.t[:], tri.t[:], r=[tri.b], w=[msk.b])
        HG = ph.sb("HG", [128, D], F32)
        k.load("sp", HG, HG.t[:], d["ml_head_g"][0].partition_broadcast(128))
        U = ph.sb("U", [128, 8, 257], F32)
        Cb = ph.sb("Cb", [128, 8, 257], BF16)
        Ub = [Buf("U%d" % h) for h in range(8)]
        Cbb = [Buf("Cb%d" % h) for h in range(8)]
        qkr = Rot(ph.sb("qk", [128, D], BF16, n=4))
        var = ph.sb("va", [128, 8, 257], BF16, n=4)
        for v in var:
            k.memset("pool", v.t[:, :, 256:257], 1.0, w=[v.b])
        var = Rot(var)
        gtr = Rot(ph.sb("gt", [128, 32], F32, n=4))
        smr = Rot(ph.sb("sm", [128, 40], F32, n=6))
        Qer = Rot(ph.sb("Qe", [128, 8, 128], BF16, n=2))
        Kfr = Rot(ph.sb("Kf", [128, 8, 128], BF16, n=2))
        QeTr = Rot(ph.sb("QeT", [128, 8, 128], BF16, n=2))
        KfTr = Rot(ph.sb("KfT", [128, 8, 128], BF16, n=2))
        ATr = Rot(ph.sb("AT", [128, 8, 128], BF16, n=2))
        hfr = Rot(ph.sb("hf", [128, D], F32, n=4))
        hsr = Rot(ph.sb("hs", [128, D], F32, n=2))
        ogr = Rot(ph.sb("og", [128, D], BF16, n=2))
        sgr = Rot(ph.sb("sg", [128, D], BF16, n=2))
        w1r = Rot(ph.sb("w1", [128, D], BF16, n=2))
        yir = Rot(ph.sb("yi", [128, D], BF16, n=2))
        rdr = Rot(ph.sb("rd", [128, 2], F32, n=4))
        msr = Rot(ph.sb("ms", [128, 24], F32, n=2))
        jkr = Rot(ph.sb("jk", [128, 256], BF16, n=2))
        cs_ps = Rot(ph.ps("cs", [128, 16], F32, n=1))
        tr_ps = Rot(ph.ps("tr", [128, 4, 128], BF16, n=2))
        st_ps = Rot(ph.ps("st", [128, 4, 128], F32, n=2))
        nm_ps = Rot(ph.ps("nm", [128, 257], F32, n=2))
        pp_ps = Rot(ph.ps("pp", [128, 257], F32, n=1))
        orders = (list(range(NT)), [1, 0] + list(range(NT - 1, 1, -1)))
        Us = (U, ph.sb("U2", [128, 8, 257], F32))
        Cbs = (Cb, ph.sb("Cb2", [128, 8, 257], BF16))
        Ubs = (Ub, [Buf("U2_%d" % h) for h in range(8)])
        Cbbs = (Cbb, [Buf("Cb2_%d" % h) for h in range(8)])
        smps = [None, None]
        for dirn in range(2):
            k.memset("pool", Us[dirn].t[:], 0.0, w=Ubs[dirn])
            k.memset("pool", Cbs[dirn].t[:], 0.0, w=Cbbs[dirn])

        def step_loads(dirn, tt):
            rows = slice(tt * 128, (tt + 1) * 128)
            qk = qkr.next(); va = var.next(); gt = gtr.next()
            k.load("sp", qk, qk.t[:], d["qk0"][bl][rows, :])
            k.load("sp", va, va.t[:, :, 0:256], d["v0"][bl][rows, :].rearrange("p (h e) -> p h e", h=8))
            k.load("sp", gt, gt.t[:], d["gt0"][bl][rows, :])
            return (qk, va, gt)

        def step(dirn, tt, tiles3):
            tri = triF if dirn == 0 else triB
            msk = mskF if dirn == 0 else mskB
            U_, Cb_, Ub_, Cbb_ = Us[dirn], Cbs[dirn], Ubs[dirn], Cbbs[dirn]
            smp = smps[dirn]
            rows = slice(tt * 128, (tt + 1) * 128)
            qk, va, gt = tiles3
            sm = smr.next()
            ig = gt.t[:, dirn * 16:dirn * 16 + 8]
            lf = gt.t[:, dirn * 16 + 8:dirn * 16 + 16]
            cs = cs_ps.next()
            k.mm(cs.t[:, 0:8], tri.t[:], lf, True, True, r=[tri.b, gt.b], w=[cs.b])
            k.mm(cs.t[:, 8:16], ones.t[:], lf, True, True, r=[ones.b, gt.b], w=[cs.b])
            E = sm.t[:, 0:8]; Fv = sm.t[:, 8:16]; Tc = sm.t[:, 16:24]; tmp = sm.t[:, 24:32]
            k.act(E, cs.t[:, 0:8], AF.Exp, r=[cs.b], w=[sm.b])
            k.act(Tc, cs.t[:, 8:16], AF.Exp, r=[cs.b], w=[sm.b])
            k.tt("dve", tmp, ig, cs.t[:, 0:8], ALU.subtract, r=[gt.b, cs.b], w=[sm.b])
            k.act(Fv, tmp, AF.Exp, r=[sm.b], w=[sm.b])
            yield
            Qe = Qer.next(); Kf = Kfr.next(); QeT = QeTr.next(); KfT = KfTr.next(); AT = ATr.next()
            k.tt("dve", Qe.t[:], qk.t[:, 0:1024].rearrange("p (h e) -> p h e", h=8), bcast_last(E, 128), ALU.mult,
                 r=[qk.b, sm.b], w=[Qe.b])
            k.tt("pool", Kf.t[:], qk.t[:, 1024:2048].rearrange("p (h e) -> p h e", h=8), bcast_last(Fv, 128), ALU.mult,
                 r=[qk.b, sm.b], w=[Kf.b])
            yield
            for src, dstT in ((Qe, QeT), (Kf, KfT)):
                for g in range(2):
                    pt = tr_ps.next()
                    for j in range(4):
                        k.tr(pt.t[:, j, :], src.t[:, g * 4 + j, :], k.ident.t[:], r=[src.b, k.ident.b], w=[pt.b])
                    k.copy(k.evac_eng(), dstT.t[:, g * 4:(g + 1) * 4, :], pt.t[:], r=[pt.b], w=[dstT.b])
            yield
            for g in range(2):
                st = st_ps.next()
                for j in range(4):
                    h = g * 4 + j
                    k.mm(st.t[:, j, :], KfT.t[:, h, :], QeT.t[:, h, :], True, True, r=[KfT.b, QeT.b], w=[st.b])
                k.tt("dve", AT.t[:, g * 4:(g + 1) * 4, :], st.t[:], bcast_mid(msk.t[:], 4), ALU.mult,
                     r=[st.b, msk.b], w=[AT.b])
            yield
            hf = hfr.next()
            for h in range(8):
                nm = nm_ps.next(); rd = rdr.next(); pp = pp_ps.next()
                k.mm(nm.t[:], AT.t[:, h, :], va.t[:, h, :], True, False, r=[AT.b, va.b], w=[nm.b])
                k.mm(nm.t[:], QeT.t[:, h, :], Cb_.t[:, h, :], False, True, r=[QeT.b, Cbb_[h]], w=[nm.b])
                k.act(rd.t[:, 0:1], nm.t[:, 256:257], AF.Abs, r=[nm.b], w=[rd.b])
                k.ts("dve", rd.t[:, 0:1], rd.t[:, 0:1], 1.0, None, ALU.max, None, r=[rd.b], w=[rd.b])
                k.recip(rd.t[:, 1:2], rd.t[:, 0:1], r=[rd.b], w=[rd.b])
                hsl = slice(h * 256, (h + 1) * 256)
                if h % 2 == 0:
                    k.act(hf.t[:, hsl], nm.t[:, 0:256], AF.Copy, r=[nm.b, rd.b], w=[hf.b], scale=rd.t[:, 1:2])
                else:
                    k.ts("dve", hf.t[:, hsl], nm.t[:, 0:256], rd.t[:, 1:2], None, ALU.mult, None, r=[nm.b, rd.b], w=[hf.b])
                yield
                k.mm(pp.t[:], Kf.t[:, h, :], va.t[:, h, :], True, True, r=[Kf.b, va.b], w=[pp.b])
                if smp is None:
                    k.copy("dve", U_.t[:, h, :], pp.t[:], r=[pp.b], w=[Ub_[h]])
                else:
                    k.stt("dve", U_.t[:, h, :], U_.t[:, h, :], smp.t[:, 16 + h:17 + h], pp.t[:], ALU.mult, ALU.add,
                          r=[Ub_[h], smp.b, pp.b], w=[Ub_[h]])
                k.act(Cb_.t[:, h, :], U_.t[:, h, :], AF.Copy, r=[Ub_[h], sm.b], w=[Cbb_[h]], scale=sm.t[:, 16 + h:17 + h])
            yield
            smps[dirn] = sm
            dst = "hf0" if dirn == 0 else "hb0"
            k.store("sp", hf, d[dst][bl][rows, :], hf.t[:], w=[k.dbuf((dst, bl, tt))])

        thunks = list(thunks)
        done_at = {}
        for i in range(NT):
            done_at.setdefault(orders[0][i], []).append(i)
            done_at.setdefault(orders[1][i], []).append(i)
        ready = {}
        for tt, v in done_at.items():
            ready.setdefault(max(v), []).append(tt)

        def combine(tt):
            rows = slice(tt * 128, (tt + 1) * 128)
            hf = hfr.next(); hb = hfr.next(); hs = hsr.next()
            og = ogr.next(); sg = sgr.next(); w1 = w1r.next(); yi = yir.next(); ms = msr.next(); jk = jkr.next()
            k.load("sp", hf, hf.t[:], d["hf0"][bl][rows, :], r=[k.dbuf(("hf0", bl, tt))])
            k.load("sp", hb, hb.t[:], d["hb0"][bl][rows, :], r=[k.dbuf(("hb0", bl, tt))])
            k.load("sp", og, og.t[:], d["og0"][bl][rows, :])
            k.tt("dve", hs.t[:], hf.t[:], hb.t[:], ALU.add, r=[hf.b, hb.b], w=[hs.b])
            for h in range(8):
                k.act(jk.t[:], hs.t[:, h * 256:(h + 1) * 256], AF.Square, r=[hs.b], w=[jk.b, ms.b], accum=ms.t[:, h:h + 1])
            k.act(ms.t[:, 8:16], ms.t[:, 0:8], AF.Sqrt, r=[ms.b], w=[ms.b], bias=EPS_T(k), scale=1.0 / 256)
            k.recip(ms.t[:, 16:24], ms.t[:, 8:16], r=[ms.b], w=[ms.b])
            k.act(sg.t[:], og.t[:], AF.Sigmoid, r=[og.b], w=[sg.b])
            k.tt("pool", w1.t[:], sg.t[:], HG.t[:], ALU.mult, r=[sg.b, HG.b], w=[w1.b])
            for h in range(8):
                hsl = slice(h * 256, (h + 1) * 256)
                k.stt("dve", yi.t[:, hsl], hs.t[:, hsl], ms.t[:, 16 + h:17 + h], w1.t[:, hsl], ALU.mult, ALU.mult,
                      r=[hs.b, ms.b, w1.b], w=[yi.b])
            k.store("sp", yi, d["yin"][bl][rows, :], yi.t[:])

        pre = [step_loads(0, orders[0][0]), step_loads(1, orders[1][0])]
        for i in range(NT):
            gens = []
            for dirn in range(2):
                cur3 = pre[dirn]
                if i + 1 < NT:
                    pre[dirn] = step_loads(dirn, orders[dirn][i + 1])
                gens.append(step(dirn, orders[dirn][i], cur3))
            while gens:
                for g_ in list(gens):
                    try:
                        next(g_)
                    except StopIteration:
                        gens.remove(g_)
            for _ in range(2):
                if thunks:
                    thunks.pop(0)()
            for tt in ready.get(i - 1, []):
                combine(tt)
        for tt in ready.get(NT - 1, []):
            combine(tt)
        while thunks:
            thunks.pop(0)()


def bcast_row_tile(k, ph, name, vec_ap):
    t = ph.sb(name, [128, D], F32)
    k.load("sp", t, t.t[:], vec_ap.partition_broadcast(128))
    return t


def phase_post(k, l, blocks, w_out_ap):
    for blk in blocks:
        phase_post_block(k, l, [blk], w_out_ap)


def phase_post_block(k, l, blocks, w_out_ap):
    d = k.d
    P = k.P
    with k.phase("post%d" % l) as ph:
        NTB = 4
        uT = ph.sb("uT", [128, 64, NTB * 128], BF16)
        hT = ph.sb("hT", [128, 16, NTB * 128], BF16)
        fg = ph.sb("fg", [128, NTB, D], F32)
        fgb = [Buf("fg%d" % i) for i in range(NTB)]
        hTb = [Buf("hTb%d" % i) for i in range(NTB)]
        xrot = Rot(ph.sb("x", [128, D], F32, n=2))
        yrot = Rot(ph.sb("y", [128, D], BF16, n=2))
        pools = (Rot(ph.sb("junk", [128, D], BF16, n=1)), Rot(ph.sb("ss", [128, 4], F32, n=2)),
                 Rot(ph.sb("xn", [128, D], BF16, n=1)), Rot(ph.ps("tr", [128, 4, 128], BF16, n=2)))
        wrot = Rot(ph.sb("w", [128, 16, 256], BF16, n=2))
        w2rot = Rot(ph.sb("w2", [128, 4, 512], BF16, n=2))
        G1 = ph.sb("G1", [128, D], F32)
        G3 = ph.sb("G3", [128, D], F32)
        gtmp = ph.sb("gtmp", [128, D], F32)
        rlr = Rot(ph.sb("rl", [128, NTB * 128], BF16, n=2))
        sqr = Rot(ph.sb("sq", [128, 16], F32, n=NTB * 2))
        jk2 = Rot(ph.sb("jk2", [128, 512], BF16, n=1))
        prot = Rot(ph.ps("ps", [128, 512], F32, n=6))
        ubuf = [Buf("u%d" % i) for i in range(64)]
        cols = {}
        cur_cond = None
        for blk in blocks:
            nt = len(blk)
            N = nt * 128
            cond = blk[0][0]
            if cond != cur_cond:
                cur_cond = cond
                for (G, gi, mi) in ((G1, 1, 2), (G3, 3, 5)):
                    k.load("sp", G, G.t[:], d["mod"][l][cond][mi * D:(mi + 1) * D].partition_broadcast(128))
                    k.load("sp", gtmp, gtmp.t[:], d["norm_g"][l][gi].partition_broadcast(128))
                    k.tt("pool", G.t[:], G.t[:], gtmp.t[:], ALU.mult, r=[G.b, gtmp.b], w=[G.b])
                if cond not in cols:
                    cols[cond] = mod_cols(k, ph, l, cond, 2, 3, 4, "m%d" % cond)
                A2, B2 = cols[cond]
            for t, (_, yin_ap, xin_ap, xmid_ap, xout_ap, key) in enumerate(blk):
                yt = yrot.next()
                k.load("sp", yt, yt.t[:], yin_ap)
                for g4 in range(4):
                    pt = pools[3].next()
                    for j in range(4):
                        kc = g4 * 4 + j
                        k.tr(pt.t[:, j, :], yt.t[:, kc * 128:(kc + 1) * 128], k.ident.t[:], r=[yt.b, k.ident.b], w=[pt.b])
                    k.copy(k.evac_eng(), hT.t[:, g4 * 4:(g4 + 1) * 4, t * 128:(t + 1) * 128], pt.t[:], r=[pt.b], w=[hTb[t]])
            sqs = [sqr.next() for _ in range(nt)]
            for cb in range(8):
                wt = wrot.next()
                cs = slice(cb * 256, (cb + 1) * 256)
                k.load("pool", wt, wt.t[:], w_out_ap[:, cs].rearrange("(kc p) n -> p kc n", p=128))
                for t in range(nt):
                    ps = prot.next()
                    for kc in range(16):
                        k.mm(ps.t[:, 0:256], hT.t[:, kc, t * 128:(t + 1) * 128], wt.t[:, kc, :], kc == 0, kc == 15,
                             r=[hTb[t], wt.b], w=[ps.b])
                    k.tt("dve", fg.t[:, t, cs], ps.t[:, 0:256], G1.t[:, cs], ALU.mult, r=[ps.b, G1.b], w=[fgb[t]])
                    jk = jk2.next()
                    k.act(jk.t[:, 0:256], ps.t[:, 0:256], AF.Square, r=[ps.b], w=[jk.b, sqs[t].b], accum=sqs[t].t[:, cb:cb + 1])
            for t, (_, yin_ap, xin_ap, xmid_ap, xout_ap, key) in enumerate(blk):
                sq = sqs[t]
                xt = xrot.next()
                k.load("sp", xt, xt.t[:], xin_ap)
                P.op("dve", lambda e, sq=sq: e.tensor_reduce(out=sq.t[:, 8:9], in_=sq.t[:, 0:8], axis=AX.X, op=ALU.add),
                     r=[sq.b], w=[sq.b])
                k.act(sq.t[:, 9:10], sq.t[:, 8:9], AF.Sqrt, r=[sq.b], w=[sq.b], bias=EPS_T(k), scale=1.0 / D)
                k.recip(sq.t[:, 10:11], sq.t[:, 9:10], r=[sq.b], w=[sq.b])
                k.stt("dve", xt.t[:], fg.t[:, t, :], sq.t[:, 10:11], xt.t[:], ALU.mult, ALU.add, r=[fgb[t], sq.b, xt.b], w=[xt.b])
                k.store("sp", xt, xmid_ap, xt.t[:], w=[k.dbuf(("xmid", key))])
                norm_mod_T(k, ph, xt, lambda kc, t=t: (hT.t[:, kc, t * 128:(t + 1) * 128], hTb[t]), A2, B2, k.ident, pools)
            if POST_STOP == 1:
                continue
            for mb in range(32):
                wt = wrot.next()
                k.load("sp", wt, wt.t[:], d["w1b"][l][:, mb * 256:(mb + 1) * 256].rearrange("(kc p) n -> p kc n", p=128),
                       r=[k.dbuf(("w1b", l, i)) for i in range(16)])
                for mc in range(2):
                    ps = prot.next()
                    for kc in range(16):
                        k.mm(ps.t[:, 0:N], wt.t[:, kc, mc * 128:(mc + 1) * 128], hT.t[:, kc, 0:N], kc == 0, kc == 15,
                             r=[wt.b] + hTb[:nt], w=[ps.b])
                    rl = rlr.next()
                    k.act(rl.t[:, 0:N], ps.t[:, 0:N], AF.Relu, r=[ps.b], w=[rl.b])
                    m = mb * 2 + mc
                    k.tt("pool", uT.t[:, m, 0:N], rl.t[:, 0:N], rl.t[:, 0:N], ALU.mult, r=[rl.b], w=[ubuf[m]])
            if POST_STOP == 2:
                continue
            sqs = [sqr.next() for _ in range(nt)]
            for cb in range(4):
                cs = slice(cb * 512, (cb + 1) * 512)
                pss = [prot.next() for _ in range(nt)]
                for g in range(16):
                    w2 = w2rot.next()
                    k.load("sp", w2, w2.t[:], d["w2b"][l][g * 512:(g + 1) * 512, cs].rearrange("(kc p) n -> p kc n", p=128),
                           r=[k.dbuf(("w2b", l, g))])
                    for t in range(nt):
                        for j in range(4):
                            m = g * 4 + j
                            k.mm(pss[t].t[:], uT.t[:, m, t * 128:(t + 1) * 128], w2.t[:, j, :], m == 0, m == 63,
                                 r=[ubuf[m], w2.b], w=[pss[t].b])
                for t in range(nt):
                    k.tt("dve", fg.t[:, t, cs], pss[t].t[:], G3.t[:, cs], ALU.mult, r=[pss[t].b, G3.b], w=[fgb[t]])
                    jk = jk2.next()
                    k.act(jk.t[:], pss[t].t[:], AF.Square, r=[pss[t].b], w=[jk.b, sqs[t].b], accum=sqs[t].t[:, cb:cb + 1])
            for t, (_, yin_ap, xin_ap, xmid_ap, xout_ap, key) in enumerate(blk):
                sq = sqs[t]
                xt = xrot.next()
                k.load("sp", xt, xt.t[:], xmid_ap, r=[k.dbuf(("xmid", key))])
                P.op("dve", lambda e, sq=sq: e.tensor_reduce(out=sq.t[:, 8:9], in_=sq.t[:, 0:4], axis=AX.X, op=ALU.add),
                     r=[sq.b], w=[sq.b])
                k.act(sq.t[:, 9:10], sq.t[:, 8:9], AF.Sqrt, r=[sq.b], w=[sq.b], bias=EPS_T(k), scale=1.0 / D)
                k.recip(sq.t[:, 10:11], sq.t[:, 9:10], r=[sq.b], w=[sq.b])
                k.stt("dve", xt.t[:], fg.t[:, t, :], sq.t[:, 10:11], xt.t[:], ALU.mult, ALU.add, r=[fgb[t], sq.b, xt.b], w=[xt.b])
                k.store("sp", xt, xout_ap, xt.t[:])


def phase_P(k, l, tiles, w_out_ap):
    d = k.d
    P = k.P
    with k.phase("P%d" % l) as ph:
        Wo = ph.sb("Wo", [128, 16, D], BF16)
        Wob = [Buf("Wo%d" % i) for i in range(4)]
        for cb in range(4):
            cs = slice(cb * 512, (cb + 1) * 512)
            k.dma("pool", Wo.t[:, :, cs], w_out_ap[:, cs].rearrange("(kc p) n -> p kc n", p=128), Wo.b, w=[Wob[cb]])
        G1 = ph.sb("G1", [128, D], F32)
        gtmp = ph.sb("gtmp", [128, D], F32)
        yrot = Rot(ph.sb("y", [128, D], BF16, n=3))
        yTr = Rot(ph.sb("yT", [128, 16, 128], BF16, n=2))
        xrot = Rot(ph.sb("x", [128, D], F32, n=4))
        fgr = Rot(ph.sb("fg", [128, D], F32, n=2))
        hTr = Rot(ph.sb("hTt", [128, 16, 128], BF16, n=2))
        pools = (Rot(ph.sb("junk", [128, D], BF16, n=1)), Rot(ph.sb("ss", [128, 4], F32, n=3)),
                 Rot(ph.sb("xn", [128, D], BF16, n=3)), Rot(ph.ps("tr", [128, 4, 128], BF16, n=3)))
        sqr = Rot(ph.sb("sq", [128, 16], F32, n=4))
        jk2 = Rot(ph.sb("jk2", [128, 512], BF16, n=2))
        prot = Rot(ph.ps("ps", [128, 512], F32, n=5))
        cols = {}
        cur_cond = None
        prev = None
        tl = list(tiles)

        def finish_prev(pv):
            xn_p, A_p, B_p, key_p = pv
            hTt = hTr.next()
            trans_part(k, xn_p, lambda kc, hTt=hTt: (hTt.t[:, kc, :], hTt.b), A_p, B_p, k.ident, pools)
            k.store("sp", hTt, d["hTd"][key_p[0]][key_p[1]], hTt.t[:].rearrange("p a b -> p (a b)"),
                    w=[k.dbuf(("hTd", key_p))])

        def tile_loads(tile):
            yt = yrot.next(); xt = xrot.next()
            k.load("sp", yt, yt.t[:], tile[1])
            k.load("sp", xt, xt.t[:], tile[2])
            return yt, xt
        nxt_ld = tile_loads(tl[0])
        for ti, tile in enumerate(tl + [None]):
            cur = None
            if tile is not None:
                (cond, yin_ap, xin_ap, xmid_ap, xout_ap, key) = tile
                yt, xt = nxt_ld
                if ti + 1 < len(tl):
                    nxt_ld = tile_loads(tl[ti + 1])
                if cond != cur_cond:
                    cur_cond = cond
                    k.load("sp", G1, G1.t[:], d["mod"][l][cond][2 * D:3 * D].partition_broadcast(128))
                    k.load("sp", gtmp, gtmp.t[:], d["norm_g"][l][1].partition_broadcast(128))
                    k.tt("pool", G1.t[:], G1.t[:], gtmp.t[:], ALU.mult, r=[G1.b, gtmp.b], w=[G1.b])
                    if cond not in cols:
                        cols[cond] = mod_cols(k, ph, l, cond, 2, 3, 4, "m%d" % cond)
                A2, B2 = cols[cond]
                yT = yTr.next(); fg = fgr.next(); sq = sqr.next()
                for g4 in range(4):
                    pt = pools[3].next()
                    for j in range(4):
                        kc = g4 * 4 + j
                        k.tr(pt.t[:, j, :], yt.t[:, kc * 128:(kc + 1) * 128], k.ident.t[:], r=[yt.b, k.ident.b], w=[pt.b])
                    k.copy(k.evac_eng(), yT.t[:, g4 * 4:(g4 + 1) * 4, :], pt.t[:], r=[pt.b], w=[yT.b])
                pss4 = []
                for cb in range(4):
                    cs = slice(cb * 512, (cb + 1) * 512)
                    ps = prot.next()
                    pss4.append(ps)
                    for kc in range(16):
                        k.mm(ps.t[:], yT.t[:, kc, :], Wo.t[:, kc, cs], kc == 0, kc == 15, r=[yT.b, Wob[cb]], w=[ps.b])
                if prev is not None:
                    finish_prev(prev)
                    prev = None
                for cb in range(4):
                    cs = slice(cb * 512, (cb + 1) * 512)
                    ps = pss4[cb]
                    k.tt("dve", fg.t[:, cs], ps.t[:], G1.t[:, cs], ALU.mult, r=[ps.b, G1.b], w=[fg.b])
                    jk = jk2.next()
                    k.act(jk.t[:], ps.t[:], AF.Square, r=[ps.b], w=[jk.b, sq.b], accum=sq.t[:, cb:cb + 1])
                P.op("dve", lambda e, sq=sq: e.tensor_reduce(out=sq.t[:, 8:9], in_=sq.t[:, 0:4], axis=AX.X, op=ALU.add),
                     r=[sq.b], w=[sq.b])
                k.act(sq.t[:, 9:10], sq.t[:, 8:9], AF.Sqrt, r=[sq.b], w=[sq.b], bias=EPS_T(k), scale=1.0 / D)
                k.recip(sq.t[:, 10:11], sq.t[:, 9:10], r=[sq.b], w=[sq.b])
                k.stt("dve", xt.t[:], fg.t[:], sq.t[:, 10:11], xt.t[:], ALU.mult, ALU.add, r=[fg.b, sq.b, xt.b], w=[xt.b])
                k.store("sp", xt, xmid_ap, xt.t[:], w=[k.dbuf(("xmid", key))])
                xn = norm_part(k, xt, pools)
                cur = (xn, A2, B2, key)
            if prev is not None:
                finish_prev(prev)
            prev = cur


def phase_M(k, l, blocks):
    d = k.d
    P = k.P
    with k.phase("M%d" % l) as ph:
        NTB = 4
        uT = ph.sb("uT", [128, 64, NTB * 128], BF16)
        hT = ph.sb("hT", [128, NTB, 16, 128], BF16)
        fg = ph.sb("fg", [128, NTB, D], F32)
        fgb = [Buf("fg%d" % i) for i in range(NTB)]
        hTb = [Buf("hTb%d" % i) for i in range(NTB)]
        xrot = Rot(ph.sb("x", [128, D], F32, n=2))
        wrot = Rot(ph.sb("w", [128, 16, 256], BF16, n=3))
        w2rot = Rot(ph.sb("w2", [128, 4, 512], BF16, n=4))
        G3 = ph.sb("G3", [128, D], F32)
        gtmp = ph.sb("gtmp", [128, D], F32)
        rlr = Rot(ph.sb("rl", [128, NTB * 128], BF16, n=2))
        sqr = Rot(ph.sb("sq", [128, 16], F32, n=NTB * 2))
        jk2 = Rot(ph.sb("jk2", [128, 512], BF16, n=1))
        prot = Rot(ph.ps("ps", [128, 512], F32, n=8))
        ubuf = [Buf("u%d" % i) for i in range(64)]
        cur_cond = None

        def w1_load(mb):
            wt = wrot.next()
            k.load("sp", wt, wt.t[:], d["w1b"][l][:, mb * 256:(mb + 1) * 256].rearrange("(kc p) n -> p kc n", p=128),
                   r=[k.dbuf(("w1b", l, i)) for i in range(16)])
            return wt

        def block_prefetch(blk):
            for t, (_, yin_ap, xin_ap, xmid_ap, xout_ap, key) in enumerate(blk):
                k.dma("sp", hT.t[:, t, :, :].rearrange("p a b -> p (a b)"), d["hTd"][key[0]][key[1]], hT.b,
                      r=[k.dbuf(("hTd", key))], w=[hTb[t]])
            return [w1_load(mb) for mb in range(3)]
        pre_w = None
        for bi, blk in enumerate(blocks):
            nt = len(blk)
            N = nt * 128
            cond = blk[0][0]
            if cond != cur_cond:
                cur_cond = cond
                k.load("sp", G3, G3.t[:], d["mod"][l][cond][5 * D:6 * D].partition_broadcast(128))
                k.load("sp", gtmp, gtmp.t[:], d["norm_g"][l][3].partition_broadcast(128))
                k.tt("pool", G3.t[:], G3.t[:], gtmp.t[:], ALU.mult, r=[G3.b, gtmp.b], w=[G3.b])
            if pre_w is None:
                pre_w = block_prefetch(blk)
            my_w = pre_w
            pre_w = None
            for mb in range(32):
                if mb < len(my_w):
                    wt = my_w[mb]
                else:
                    wt = w1_load(mb)
                for mc in range(2):
                    ps = prot.next()
                    for kc in range(16):
                        k.mm(ps.t[:, 0:N].rearrange("p (a b) -> p a b", a=nt), wt.t[:, kc, mc * 128:(mc + 1) * 128],
                             hT.t[:, 0:nt, kc, :], kc == 0, kc == 15, r=[wt.b] + hTb[:nt], w=[ps.b])
                    rl = rlr.next()
                    k.act(rl.t[:, 0:N], ps.t[:, 0:N], AF.Relu, r=[ps.b], w=[rl.b])
                    m = mb * 2 + mc
                    k.tt("pool", uT.t[:, m, 0:N], rl.t[:, 0:N], rl.t[:, 0:N], ALU.mult, r=[rl.b], w=[ubuf[m]])
            sqs = [sqr.next() for _ in range(nt)]
            for cb in range(4):
                cs = slice(cb * 512, (cb + 1) * 512)
                pss = [prot.next() for _ in range(nt)]
                for g in range(16):
                    w2 = w2rot.next()
                    k.load("sp", w2, w2.t[:], d["w2b"][l][g * 512:(g + 1) * 512, cs].rearrange("(kc p) n -> p kc n", p=128),
                           r=[k.dbuf(("w2b", l, g))])
                    if cb == 3 and g == 12 and bi + 1 < len(blocks):
                        pre_w = block_prefetch(blocks[bi + 1])
                    for t in range(nt):
                        for j in range(4):
                            m = g * 4 + j
                            k.mm(pss[t].t[:], uT.t[:, m, t * 128:(t + 1) * 128], w2.t[:, j, :], m == 0, m == 63,
                                 r=[ubuf[m], w2.b], w=[pss[t].b])
                for t in range(nt):
                    k.tt("dve", fg.t[:, t, cs], pss[t].t[:], G3.t[:, cs], ALU.mult, r=[pss[t].b, G3.b], w=[fgb[t]])
                    jk = jk2.next()
                    k.act(jk.t[:], pss[t].t[:], AF.Square, r=[pss[t].b], w=[jk.b, sqs[t].b], accum=sqs[t].t[:, cb:cb + 1])
            for t, (_, yin_ap, xin_ap, xmid_ap, xout_ap, key) in enumerate(blk):
                sq = sqs[t]
                xt = xrot.next()
                k.load("sp", xt, xt.t[:], xmid_ap, r=[k.dbuf(("xmid", key))])
                P.op("dve", lambda e, sq=sq: e.tensor_reduce(out=sq.t[:, 8:9], in_=sq.t[:, 0:4], axis=AX.X, op=ALU.add),
                     r=[sq.b], w=[sq.b])
                k.act(sq.t[:, 9:10], sq.t[:, 8:9], AF.Sqrt, r=[sq.b], w=[sq.b], bias=EPS_T(k), scale=1.0 / D)
                k.recip(sq.t[:, 10:11], sq.t[:, 9:10], r=[sq.b], w=[sq.b])
                k.stt("dve", xt.t[:], fg.t[:, t, :], sq.t[:, 10:11], xt.t[:], ALU.mult, ALU.add, r=[fgb[t], sq.b, xt.b], w=[xt.b])
                k.store("sp", xt, xout_ap, xt.t[:])


def phase_post2(k, l, blocks, w_out_ap):
    tiles = [t for blk in blocks for t in blk]
    phase_P(k, l, tiles, w_out_ap)
    phase_M(k, l, blocks)


def post_blocks0(k):
    d = k.d
    blocks = []
    for bl in range(NB):
        def tile(tt, bl=bl):
            rows = slice(tt * 128, (tt + 1) * 128)
            return (2 if tt < 2 else bl, d["yin"][bl][rows, :], token_src(k, 0, bl, tt), d["xmid"][bl][rows, :],
                    d["xl1"][bl][rows, :], (bl, tt))
        blocks.append([tile(0), tile(1)])
        for b4 in range(4):
            blocks.append([tile(2 + b4 * 4 + i) for i in range(4)])
    return blocks


def build_hT(k, ph, l, bl, hT, hTb, ntr=2):
    A_c, B_c = mod_cols(k, ph, l, 2, 0, 0, 1, "c")
    A_l, B_l = mod_cols(k, ph, l, bl, 0, 0, 1, "l")
    xrot = Rot(ph.sb("x", [128, D], F32, n=3))
    pools = (Rot(ph.sb("junk", [128, D], BF16, n=1)), Rot(ph.sb("ss", [128, 4], F32, n=3)),
             Rot(ph.sb("xn", [128, D], BF16, n=3)), Rot(ph.ps("tr", [128, 4, 128], BF16, n=ntr)))
    prev = None
    for tt in range(NT + 1):
        cur = None
        if tt < NT:
            xt = xrot.next()
            k.load("sp", xt, xt.t[:], token_src(k, l, bl, tt))
            cur = (tt, norm_part(k, xt, pools))
        if prev is not None:
            pt_, xn = prev
            A, Bm = (A_c, B_c) if pt_ < 2 else (A_l, B_l)
            trans_part(k, xn, lambda kc, pt_=pt_: (hT.t[:, kc, pt_ * 128:(pt_ + 1) * 128], hTb[pt_]), A, Bm, k.ident, pools)
        prev = cur


def phase_inproj1(k, bl):
    d = k.d
    P = k.P
    with k.phase("ip1") as ph:
        hT = ph.sb("hT", [128, 16, NTOK], BF16)
        hTb = [Buf("hT%d" % i) for i in range(NT)]
        build_hT(k, ph, 1, bl, hT, hTb)
        cosT = ph.sb("cosT", [128, SEQ], F32)
        sinT = ph.sb("sinT", [128, SEQ], F32)
        ang = ph.sb("ang", [128, SEQ], F32)
        tmpT = ph.sb("tmpT", [128, SEQ], F32)
        pcol = ph.sb("pcol", [128, 4], F32)
        Lm = ph.sb("Lm", [128, 128], BF16)
        Rm = ph.sb("Rm", [128, 32], F32)
        Jm = ph.sb("Jm", [128, 32], F32)
        k.memset("pool", Rm.t[:], 0.0, w=[Rm.b])
        for i in range(4):
            P.op("pool", lambda e, i=i: e.affine_select(out=Rm.t[:], in_=Rm.t[:], pattern=[[-1, 32]], compare_op=ALU.not_equal,
                                                         fill=1.0, base=-32 * i, channel_multiplier=1), r=[Rm.b], w=[Rm.b])
        P.op("pool", lambda e: e.iota(Jm.t[:], [[1, 32]], base=0, channel_multiplier=0,
                                      allow_small_or_imprecise_dtypes=True), w=[Jm.b])
        k.tt("dve", Rm.t[:], Rm.t[:], Jm.t[:], ALU.mult, r=[Rm.b, Jm.b], w=[Rm.b])
        P.op("dve", lambda e: e.tensor_reduce(out=pcol.t[:, 1:2], in_=Rm.t[:], axis=AX.X, op=ALU.add), r=[Rm.b], w=[pcol.b])
        k.act(pcol.t[:, 2:3], pcol.t[:, 1:2], AF.Exp, r=[pcol.b], w=[pcol.b], scale=-math.log(10000.0) / 32.0)
        P.op("pool", lambda e: e.iota(ang.t[0:64, :].rearrange("p (a b) -> p a b", a=32), [[1, 32], [0, 64]], base=0,
                                      channel_multiplier=0, allow_small_or_imprecise_dtypes=True), w=[ang.b])
        P.op("pool", lambda e: e.iota(ang.t[64:128, :].rearrange("p (a b) -> p a b", a=32), [[0, 32], [1, 64]], base=0,
                                      channel_multiplier=0, allow_small_or_imprecise_dtypes=True), w=[ang.b])
        k.ts("dve", ang.t[:], ang.t[:], pcol.t[:, 2:3], None, ALU.mult, None, r=[ang.b, pcol.b], w=[ang.b])
        k.act(tmpT.t[:], ang.t[:], AF.Sin, r=[ang.b], w=[tmpT.b], scale=1.0 / 64)
        k.act(sinT.t[:], ang.t[:], AF.Sin, r=[ang.b], w=[sinT.b], scale=1.0 / 32)
        k.tt("dve", tmpT.t[:], tmpT.t[:], tmpT.t[:], ALU.mult, r=[tmpT.b], w=[tmpT.b])
        k.ts("dve", cosT.t[:], tmpT.t[:], -2.0, 1.0, ALU.mult, ALU.add, r=[tmpT.b], w=[cosT.b])
        for it in range(5):
            k.tt("dve", tmpT.t[:], sinT.t[:], sinT.t[:], ALU.mult, r=[sinT.b], w=[tmpT.b])
            k.stt("dve", sinT.t[:], sinT.t[:], 2.0, cosT.t[:], ALU.mult, ALU.mult, r=[sinT.b, cosT.b], w=[sinT.b])
            k.ts("dve", cosT.t[:], tmpT.t[:], -2.0, 1.0, ALU.mult, ALU.add, r=[tmpT.b], w=[cosT.b])
        k.memset("pool", Lm.t[:], 0.0, w=[Lm.b])
        for a in range(2):
            for half, val in ((0, -1.0), (1, 1.0)):
                c0 = a * 64 + half * 32
                off = 32 if half == 0 else -32
                P.op("pool", lambda e, c0=c0, off=off, val=val: e.affine_select(
                    out=Lm.t[:, c0:c0 + 32], in_=Lm.t[:, c0:c0 + 32], pattern=[[-1, 32]], compare_op=ALU.not_equal,
                    fill=val, base=-c0 - off, channel_multiplier=1), r=[Lm.b], w=[Lm.b])
        wrot = Rot(ph.sb("w", [128, 16, 512], BF16, n=2))
        prot = Rot(ph.ps("ps", [128, 512], F32, n=4))
        rrot = Rot(ph.ps("rp", [128, 512], F32, n=2))
        qrr = Rot(ph.sb("qr", [128, 512], BF16, n=3))
        t1r = Rot(ph.sb("t1", [128, 512], F32, n=2))
        t2r = Rot(ph.sb("t2", [128, 512], F32, n=2))
        orot = Rot(ph.sb("o", [128, 512], BF16, n=4))
        pending = [None]
        for which in ("q", "k"):
            for wb in range(4):
                wt = wrot.next()
                c0 = (0 if which == "q" else D) + wb * 512
                k.load("pool", wt, wt.t[:], d["da_w_in"][0][:, c0:c0 + 512].rearrange("(kc p) n -> p kc n", p=128))
                for j in range(4):
                    mc = wb * 4 + j
                    groups = [(256 + g * 512, 512, True) for g in range(4)]
                    if which == "k":
                        groups = [(0, 256, False)] + groups
                    for (t0, n, rope) in groups:
                        ps = prot.next()
                        for kc in range(16):
                            k.mm(ps.t[:, 0:n], wt.t[:, kc, j * 128:(j + 1) * 128], hT.t[:, kc, t0:t0 + n], kc == 0, kc == 15,
                                 r=[wt.b] + hTb, w=[ps.b])
                        ot = orot.next()
                        if not rope:
                            k.copy("act", ot.t[:, 0:n], ps.t[:, 0:n], r=[ps.b], w=[ot.b])
                            k.store("sp", ot, d["kT1"][bl][mc][:, t0:t0 + n], ot.t[:, 0:n])
                            continue
                        qr = qrr.next()
                        l0 = t0 - 256
                        k.copy("act", qr.t[:], ps.t[:], r=[ps.b], w=[qr.b])
                        if pending[0] is not None:
                            pending[0]()

                        def fin(ps=ps, qr=qr, ot=ot, l0=l0, t0=t0, mc=mc, which=which):
                            t1 = t1r.next(); t2 = t2r.next(); rp = rrot.next()
                            k.mm(rp.t[:], Lm.t[:], qr.t[:], True, True, r=[Lm.b, qr.b], w=[rp.b])
                            k.tt("dve", t1.t[:], ps.t[:], cosT.t[:, l0:l0 + 512], ALU.mult, r=[ps.b, cosT.b], w=[t1.b])
                            k.tt("dve", t2.t[:], rp.t[:], sinT.t[:, l0:l0 + 512], ALU.mult, r=[rp.b, sinT.b], w=[t2.b])
                            k.tt("pool", ot.t[:], t1.t[:], t2.t[:], ALU.add, r=[t1.b, t2.b], w=[ot.b])
                            if which == "q":
                                k.store("sp", ot, d["qT1"][bl][mc][:, l0:l0 + 512], ot.t[:])
                            else:
                                k.store("sp", ot, d["kT1"][bl][mc][:, t0:t0 + 512], ot.t[:])
                        pending[0] = fin
        if pending[0] is not None:
            pending[0]()
            pending[0] = None
        for wb in range(4):
            wt = wrot.next()
            c0 = 2 * D + wb * 512
            k.load("pool", wt, wt.t[:], d["da_w_in"][0][:, c0:c0 + 512].rearrange("(kc p) n -> p kc n", p=128))
            for tt in range(NT):
                ps = prot.next()
                for kc in range(16):
                    k.mm(ps.t[:], hT.t[:, kc, tt * 128:(tt + 1) * 128], wt.t[:, kc, :], kc == 0, kc == 15,
                         r=[hTb[tt], wt.b], w=[ps.b])
                ot = orot.next()
                k.copy(k.evac_eng(), ot.t[:], ps.t[:], r=[ps.b], w=[ot.b])
                k.store("sp", ot, d["v1"][bl][tt * 128:(tt + 1) * 128, wb * 512:(wb + 1) * 512], ot.t[:])


LAM_INIT1 = 0.8 - 0.6 * math.exp(-0.3 * 1)


def phase_attn1(k, bl):
    d = k.d
    P = k.P
    with k.phase("attn") as ph:
        lam = ph.sb("lam", [128, 8], F32)
        lq = ph.sb("lq", [128, 4, 128], F32)
        lj = ph.sb("lj", [128, 128], F32)
        k.load("sp", lq, lq.t[:], d["da_lambda"][0].rearrange("a b -> (a b)").partition_broadcast(128).rearrange("p (a b) -> p a b", a=4))
        for i in range(2):
            k.tt("dve", lj.t[:], lq.t[:, 2 * i, :], lq.t[:, 2 * i + 1, :], ALU.mult, r=[lq.b], w=[lj.b])
            P.op("dve", lambda e, i=i: e.tensor_reduce(out=lam.t[:, i:i + 1], in_=lj.t[:], axis=AX.X, op=ALU.add),
                 r=[lj.b], w=[lam.b])
        k.act(lam.t[:, 2:4], lam.t[:, 0:2], AF.Exp, r=[lam.b], w=[lam.b])
        k.tt("dve", lam.t[:, 4:5], lam.t[:, 2:3], lam.t[:, 3:4], ALU.subtract, r=[lam.b], w=[lam.b])
        k.ts("dve", lam.t[:, 5:6], lam.t[:, 4:5], LAM_INIT1, None, ALU.add, None, r=[lam.b], w=[lam.b])
        gs = ph.sb("gs", [128, 256], F32)
        k.load("sp", gs, gs.t[:], d["da_sub_g"][0].partition_broadcast(128))
        k.ts("dve", gs.t[:], gs.t[:], 1.0 - LAM_INIT1, None, ALU.mult, None, r=[gs.b], w=[gs.b])
        qtr = Rot(ph.sb("QT", [128, 2, SEQ], BF16, n=2))
        ktr = Rot(ph.sb("KT", [128, 2, NTOK], BF16, n=2))
        var = ph.sb("VA", [128, NT, 257], BF16, n=2)
        for v in var:
            k.memset("pool", v.t[:, :, 256:257], 1.0, w=[v.b])
        var = Rot(var)
        ptr_ = Rot(ph.sb("PT", [128, 512], BF16, n=4))
        t0r = Rot(ph.sb("t0", [128, 4, 256], F32, n=2))
        orr = Rot(ph.sb("oo", [128, 256], F32, n=8))
        yir = Rot(ph.sb("yi", [128, 256], BF16, n=4))
        rlr = Rot(ph.sb("rl", [128, 8], F32, n=10))
        jkr = Rot(ph.sb("jk", [128, 256], F32, n=2))
        st_ps = Rot(ph.ps("st", [128, 512], F32, n=4))
        o_ps = Rot(ph.ps("o", [128, 257], F32, n=4))
        scale = 128.0 ** -0.5
        def head_loads(h):
            QT = qtr.next(); KT = ktr.next(); VA = var.next()
            for c in range(2):
                k.load("sp", QT, QT.t[:, c, :], d["qT1"][bl][2 * h + c])
                k.load("sp", KT, KT.t[:, c, :], d["kT1"][bl][2 * h + c])
            k.load("sp", VA, VA.t[:, :, 0:256], d["v1"][bl][:, h * 256:(h + 1) * 256].rearrange("(kc p) e -> p kc e", p=128))
            return QT, KT, VA
        nxt_h = head_loads(0)
        for h in range(H):
            QT, KT, VA = nxt_h
            if h + 1 < H:
                nxt_h = head_loads(h + 1)
            for qb in range(4):
                t0 = t0r.next()
                for c in range(2):
                    os_ = [o_ps.next() for _ in range(4)]
                    sts = {}

                    def issue_s(kc, c=c, qb=qb):
                        st = st_ps.next()
                        k.mm(st.t[:], KT.t[:, c, kc * 128:(kc + 1) * 128], QT.t[:, c, qb * 512:(qb + 1) * 512], True, True,
                             r=[KT.b, QT.b], w=[st.b])
                        sts[kc] = st
                    issue_s(0)
                    issue_s(1)
                    for kc in range(NT):
                        if kc + 2 < NT:
                            issue_s(kc + 2)
                        st = sts.pop(kc)
                        pt = ptr_.next()
                        k.act(pt.t[:], st.t[:], AF.Exp, r=[st.b], w=[pt.b], scale=scale)
                        for qt in range(4):
                            k.mm(os_[qt].t[:], pt.t[:, qt * 128:(qt + 1) * 128], VA.t[:, kc, :], kc == 0, kc == NT - 1,
                                 r=[pt.b, VA.b], w=[os_[qt].b])
                    if c == 0:
                        for qt in range(4):
                            o = os_[qt]
                            rl = rlr.next()
                            k.recip(rl.t[:, 0:1], o.t[:, 256:257], r=[o.b], w=[rl.b])
                            k.ts("dve", t0.t[:, qt, :], o.t[:, 0:256], rl.t[:, 0:1], None, ALU.mult, None, r=[o.b, rl.b], w=[t0.b])
                    else:
                        ss4 = rlr.next()
                        oos = []
                        for qt in range(4):
                            o = os_[qt]
                            rl = rlr.next(); oo = orr.next(); jk = jkr.next()
                            oos.append(oo)
                            k.recip(rl.t[:, 0:1], o.t[:, 256:257], r=[o.b], w=[rl.b])
                            k.ts("dve", rl.t[:, 1:2], rl.t[:, 0:1], lam.t[:, 5:6], -1.0, ALU.mult, ALU.mult, r=[rl.b, lam.b], w=[rl.b])
                            k.stt("dve", oo.t[:], o.t[:, 0:256], rl.t[:, 1:2], t0.t[:, qt, :], ALU.mult, ALU.add,
                                  r=[o.b, rl.b, t0.b], w=[oo.b])
                            k.tt("pool", jk.t[:], oo.t[:], oo.t[:], ALU.mult, r=[oo.b], w=[jk.b])
                            P.op("dve", lambda e, jk=jk, ss4=ss4, qt=qt: e.tensor_reduce(out=ss4.t[:, qt:qt + 1], in_=jk.t[:], axis=AX.X, op=ALU.add),
                                 r=[jk.b], w=[ss4.b])
                        k.act(ss4.t[:, 4:8], ss4.t[:, 0:4], AF.Sqrt, r=[ss4.b], w=[ss4.b], bias=EPS_T(k), scale=1.0 / 256)
                        k.recip(ss4.t[:, 0:4], ss4.t[:, 4:8], r=[ss4.b], w=[ss4.b])
                        for qt in range(4):
                            yi = yir.next()
                            k.stt("dve", yi.t[:], oos[qt].t[:], ss4.t[:, qt:qt + 1], gs.t[:], ALU.mult, ALU.mult,
                                  r=[oos[qt].b, ss4.b, gs.b], w=[yi.b])
                            r0 = 256 + qb * 512 + qt * 128
                            k.store("sp", yi, d["yin"][bl][r0:r0 + 128, h * 256:(h + 1) * 256], yi.t[:])


def post_blocks1(k):
    d = k.d
    blocks = []
    for bl in range(NB):
        def tile(tt, bl=bl):
            rows = slice(tt * 128, (tt + 1) * 128)
            orow = slice((tt - 2) * 128, (tt - 1) * 128)
            return (bl, d["yin"][bl][rows, :], d["xl1"][bl][rows, :], d["xmid"][bl][rows, :],
                    d["out"][bl][orow, :], (bl, tt))
        for b4 in range(4):
            blocks.append([tile(2 + b4 * 4 + i) for i in range(4)])
    return blocks


def declare_dram(k):
    nc = k.nc
    need = getattr(k, "need_inputs", None)
    inp = [("x", [NB, SEQ, D]), ("ctx", [NB, CTX, D]), ("c", [NB, D]), ("c_ctx", [D]),
           ("ada_w", [2, D, 6 * D]), ("ada_b", [2, 6 * D]), ("norm_g", [2, 4, D]),
           ("mlp_w1", [2, D, DFF]), ("mlp_w2", [2, DFF, D]), ("ml_w_in", [1, D, ML_IN]),
           ("ml_b_gates", [1, 32]), ("ml_head_g", [1, D]), ("ml_w_out", [1, D, D]),
           ("da_w_in", [1, D, 3 * D]), ("da_lambda", [1, 4, 128]), ("da_sub_g", [1, 256]), ("da_w_out", [1, D, D])]
    for name, shape in inp:
        k.dram(name, shape, F32, kind="ExternalInput" if (need is None or name in need) else "Internal")
    k.dram("out", [NB, SEQ, D], F32, kind="ExternalOutput")
    k.dram("mod", [2, 3, 6 * D], F32)
    k.dram("w1b", [2, D, DFF], BF16)
    k.dram("w2b", [2, DFF, D], BF16)
    k.dram("qk0", [NB, NTOK, D], BF16)
    k.dram("v0", [NB, NTOK, D], BF16)
    k.dram("og0", [NB, NTOK, D], BF16)
    k.dram("gt0", [NB, NTOK, 32], F32)
    k.dram("hf0", [NB, NTOK, D], F32)
    k.dram("hb0", [NB, NTOK, D], F32)
    k.dram("yin", [NB, NTOK, D], BF16)
    k.dram("xmid", [NB, NTOK, D], F32)
    k.dram("hTd", [NB, NT, 128, 16 * 128], BF16)
    k.dram("xl1", [NB, NTOK, D], F32)
    k.dram("qT1", [NB, 16, 128, SEQ], BF16)
    k.dram("kT1", [NB, 16, 128, NTOK], BF16)
    k.dram("v1", [NB, NTOK, D], BF16)


def build(phases, dbg=(), ext_in=(), need_inputs=None):
    nc = bass.Bass("TRN2", target_bir_lowering=False)
    k = K(nc, dbg, ext_in)
    k.need_inputs = need_inputs
    declare_dram(k)
    setup_consts(k)
    for ph in phases:
        ph(k)
    k.P.emit()
    return nc, k


def all_phases(k):
    phase_ada(k)
    for bl in range(NB):
        phase_inproj0(k, bl)
        phase_scan0(k, bl, cast_thunks(k, bl))
    phase_post2(k, 0, post_blocks0(k), k.d["ml_w_out"][0])
    for bl in range(NB):
        phase_inproj1(k, bl)
        phase_attn1(k, bl)
    phase_post2(k, 1, post_blocks1(k), k.d["da_w_out"][0])


_NC_CACHE = {}


def kernel(**inputs):
    n = 8
    if "nc" not in _NC_CACHE:
        _NC_CACHE["nc"] = build([all_phases])[0]
    nc = _NC_CACHE["nc"]
    shared = {kk: np.ascontiguousarray(v, dtype=np.float32) for kk, v in inputs.items() if kk not in ("x", "c", "ctx")}
    in_maps = []
    for i in range(n):
        m = dict(shared)
        for kk in ("x", "c", "ctx"):
            m[kk] = np.ascontiguousarray(inputs[kk][i * NB:(i + 1) * NB], dtype=np.float32)
        in_maps.append(m)
    res = run_bass_kernel_spmd(nc, in_maps, core_ids=list(range(n)))
    return np.concatenate([r["out"] for r in res.results], axis=0).astype(np.float32)
```

```python
import math
import numpy as np
import concourse.bass as bass
import concourse.mybir as mybir
from concourse.bass_utils import run_bass_kernel_spmd

F32 = mybir.dt.float32
BF16 = mybir.dt.bfloat16
I32 = mybir.dt.int32
AF = mybir.ActivationFunctionType
ALU = mybir.AluOpType
AX = mybir.AxisListType

D = 2048
DFF = 8192
SEQ = 2048
CTX = 256
NTOK = SEQ + CTX
NT = NTOK // 128
NB = 2
H = 8
EPS = 1e-6
ML_IN = 6176
EPOCH = 12000
POST_STOP = 0

ENGS = ("pe", "act", "dve", "pool", "sp")


class Buf:
    __slots__ = ("name", "writer", "readers", "sem", "ndma", "excl")

    def __init__(self, name, excl=False):
        self.name = name
        self.excl = excl
        self.writer = None
        self.readers = {}
        self.sem = None
        self.ndma = 0


class Op:
    __slots__ = ("eng", "fn", "deps", "signal", "sem", "tick", "dma", "idx", "vsem")


class VSem:
    __slots__ = ("count", "handle")

    def __init__(self):
        self.count = 0
        self.handle = None


class Prog:
    def __init__(self, nc):
        self.nc = nc
        self.q = {e: [] for e in ENGS}
        self.fence = {e: None for e in ENGS}
        self.dma_last = {}
        self.nsem = 0
        self.free_vsems = []
        self.live_dma_bufs = []

    def op(self, eng, fn, r=(), w=(), dma=None):
        o = Op()
        o.eng = eng; o.fn = fn; o.signal = False; o.sem = None; o.tick = 0; o.dma = dma
        deps = []
        f = self.fence[eng]
        if f is not None:
            deps.extend(f)
            self.fence[eng] = None
        for b in r:
            d = b.writer
            if d is not None:
                if d.dma is not None or d.eng != eng or eng != "pe":
                    deps.append(d)
            if b.excl:
                for d in b.readers.values():
                    if d.eng != eng:
                        deps.append(d)
        for b in w:
            d = b.writer
            if d is not None and (d.dma is not None or d.eng != eng):
                deps.append(d)
            for d in b.readers.values():
                if d.dma is not None or d.eng != eng:
                    deps.append(d)
        for b in r:
            b.readers[id(dma) if dma is not None else eng] = o
        for b in w:
            b.writer = o
            b.readers = {}
        seen = set()
        dd = []
        for d in deps:
            if id(d) not in seen:
                seen.add(id(d)); dd.append(d); d.signal = True
        o.deps = dd
        o.idx = len(self.q[eng])
        self.q[eng].append(o)
        if dma is not None:
            o.signal = True
            self.dma_last[id(dma)] = o
            if dma.sem is None:
                dma.sem = self.free_vsems.pop() if self.free_vsems else VSem()
                self.live_dma_bufs.append(dma)
            dma.sem.count += 1
            o.vsem = dma.sem
            o.tick = 16 * dma.sem.count
            assert o.tick < 60000, dma.name
        return o

    def release_dma_sems(self):
        for b in self.live_dma_bufs:
            self.free_vsems.append(b.sem)
            self.dma_last.pop(id(b), None)
            b.sem = None
        self.live_dma_bufs = []

    def barrier(self):
        ops = []
        for e in ENGS:
            for o in reversed(self.q[e]):
                if o.dma is None:
                    o.signal = True
                    ops.append(o)
                    break
        ops.extend(self.dma_last.values())
        for e in ENGS:
            self.fence[e] = list(ops) + (self.fence[e] or [])

    def emit(self):
        nc = self.nc
        self.barrier()
        for e in ENGS:
            cnt = 0
            sem = None
            for o in self.q[e]:
                if o.dma is not None:
                    v = o.vsem
                    if v.handle is None:
                        v.handle = nc.alloc_semaphore("d%d" % self.nsem); self.nsem += 1
                    o.sem = v.handle
                elif o.signal:
                    if sem is None or cnt >= EPOCH:
                        sem = nc.alloc_semaphore("e%d" % self.nsem); self.nsem += 1
                        cnt = 0
                    cnt += 1
                    o.sem = sem
                    o.tick = cnt
        q = self.q

        def run(ename):
            def f(e):
                known = {}
                for o in q[ename]:
                    for d in o.deps:
                        k = known.get(d.sem, 0)
                        if d.tick > k:
                            e.wait_ge(d.sem, d.tick)
                            known[d.sem] = d.tick
                    ins = o.fn(e)
                    if o.signal:
                        ins.then_inc(o.sem, 16 if o.dma is not None else 1)
                fin = self.fence[ename]
                if fin:
                    for d in fin:
                        k = known.get(d.sem, 0)
                        if d.tick > k:
                            e.wait_ge(d.sem, d.tick)
                            known[d.sem] = d.tick
            return f

        with nc.Block() as block:
            block.sync(run("sp"))
            block.tensor(run("pe"))
            block.scalar(run("act"))
            block.vector(run("dve"))
            block.gpsimd(run("pool"))


class Pool:
    def __init__(self, K, name, shape, dtype, n=1, psum=False):
        self.tiles = []
        for i in range(n):
            nm = "%s_%d_%d" % (name, K.uid(), i)
            if psum:
                t = K.nc.alloc_psum_tensor(nm, shape, dtype)
            else:
                t = K.nc.alloc_sbuf_tensor(nm, shape, dtype)
            self.tiles.append((t, Buf(nm)))
        self.i = 0

    def next(self):
        t = self.tiles[self.i % len(self.tiles)]
        self.i += 1
        return t


class Tile:
    __slots__ = ("t", "b")

    def __init__(self, t, b):
        self.t = t
        self.b = b


class Phase:
    def __init__(self, K, name):
        self.K = K
        self.name = name
        self.stack = []

    def __enter__(self):
        return self

    def __exit__(self, *a):
        self.K.P.barrier()
        self.K.P.release_dma_sems()
        for g in reversed(self.stack):
            g.__exit__(None, None, None)
        return False

    def sb(self, name, shape, dtype, n=None):
        K = self.K
        res = []
        for i in range(n or 1):
            nm = "%s_%s_%d_%d" % (self.name, name, K.uid(), i)
            g = K.nc.sbuf_tensor(nm, shape, dtype)
            t = g.__enter__()
            self.stack.append(g)
            res.append(Tile(t, Buf(nm)))
        return res if n is not None else res[0]

    def ps(self, name, shape, dtype, n=None):
        K = self.K
        res = []
        for i in range(n or 1):
            nm = "%s_%s_%d_%d" % (self.name, name, K.uid(), i)
            g = K.nc.psum_tensor(nm, shape, dtype)
            t = g.__enter__()
            self.stack.append(g)
            res.append(Tile(t, Buf(nm, excl=True)))
        return res if n is not None else res[0]


class Rot:
    def __init__(self, tiles):
        self.tiles = tiles
        self.i = 0

    def next(self):
        t = self.tiles[self.i % len(self.tiles)]
        self.i += 1
        return t


class K:
    def __init__(self, nc, dbg=(), ext_in=()):
        self.ext_in = set(ext_in)
        self.nc = nc
        self.P = Prog(nc)
        self._uid = 0
        self.dbg = set(dbg)
        self.d = {}
        self.db = {}
        self.evac_i = 0

    def uid(self):
        self._uid += 1
        return self._uid

    def phase(self, name):
        return Phase(self, name)

    def dram(self, name, shape, dtype, kind=None):
        if kind is None:
            kind = "ExternalOutput" if name in self.dbg else ("ExternalInput" if name in self.ext_in else "Internal")
        self.d[name] = self.nc.dram_tensor(name, list(shape), dtype, kind=kind).ap()
        return self.d[name]

    def dbuf(self, key):
        b = self.db.get(key)
        if b is None:
            b = Buf(str(key))
            self.db[key] = b
        return b

    def dma(self, q, out, in_, slot, r=(), w=(), slow=False):
        if slow:
            fn = lambda e: e.dma_start(out=out, in_=in_, allow_slow_non_contiguous=True)
        else:
            fn = lambda e: e.dma_start(out=out, in_=in_)
        return self.P.op(q, fn, r=r, w=w, dma=slot)

    def load(self, q, tile, out, in_, r=(), slow=False):
        return self.dma(q, out, in_, tile.b, r=r, w=[tile.b], slow=slow)

    def store(self, q, tile, out, in_, w=(), slow=False):
        return self.dma(q, out, in_, tile.b, r=[tile.b], w=w, slow=slow)

    def mm(self, out, lhsT, rhs, start, stop, r, w):
        return self.P.op("pe", lambda e: e.matmul(out, lhsT=lhsT, rhs=rhs, start=start, stop=stop), r=r, w=w)

    def tr(self, out, in_, ident, r, w):
        return self.P.op("pe", lambda e: e.transpose(out, in_, ident), r=r, w=w)

    def act(self, out, in_, func, r, w, bias=None, scale=None, accum=None, eng="act"):
        kw = {}
        if bias is not None:
            kw["bias"] = bias
        if scale is not None:
            kw["scale"] = scale
        if accum is not None:
            kw["accum_out"] = accum
        return self.P.op("act", lambda e: e.activation(out=out, in_=in_, func=func, **kw), r=r, w=w)

    def ts(self, eng, out, in0, s1, s2, op0, op1, r, w):
        if op1 is None:
            return self.P.op(eng, lambda e: e.tensor_scalar(out=out, in0=in0, scalar1=s1, scalar2=None, op0=op0), r=r, w=w)
        return self.P.op(eng, lambda e: e.tensor_scalar(out=out, in0=in0, scalar1=s1, scalar2=s2, op0=op0, op1=op1), r=r, w=w)

    def tt(self, eng, out, in0, in1, op, r, w):
        return self.P.op(eng, lambda e: e.tensor_tensor(out=out, in0=in0, in1=in1, op=op), r=r, w=w)

    def stt(self, eng, out, in0, scalar, in1, op0, op1, r, w):
        return self.P.op(eng, lambda e: e.scalar_tensor_tensor(out=out, in0=in0, scalar=scalar, in1=in1, op0=op0, op1=op1), r=r, w=w)

    def copy(self, eng, out, in_, r, w):
        if eng == "act":
            return self.P.op("act", lambda e: e.activation(out=out, in_=in_, func=AF.Copy), r=r, w=w)
        return self.P.op(eng, lambda e: e.tensor_copy(out=out, in_=in_), r=r, w=w)

    def recip(self, out, in_, r, w):
        return self.P.op("dve", lambda e: e.reciprocal(out=out, in_=in_), r=r, w=w)

    def memset(self, eng, out, val, w):
        return self.P.op(eng, lambda e: e.memset(out, val), r=(), w=w)

    def evac_eng(self):
        self.evac_i += 1
        return "act" if self.evac_i % 2 else "dve"


def phase_ada(k):
    P = k.P
    d = k.d
    with k.phase("ada") as ph:
        cT = ph.sb("cT", [128, 16, 3], F32)
        for r in range(3):
            src = d["c"][r] if r < 2 else d["c_ctx"]
            src = src.rearrange("(kc p) -> p kc", p=128)
            k.load("sp", cT, cT.t[:, :, r], src, slow=True)
        k.act(cT.t[:], cT.t[:], AF.Silu, r=[cT.b], w=[cT.b])
        cTb = ph.sb("cTb", [128, 16, 3], BF16)
        k.copy("dve", cTb.t[:], cT.t[:], r=[cT.b], w=[cTb.b])
        wrot = Rot(ph.sb("w", [128, 16, 512], BF16, n=4))
        brot = Rot(ph.sb("b", [3, 512], F32, n=2))
        orot = Rot(ph.sb("o", [3, 512], F32, n=2))
        prot = Rot(ph.ps("ps", [128, 512], F32, n=2))
        for l in range(2):
            for j in range(24):
                wt = wrot.next(); bt = brot.next(); ot = orot.next(); ps = prot.next()
                cs = slice(j * 512, (j + 1) * 512)
                k.load("pool", wt, wt.t[:], d["ada_w"][l][:, cs].rearrange("(kc p) n -> p kc n", p=128))
                k.load("sp", bt, bt.t[:], d["ada_b"][l][cs].partition_broadcast(3))
                for kc in range(16):
                    k.mm(ps.t[0:3, :], cTb.t[:, kc, :], wt.t[:, kc, :], kc == 0, kc == 15, r=[cTb.b, wt.b], w=[ps.b])
                k.tt("dve", ot.t[:], ps.t[0:3, :], bt.t[:], ALU.add, r=[ps.b, bt.b], w=[ot.b])
                k.store("sp", ot, d["mod"][l][:, cs], ot.t[:])


def cast_thunks(k, l):
    d = k.d
    s1 = Buf("cast1_%d" % l)
    s2 = Buf("cast2_%d" % l)
    th = []
    for i in range(16):
        r0 = i * 128
        th.append(lambda r0=r0, i=i: k.dma("pool", d["w1b"][l][r0:r0 + 128, :], d["mlp_w1"][l][r0:r0 + 128, :], s1,
                                           w=[k.dbuf(("w1b", l, i))]))
    for i in range(16):
        r0 = i * 512
        th.append(lambda r0=r0, i=i: k.dma("pool", d["w2b"][l][r0:r0 + 512, :], d["mlp_w2"][l][r0:r0 + 512, :], s2,
                                           w=[k.dbuf(("w2b", l, i))]))
    out = []
    for i in range(16):
        out.append(th[i]); out.append(th[16 + i])
    return out


def load_cols(k, ph, name, src_vec):
    t = ph.sb(name, [128, 16], F32)
    k.load("sp", t, t.t[:], src_vec.rearrange("(kc p) -> p kc", p=128), slow=True)
    return t


def mod_cols(k, ph, l, r, g_idx, shift_i, scale_i, tag):
    d = k.d
    g = load_cols(k, ph, "g" + tag, d["norm_g"][l][g_idx])
    sc = load_cols(k, ph, "sc" + tag, d["mod"][l][r][scale_i * D:(scale_i + 1) * D])
    sh = load_cols(k, ph, "sh" + tag, d["mod"][l][r][shift_i * D:(shift_i + 1) * D])
    k.stt("dve", sc.t[:], sc.t[:], 1.0, g.t[:], ALU.add, ALU.mult, r=[sc.b, g.b], w=[sc.b])
    return sc, sh


def token_src(k, l, bl, tt):
    d = k.d
    if l == 0:
        if tt < 2:
            return d["ctx"][bl][tt * 128:(tt + 1) * 128, :]
        return d["x"][bl][(tt - 2) * 128:(tt - 1) * 128, :]
    return d["xl1"][bl][tt * 128:(tt + 1) * 128, :]


def norm_part(k, xt, pools):
    junk, ssr, xnr, trr = pools
    jk = junk.next(); ss = ssr.next(); xn = xnr.next()
    k.act(jk.t[:], xt.t[:], AF.Square, r=[xt.b], w=[jk.b, ss.b], accum=ss.t[:, 0:1])
    k.act(ss.t[:, 1:2], ss.t[:, 0:1], AF.Sqrt, r=[ss.b], w=[ss.b], bias=EPS_T(k), scale=1.0 / D)
    k.recip(ss.t[:, 2:3], ss.t[:, 1:2], r=[ss.b], w=[ss.b])
    k.ts("dve", xn.t[:], xt.t[:], ss.t[:, 2:3], None, ALU.mult, None, r=[xt.b, ss.b], w=[xn.b])
    return xn


def trans_part(k, xn, hT_dst, A, Bm, ident, pools):
    trr = pools[3]
    for g4 in range(4):
        pt = trr.next()
        for j in range(4):
            kc = g4 * 4 + j
            k.tr(pt.t[:, j, :], xn.t[:, kc * 128:(kc + 1) * 128], ident.t[:], r=[xn.b, ident.b], w=[pt.b])
        for j in range(4):
            kc = g4 * 4 + j
            dst, dbuf = hT_dst(kc)
            if (kc % 2) == 0:
                k.act(dst, pt.t[:, j, :], AF.Identity, r=[pt.b, A.b, Bm.b], w=[dbuf],
                      bias=Bm.t[:, kc:kc + 1], scale=A.t[:, kc:kc + 1])
            else:
                k.ts("dve", dst, pt.t[:, j, :], A.t[:, kc:kc + 1], Bm.t[:, kc:kc + 1], ALU.mult, ALU.add,
                     r=[pt.b, A.b, Bm.b], w=[dbuf])


def norm_mod_T(k, ph, xt, hT_dst, A, Bm, ident, pools):
    xn = norm_part(k, xt, pools)
    trans_part(k, xn, hT_dst, A, Bm, ident, pools)


def EPS_T(k):
    return k.eps.t[:, 0:1]


def setup_consts(k):
    nc = k.nc
    P = k.P

    def sbt(name, shape, dtype):
        return Tile(nc.alloc_sbuf_tensor(name, shape, dtype), Buf(name))

    k.ident = sbt("ident", [128, 128], BF16)
    k.eps = sbt("epsc", [128, 1], F32)
    k.memset("pool", k.ident.t[:], 0.0, w=[k.ident.b])
    P.op("pool", lambda e: e.affine_select(out=k.ident.t[:], in_=k.ident.t[:], pattern=[[-1, 128]],
                                           compare_op=ALU.not_equal, fill=1.0, base=0, channel_multiplier=1),
         r=[k.ident.b], w=[k.ident.b])
    k.memset("pool", k.eps.t[:], EPS, w=[k.eps.b])


def phase_inproj0(k, bl):
    d = k.d
    with k.phase("ip0") as ph:
        hT = ph.sb("hT", [128, 16, NTOK], BF16)
        hTb = [Buf("hT%d" % i) for i in range(NT)]
        build_hT(k, ph, 0, bl, hT, hTb, ntr=4)
        wrot = Rot(ph.sb("w", [128, 16, 512], BF16, n=2))
        prot = Rot(ph.ps("ps", [128, 512], F32, n=3))
        orot = Rot(ph.sb("o", [128, 512], BF16, n=3))
        grot = Rot(ph.sb("g", [128, 32], F32, n=2))
        gtmp = Rot(ph.sb("gtmp", [128, 16], F32, n=2))
        bg = ph.sb("bg", [128, 32], F32)
        k.load("sp", bg, bg.t[:], d["ml_b_gates"][0].partition_broadcast(128))
        blocks = []
        for i in range(2):
            blocks.append(("q", i * 512, 512, "qk0", i * 512))
        for i in range(2):
            blocks.append(("k", 1024 + i * 512, 512, "qk0", 1024 + i * 512))
        for i in range(4):
            blocks.append(("v", 2048 + i * 512, 512, "v0", i * 512))
        for i in range(4):
            blocks.append(("og", 4096 + i * 512, 512, "og0", i * 512))
        blocks.append(("g", 6144, 32, "gt0", 0))
        for (kind, c0, ncols, dst, dc0) in blocks:
            wt = wrot.next()
            k.load("pool", wt, wt.t[:, :, 0:ncols], d["ml_w_in"][0][:, c0:c0 + ncols].rearrange("(kc p) n -> p kc n", p=128))
            for tt in range(NT):
                ps = prot.next()
                for kc in range(16):
                    k.mm(ps.t[:, 0:ncols], hT.t[:, kc, tt * 128:(tt + 1) * 128], wt.t[:, kc, 0:ncols], kc == 0, kc == 15,
                         r=[hTb[tt], wt.b], w=[ps.b])
                rows = slice(tt * 128, (tt + 1) * 128)
                if kind == "g":
                    gt = grot.next(); tmp = gtmp.next()
                    k.tt("dve", gt.t[:], ps.t[:, 0:32], bg.t[:], ALU.add, r=[ps.b, bg.b], w=[gt.b])
                    gv = gt.t[:].rearrange("p (a b c) -> p a b c", a=2, b=2, c=8)[:, :, 1, :]
                    tv = tmp.t[:].rearrange("p (a c) -> p a c", a=2, c=8)
                    k.act(tv, gv, AF.Exp, r=[gt.b], w=[tmp.b], scale=-1.0)
                    k.act(tv, tv, AF.Ln, r=[tmp.b], w=[tmp.b], bias=1.0)
                    k.ts("dve", gv, tv, -1.0, None, ALU.mult, None, r=[tmp.b, gt.b], w=[gt.b])
                    k.store("sp", gt, d["gt0"][bl][rows, :], gt.t[:])
                else:
                    ot = orot.next()
                    eng = k.evac_eng()
                    if kind == "q":
                        if eng == "act":
                            k.act(ot.t[:], ps.t[:], AF.Copy, r=[ps.b], w=[ot.b], scale=128.0 ** -0.5)
                        else:
                            k.ts("dve", ot.t[:], ps.t[:], 128.0 ** -0.5, None, ALU.mult, None, r=[ps.b], w=[ot.b])
                    else:
                        k.copy(eng, ot.t[:], ps.t[:], r=[ps.b], w=[ot.b])
                    k.store("sp", ot, d[dst][bl][rows, dc0:dc0 + 512], ot.t[:])


def bcast_last(ap2, n):
    a = ap2.ap
    return bass.AP(ap2.tensor, ap2.offset, [list(a[0]), list(a[1]), [0, n]])


def bcast_mid(ap2, n):
    a = ap2.ap
    return bass.AP(ap2.tensor, ap2.offset, [list(a[0]), [0, n], list(a[1])])


def phase_scan0(k, bl, thunks=()):
    d = k.d
    P = k.P
    with k.phase("scan") as ph:
        ones = ph.sb("ones", [128, 128], F32)
        triF = ph.sb("triF", [128, 128], F32)
        triB = ph.sb("triB", [128, 128], F32)
        mskF = ph.sb("mskF", [128, 128], BF16)
        mskB = ph.sb("mskB", [128, 128], BF16)
        k.memset("pool", ones.t[:], 1.0, w=[ones.b])
        for tri, msk, sgn in ((triF, mskF, -1), (triB, mskB, 1)):
            k.memset("pool", tri.t[:], 1.0, w=[tri.b])
            P.op("pool", lambda e, tri=tri, sgn=sgn: e.affine_select(out=tri.t[:], in_=tri.t[:], pattern=[[-sgn, 128]],
                 compare_op=ALU.is_ge, fill=0.0, base=0, channel_multiplier=sgn), r=[tri.b], w=[tri.b])
            k.copy("pool", msk.t[:], tri.t[:], r=[tri.b], w=[msk.b])
        HG = ph.sb("HG", [128, D], F32)
        k.load("sp", HG, HG.t[:], d["ml_head_g"][0].partition_broadcast(128))
        U = ph.sb("U", [128, 8, 257], F32)
        Cb = ph.sb("Cb", [128, 8, 257], BF16)
        Ub = [Buf("U%d" % h) for h in range(8)]
        Cbb = [Buf("Cb%d" % h) for h in range(8)]
        qkr = Rot(ph.sb("qk", [128, D], BF16, n=4))
        var = ph.sb("va", [128, 8, 257], BF16, n=4)
        for v in var:
            k.memset("pool", v.t[:, :, 256:257], 1.0, w=[v.b])
        var = Rot(var)
        gtr = Rot(ph.sb("gt", [128, 32], F32, n=4))
        smr = Rot(ph.sb("sm", [128, 40], F32, n=6))
        Qer = Rot(ph.sb("Qe", [128, 8, 128], BF16, n=2))
        Kfr = Rot(ph.sb("Kf", [128, 8, 128], BF16, n=2))
        QeTr = Rot(ph.sb("QeT", [128, 8, 128], BF16, n=2))
        KfTr = Rot(ph.sb("KfT", [128, 8, 128], BF16, n=2))
        ATr = Rot(ph.sb("AT", [128, 8, 128], BF16, n=2))
        hfr = Rot(ph.sb("hf", [128, D], F32, n=4))
        hsr = Rot(ph.sb("hs", [128, D], F32, n=2))
        ogr = Rot(ph.sb("og", [128, D], BF16, n=2))
        sgr = Rot(ph.sb("sg", [128, D], BF16, n=2))
        w1r = Rot(ph.sb("w1", [128, D], BF16, n=2))
        yir = Rot(ph.sb("yi", [128, D], BF16, n=2))
        rdr = Rot(ph.sb("rd", [128, 2], F32, n=4))
        msr = Rot(ph.sb("ms", [128, 24], F32, n=2))
        jkr = Rot(ph.sb("jk", [128, 256], BF16, n=2))
        cs_ps = Rot(ph.ps("cs", [128, 16], F32, n=1))
        tr_ps = Rot(ph.ps("tr", [128, 4, 128], BF16, n=2))
        st_ps = Rot(ph.ps("st", [128, 4, 128], F32, n=2))
        nm_ps = Rot(ph.ps("nm", [128, 257], F32, n=2))
        pp_ps = Rot(ph.ps("pp", [128, 257], F32, n=1))
        orders = (list(range(NT)), [1, 0] + list(range(NT - 1, 1, -1)))
        Us = (U, ph.sb("U2", [128, 8, 257], F32))
        Cbs = (Cb, ph.sb("Cb2", [128, 8, 257], BF16))
        Ubs = (Ub, [Buf("U2_%d" % h) for h in range(8)])
        Cbbs = (Cbb, [Buf("Cb2_%d" % h) for h in range(8)])
        smps = [None, None]
        for dirn in range(2):
            k.memset("pool", Us[dirn].t[:], 0.0, w=Ubs[dirn])
            k.memset("pool", Cbs[dirn].t[:], 0.0, w=Cbbs[dirn])

        def step_loads(dirn, tt):
            rows = slice(tt * 128, (tt + 1) * 128)
            qk = qkr.next(); va = var.next(); gt = gtr.next()
            k.load("sp", qk, qk.t[:], d["qk0"][bl][rows, :])
            k.load("sp", va, va.t[:, :, 0:256], d["v0"][bl][rows, :].rearrange("p (h e) -> p h e", h=8))
            k.load("sp", gt, gt.t[:], d["gt0"][bl][rows, :])
            return (qk, va, gt)

        def step(dirn, tt, tiles3):
            tri = triF if dirn == 0 else triB
            msk = mskF if dirn == 0 else mskB
            U_, Cb_, Ub_, Cbb_ = Us[dirn], Cbs[dirn], Ubs[dirn], Cbbs[dirn]
            smp = smps[dirn]
            rows = slice(tt * 128, (tt + 1) * 128)
            qk, va, gt = tiles3
            sm = smr.next()
            ig = gt.t[:, dirn * 16:dirn * 16 + 8]
            lf = gt.t[:, dirn * 16 + 8:dirn * 16 + 16]
            cs = cs_ps.next()
            k.mm(cs.t[:, 0:8], tri.t[:], lf, True, True, r=[tri.b, gt.b], w=[cs.b])
            k.mm(cs.t[:, 8:16], ones.t[:], lf, True, True, r=[ones.b, gt.b], w=[cs.b])
            E = sm.t[:, 0:8]; Fv = sm.t[:, 8:16]; Tc = sm.t[:, 16:24]; tmp = sm.t[:, 24:32]
            k.act(E, cs.t[:, 0:8], AF.Exp, r=[cs.b], w=[sm.b])
            k.act(Tc, cs.t[:, 8:16], AF.Exp, r=[cs.b], w=[sm.b])
            k.tt("dve", tmp, ig, cs.t[:, 0:8], ALU.subtract, r=[gt.b, cs.b], w=[sm.b])
            k.act(Fv, tmp, AF.Exp, r=[sm.b], w=[sm.b])
            yield
            Qe = Qer.next(); Kf = Kfr.next(); QeT = QeTr.next(); KfT = KfTr.next(); AT = ATr.next()
            k.tt("dve", Qe.t[:], qk.t[:, 0:1024].rearrange("p (h e) -> p h e", h=8), bcast_last(E, 128), ALU.mult,
                 r=[qk.b, sm.b], w=[Qe.b])
            k.tt("pool", Kf.t[:], qk.t[:, 1024:2048].rearrange("p (h e) -> p h e", h=8), bcast_last(Fv, 128), ALU.mult,
                 r=[qk.b, sm.b], w=[Kf.b])
            yield
            for src, dstT in ((Qe, QeT), (Kf, KfT)):
                for g in range(2):
                    pt = tr_ps.next()
                    for j in range(4):
                        k.tr(pt.t[:, j, :], src.t[:, g * 4 + j, :], k.ident.t[:], r=[src.b, k.ident.b], w=[pt.b])
                    k.copy(k.evac_eng(), dstT.t[:, g * 4:(g + 1) * 4, :], pt.t[:], r=[pt.b], w=[dstT.b])
            yield
            for g in range(2):
                st = st_ps.next()
                for j in range(4):
                    h = g * 4 + j
                    k.mm(st.t[:, j, :], KfT.t[:, h, :], QeT.t[:, h, :], True, True, r=[KfT.b, QeT.b], w=[st.b])
                k.tt("dve", AT.t[:, g * 4:(g + 1) * 4, :], st.t[:], bcast_mid(msk.t[:], 4), ALU.mult,
                     r=[st.b, msk.b], w=[AT.b])
            yield
            hf = hfr.next()
            for h in range(8):
                nm = nm_ps.next(); rd = rdr.next(); pp = pp_ps.next()
                k.mm(nm.t[:], AT.t[:, h, :], va.t[:, h, :], True, False, r=[AT.b, va.b], w=[nm.b])
                k.mm(nm.t[:], QeT.t[:, h, :], Cb_.t[:, h, :], False, True, r=[QeT.b, Cbb_[h]], w=[nm.b])
                k.act(rd.t[:, 0:1], nm.t[:, 256:257], AF.Abs, r=[nm.b], w=[rd.b])
                k.ts("dve", rd.t[:, 0:1], rd.t[:, 0:1], 1.0, None, ALU.max, None, r=[rd.b], w=[rd.b])
                k.recip(rd.t[:, 1:2], rd.t[:, 0:1], r=[rd.b], w=[rd.b])
                hsl = slice(h * 256, (h + 1) * 256)
                if h % 2 == 0:
                    k.act(hf.t[:, hsl], nm.t[:, 0:256], AF.Copy, r=[nm.b, rd.b], w=[hf.b], scale=rd.t[:, 1:2])
                else:
                    k.ts("dve", hf.t[:, hsl], nm.t[:, 0:256], rd.t[:, 1:2], None, ALU.mult, None, r=[nm.b, rd.b], w=[hf.b])
                yield
                k.mm(pp.t[:], Kf.t[:, h, :], va.t[:, h, :], True, True, r=[Kf.b, va.b], w=[pp.b])
                if smp is None:
                    k.copy("dve", U_.t[:, h, :], pp.t[:], r=[pp.b], w=[Ub_[h]])
                else:
                    k.stt("dve", U_.t[:, h, :], U_.t[:, h, :], smp.t[:, 16 + h:17 + h], pp.t[:], ALU.mult, ALU.add,
                          r=[Ub_[h], smp.b, pp.b], w=[Ub_[h]])
                k.act(Cb_.t[:, h, :], U_.t[:, h, :], AF.Copy, r=[Ub_[h], sm.b], w=[Cbb_[h]], scale=sm.t[:, 16 + h:17 + h])
            yield
            smps[dirn] = sm
            dst = "hf0" if dirn == 0 else "hb0"
            k.store("sp", hf, d[dst][bl][rows, :], hf.t[:], w=[k.dbuf((dst, bl, tt))])

        thunks = list(thunks)
        done_at = {}
        for i in range(NT):
            done_at.setdefault(orders[0][i], []).append(i)
            done_at.setdefault(orders[1][i], []).append(i)
        ready = {}
        for tt, v in done_at.items():
            ready.setdefault(max(v), []).append(tt)

        def combine(tt):
            rows = slice(tt * 128, (tt + 1) * 128)
            hf = hfr.next(); hb = hfr.next(); hs = hsr.next()
            og = ogr.next(); sg = sgr.next(); w1 = w1r.next(); yi = yir.next(); ms = msr.next(); jk = jkr.next()
            k.load("sp", hf, hf.t[:], d["hf0"][bl][rows, :], r=[k.dbuf(("hf0", bl, tt))])
            k.load("sp", hb, hb.t[:], d["hb0"][bl][rows, :], r=[k.dbuf(("hb0", bl, tt))])
            k.load("sp", og, og.t[:], d["og0"][bl][rows, :])
            k.tt("dve", hs.t[:], hf.t[:], hb.t[:], ALU.add, r=[hf.b, hb.b], w=[hs.b])
            for h in range(8):
                k.act(jk.t[:], hs.t[:, h * 256:(h + 1) * 256], AF.Square, r=[hs.b], w=[jk.b, ms.b], accum=ms.t[:, h:h + 1])
            k.act(ms.t[:, 8:16], ms.t[:, 0:8], AF.Sqrt, r=[ms.b], w=[ms.b], bias=EPS_T(k), scale=1.0 / 256)
            k.recip(ms.t[:, 16:24], ms.t[:, 8:16], r=[ms.b], w=[ms.b])
            k.act(sg.t[:], og.t[:], AF.Sigmoid, r=[og.b], w=[sg.b])
            k.tt("pool", w1.t[:], sg.t[:], HG.t[:], ALU.mult, r=[sg.b, HG.b], w=[w1.b])
            for h in range(8):
                hsl = slice(h * 256, (h + 1) * 256)
                k.stt("dve", yi.t[:, hsl], hs.t[:, hsl], ms.t[:, 16 + h:17 + h], w1.t[:, hsl], ALU.mult, ALU.mult,
                      r=[hs.b, ms.b, w1.b], w=[yi.b])
            k.store("sp", yi, d["yin"][bl][rows, :], yi.t[:])

        pre = [step_loads(0, orders[0][0]), step_loads(1, orders[1][0])]
        for i in range(NT):
            gens = []
            for dirn in range(2):
                cur3 = pre[dirn]
                if i + 1 < NT:
                    pre[dirn] = step_loads(dirn, orders[dirn][i + 1])
                gens.append(step(dirn, orders[dirn][i], cur3))
            while gens:
                for g_ in list(gens):
                    try:
                        next(g_)
                    except StopIteration:
                        gens.remove(g_)
            for _ in range(2):
                if thunks:
                    thunks.pop(0)()
            for tt in ready.get(i - 1, []):
                combine(tt)
        for tt in ready.get(NT - 1, []):
            combine(tt)
        while thunks:
            thunks.pop(0)()


def bcast_row_tile(k, ph, name, vec_ap):
    t = ph.sb(name, [128, D], F32)
    k.load("sp", t, t.t[:], vec_ap.partition_broadcast(128))
    return t


def phase_post(k, l, blocks, w_out_ap):
    for blk in blocks:
        phase_post_block(k, l, [blk], w_out_ap)


def phase_post_block(k, l, blocks, w_out_ap):
    d = k.d
    P = k.P
    with k.phase("post%d" % l) as ph:
        NTB = 4
        uT = ph.sb("uT", [128, 64, NTB * 128], BF16)
        hT = ph.sb("hT", [128, 16, NTB * 128], BF16)
        fg = ph.sb("fg", [128, NTB, D], F32)
        fgb = [Buf("fg%d" % i) for i in range(NTB)]
        hTb = [Buf("hTb%d" % i) for i in range(NTB)]
        xrot = Rot(ph.sb("x", [128, D], F32, n=2))
        yrot = Rot(ph.sb("y", [128, D], BF16, n=2))
        pools = (Rot(ph.sb("junk", [128, D], BF16, n=1)), Rot(ph.sb("ss", [128, 4], F32, n=2)),
                 Rot(ph.sb("xn", [128, D], BF16, n=1)), Rot(ph.ps("tr", [128, 4, 128], BF16, n=2)))
        wrot = Rot(ph.sb("w", [128, 16, 256], BF16, n=2))
        w2rot = Rot(ph.sb("w2", [128, 4, 512], BF16, n=2))
        G1 = ph.sb("G1", [128, D], F32)
        G3 = ph.sb("G3", [128, D], F32)
        gtmp = ph.sb("gtmp", [128, D], F32)
        rlr = Rot(ph.sb("rl", [128, NTB * 128], BF16, n=2))
        sqr = Rot(ph.sb("sq", [128, 16], F32, n=NTB * 2))
        jk2 = Rot(ph.sb("jk2", [128, 512], BF16, n=1))
        prot = Rot(ph.ps("ps", [128, 512], F32, n=6))
        ubuf = [Buf("u%d" % i) for i in range(64)]
        cols = {}
        cur_cond = None
        for blk in blocks:
            nt = len(blk)
            N = nt * 128
            cond = blk[0][0]
            if cond != cur_cond:
                cur_cond = cond
                for (G, gi, mi) in ((G1, 1, 2), (G3, 3, 5)):
                    k.load("sp", G, G.t[:], d["mod"][l][cond][mi * D:(mi + 1) * D].partition_broadcast(128))
                    k.load("sp", gtmp, gtmp.t[:], d["norm_g"][l][gi].partition_broadcast(128))
                    k.tt("pool", G.t[:], G.t[:], gtmp.t[:], ALU.mult, r=[G.b, gtmp.b], w=[G.b])
                if cond not in cols:
                    cols[cond] = mod_cols(k, ph, l, cond, 2, 3, 4, "m%d" % cond)
                A2, B2 = cols[cond]
            for t, (_, yin_ap, xin_ap, xmid_ap, xout_ap, key) in enumerate(blk):
                yt = yrot.next()
                k.load("sp", yt, yt.t[:], yin_ap)
                for g4 in range(4):
                    pt = pools[3].next()
                    for j in range(4):
                        kc = g4 * 4 + j
                        k.tr(pt.t[:, j, :], yt.t[:, kc * 128:(kc + 1) * 128], k.ident.t[:], r=[yt.b, k.ident.b], w=[pt.b])
                    k.copy(k.evac_eng(), hT.t[:, g4 * 4:(g4 + 1) * 4, t * 128:(t + 1) * 128], pt.t[:], r=[pt.b], w=[hTb[t]])
            sqs = [sqr.next() for _ in range(nt)]
            for cb in range(8):
                wt = wrot.next()
                cs = slice(cb * 256, (cb + 1) * 256)
                k.load("pool", wt, wt.t[:], w_out_ap[:, cs].rearrange("(kc p) n -> p kc n", p=128))
                for t in range(nt):
                    ps = prot.next()
                    for kc in range(16):
                        k.mm(ps.t[:, 0:256], hT.t[:, kc, t * 128:(t + 1) * 128], wt.t[:, kc, :], kc == 0, kc == 15,
                             r=[hTb[t], wt.b], w=[ps.b])
                    k.tt("dve", fg.t[:, t, cs], ps.t[:, 0:256], G1.t[:, cs], ALU.mult, r=[ps.b, G1.b], w=[fgb[t]])
                    jk = jk2.next()
                    k.act(jk.t[:, 0:256], ps.t[:, 0:256], AF.Square, r=[ps.b], w=[jk.b, sqs[t].b], accum=sqs[t].t[:, cb:cb + 1])
            for t, (_, yin_ap, xin_ap, xmid_ap, xout_ap, key) in enumerate(blk):
                sq = sqs[t]
                xt = xrot.next()
                k.load("sp", xt, xt.t[:], xin_ap)
                P.op("dve", lambda e, sq=sq: e.tensor_reduce(out=sq.t[:, 8:9], in_=sq.t[:, 0:8], axis=AX.X, op=ALU.add),
                     r=[sq.b], w=[sq.b])
                k.act(sq.t[:, 9:10], sq.t[:, 8:9], AF.Sqrt, r=[sq.b], w=[sq.b], bias=EPS_T(k), scale=1.0 / D)
                k.recip(sq.t[:, 10:11], sq.t[:, 9:10], r=[sq.b], w=[sq.b])
                k.stt("dve", xt.t[:], fg.t[:, t, :], sq.t[:, 10:11], xt.t[:], ALU.mult, ALU.add, r=[fgb[t], sq.b, xt.b], w=[xt.b])
                k.store("sp", xt, xmid_ap, xt.t[:], w=[k.dbuf(("xmid", key))])
                norm_mod_T(k, ph, xt, lambda kc, t=t: (hT.t[:, kc, t * 128:(t + 1) * 128], hTb[t]), A2, B2, k.ident, pools)
            if POST_STOP == 1:
                continue
            for mb in range(32):
                wt = wrot.next()
                k.load("sp", wt, wt.t[:], d["w1b"][l][:, mb * 256:(mb + 1) * 256].rearrange("(kc p) n -> p kc n", p=128),
                       r=[k.dbuf(("w1b", l, i)) for i in range(16)])
                for mc in range(2):
                    ps = prot.next()
                    for kc in range(16):
                        k.mm(ps.t[:, 0:N], wt.t[:, kc, mc * 128:(mc + 1) * 128], hT.t[:, kc, 0:N], kc == 0, kc == 15,
                             r=[wt.b] + hTb[:nt], w=[ps.b])
                    rl = rlr.next()
                    k.act(rl.t[:, 0:N], ps.t[:, 0:N], AF.Relu, r=[ps.b], w=[rl.b])
                    m = mb * 2 + mc
                    k.tt("pool", uT.t[:, m, 0:N], rl.t[:, 0:N], rl.t[:, 0:N], ALU.mult, r=[rl.b], w=[ubuf[m]])
            if POST_STOP == 2:
                continue
            sqs = [sqr.next() for _ in range(nt)]
            for cb in range(4):
                cs = slice(cb * 512, (cb + 1) * 512)
                pss = [prot.next() for _ in range(nt)]
                for g in range(16):
                    w2 = w2rot.next()
                    k.load("sp", w2, w2.t[:], d["w2b"][l][g * 512:(g + 1) * 512, cs].rearrange("(kc p) n -> p kc n", p=128),
                           r=[k.dbuf(("w2b", l, g))])
                    for t in range(nt):
                        for j in range(4):
                            m = g * 4 + j
                            k.mm(pss[t].t[:], uT.t[:, m, t * 128:(t + 1) * 128], w2.t[:, j, :], m == 0, m == 63,
                                 r=[ubuf[m], w2.b], w=[pss[t].b])
                for t in range(nt):
                    k.tt("dve", fg.t[:, t, cs], pss[t].t[:], G3.t[:, cs], ALU.mult, r=[pss[t].b, G3.b], w=[fgb[t]])
                    jk = jk2.next()
                    k.act(jk.t[:], pss[t].t[:], AF.Square, r=[pss[t].b], w=[jk.b, sqs[t].b], accum=sqs[t].t[:, cb:cb + 1])
            for t, (_, yin_ap, xin_ap, xmid_ap, xout_ap, key) in enumerate(blk):
                sq = sqs[t]
                xt = xrot.next()
                k.load("sp", xt, xt.t[:], xmid_ap, r=[k.dbuf(("xmid", key))])
                P.op("dve", lambda e, sq=sq: e.tensor_reduce(out=sq.t[:, 8:9], in_=sq.t[:, 0:4], axis=AX.X, op=ALU.add),
                     r=[sq.b], w=[sq.b])
                k.act(sq.t[:, 9:10], sq.t[:, 8:9], AF.Sqrt, r=[sq.b], w=[sq.b], bias=EPS_T(k), scale=1.0 / D)
                k.recip(sq.t[:, 10:11], sq.t[:, 9:10], r=[sq.b], w=[sq.b])
                k.stt("dve", xt.t[:], fg.t[:, t, :], sq.t[:, 10:11], xt.t[:], ALU.mult, ALU.add, r=[fgb[t], sq.b, xt.b], w=[xt.b])
                k.store("sp", xt, xout_ap, xt.t[:])


def phase_P(k, l, tiles, w_out_ap):
    d = k.d
    P = k.P
    with k.phase("P%d" % l) as ph:
        Wo = ph.sb("Wo", [128, 16, D], BF16)
        Wob = [Buf("Wo%d" % i) for i in range(4)]
        for cb in range(4):
            cs = slice(cb * 512, (cb + 1) * 512)
            k.dma("pool", Wo.t[:, :, cs], w_out_ap[:, cs].rearrange("(kc p) n -> p kc n", p=128), Wo.b, w=[Wob[cb]])
        G1 = ph.sb("G1", [128, D], F32)
        gtmp = ph.sb("gtmp", [128, D], F32)
        yrot = Rot(ph.sb("y", [128, D], BF16, n=3))
        yTr = Rot(ph.sb("yT", [128, 16, 128], BF16, n=2))
        xrot = Rot(ph.sb("x", [128, D], F32, n=4))
        fgr = Rot(ph.sb("fg", [128, D], F32, n=2))
        hTr = Rot(ph.sb("hTt", [128, 16, 128], BF16, n=2))
        pools = (Rot(ph.sb("junk", [128, D], BF16, n=1)), Rot(ph.sb("ss", [128, 4], F32, n=3)),
                 Rot(ph.sb("xn", [128, D], BF16, n=3)), Rot(ph.ps("tr", [128, 4, 128], BF16, n=3)))
        sqr = Rot(ph.sb("sq", [128, 16], F32, n=4))
        jk2 = Rot(ph.sb("jk2", [128, 512], BF16, n=2))
        prot = Rot(ph.ps("ps", [128, 512], F32, n=5))
        cols = {}
        cur_cond = None
        prev = None
        tl = list(tiles)

        def finish_prev(pv):
            xn_p, A_p, B_p, key_p = pv
            hTt = hTr.next()
            trans_part(k, xn_p, lambda kc, hTt=hTt: (hTt.t[:, kc, :], hTt.b), A_p, B_p, k.ident, pools)
            k.store("sp", hTt, d["hTd"][key_p[0]][key_p[1]], hTt.t[:].rearrange("p a b -> p (a b)"),
                    w=[k.dbuf(("hTd", key_p))])

        def tile_loads(tile):
            yt = yrot.next(); xt = xrot.next()
            k.load("sp", yt, yt.t[:], tile[1])
            k.load("sp", xt, xt.t[:], tile[2])
            return yt, xt
        nxt_ld = tile_loads(tl[0])
        for ti, tile in enumerate(tl + [None]):
            cur = None
            if tile is not None:
                (cond, yin_ap, xin_ap, xmid_ap, xout_ap, key) = tile
                yt, xt = nxt_ld
                if ti + 1 < len(tl):
                    nxt_ld = tile_loads(tl[ti + 1])
                if cond != cur_cond:
                    cur_cond = cond
                    k.load("sp", G1, G1.t[:], d["mod"][l][cond][2 * D:3 * D].partition_broadcast(128))
                    k.load("sp", gtmp, gtmp.t[:], d["norm_g"][l][1].partition_broadcast(128))
                    k.tt("pool", G1.t[:], G1.t[:], gtmp.t[:], ALU.mult, r=[G1.b, gtmp.b], w=[G1.b])
                    if cond not in cols:
                        cols[cond] = mod_cols(k, ph, l, cond, 2, 3, 4, "m%d" % cond)
                A2, B2 = cols[cond]
                yT = yTr.next(); fg = fgr.next(); sq = sqr.next()
                for g4 in range(4):
                    pt = pools[3].next()
                    for j in range(4):
                        kc = g4 * 4 + j
                        k.tr(pt.t[:, j, :], yt.t[:, kc * 128:(kc + 1) * 128], k.ident.t[:], r=[yt.b, k.ident.b], w=[pt.b])
                    k.copy(k.evac_eng(), yT.t[:, g4 * 4:(g4 + 1) * 4, :], pt.t[:], r=[pt.b], w=[yT.b])
                pss4 = []
                for cb in range(4):
                    cs = slice(cb * 512, (cb + 1) * 512)
                    ps = prot.next()
                    pss4.append(ps)
                    for kc in range(16):
                        k.mm(ps.t[:], yT.t[:, kc, :], Wo.t[:, kc, cs], kc == 0, kc == 15, r=[yT.b, Wob[cb]], w=[ps.b])
                if prev is not None:
                    finish_prev(prev)
                    prev = None
                for cb in range(4):
                    cs = slice(cb * 512, (cb + 1) * 512)
                    ps = pss4[cb]
                    k.tt("dve", fg.t[:, cs], ps.t[:], G1.t[:, cs], ALU.mult, r=[ps.b, G1.b], w=[fg.b])
                    jk = jk2.next()
                    k.act(jk.t[:], ps.t[:], AF.Square, r=[ps.b], w=[jk.b, sq.b], accum=sq.t[:, cb:cb + 1])
                P.op("dve", lambda e, sq=sq: e.tensor_reduce(out=sq.t[:, 8:9], in_=sq.t[:, 0:4], axis=AX.X, op=ALU.add),
                     r=[sq.b], w=[sq.b])
                k.act(sq.t[:, 9:10], sq.t[:, 8:9], AF.Sqrt, r=[sq.b], w=[sq.b], bias=EPS_T(k), scale=1.0 / D)
                k.recip(sq.t[:, 10:11], sq.t[:, 9:10], r=[sq.b], w=[sq.b])
                k.stt("dve", xt.t[:], fg.t[:], sq.t[:, 10:11], xt.t[:], ALU.mult, ALU.add, r=[fg.b, sq.b, xt.b], w=[xt.b])
                k.store("sp", xt, xmid_ap, xt.t[:], w=[k.dbuf(("xmid", key))])
                xn = norm_part(k, xt, pools)
                cur = (xn, A2, B2, key)
            if prev is not None:
                finish_prev(prev)
            prev = cur


def phase_M(k, l, blocks):
    d = k.d
    P = k.P
    with k.phase("M%d" % l) as ph:
        NTB = 4
        uT = ph.sb("uT", [128, 64, NTB * 128], BF16)
        hT = ph.sb("hT", [128, NTB, 16, 128], BF16)
        fg = ph.sb("fg", [128, NTB, D], F32)
        fgb = [Buf("fg%d" % i) for i in range(NTB)]
        hTb = [Buf("hTb%d" % i) for i in range(NTB)]
        xrot = Rot(ph.sb("x", [128, D], F32, n=2))
        wrot = Rot(ph.sb("w", [128, 16, 256], BF16, n=3))
        w2rot = Rot(ph.sb("w2", [128, 4, 512], BF16, n=4))
        G3 = ph.sb("G3", [128, D], F32)
        gtmp = ph.sb("gtmp", [128, D], F32)
        rlr = Rot(ph.sb("rl", [128, NTB * 128], BF16, n=2))
        sqr = Rot(ph.sb("sq", [128, 16], F32, n=NTB * 2))
        jk2 = Rot(ph.sb("jk2", [128, 512], BF16, n=1))
        prot = Rot(ph.ps("ps", [128, 512], F32, n=8))
        ubuf = [Buf("u%d" % i) for i in range(64)]
        cur_cond = None

        def w1_load(mb):
            wt = wrot.next()
            k.load("sp", wt, wt.t[:], d["w1b"][l][:, mb * 256:(mb + 1) * 256].rearrange("(kc p) n -> p kc n", p=128),
                   r=[k.dbuf(("w1b", l, i)) for i in range(16)])
            return wt

        def block_prefetch(blk):
            for t, (_, yin_ap, xin_ap, xmid_ap, xout_ap, key) in enumerate(blk):
                k.dma("sp", hT.t[:, t, :, :].rearrange("p a b -> p (a b)"), d["hTd"][key[0]][key[1]], hT.b,
                      r=[k.dbuf(("hTd", key))], w=[hTb[t]])
            return [w1_load(mb) for mb in range(3)]
        pre_w = None
        for bi, blk in enumerate(blocks):
            nt = len(blk)
            N = nt * 128
            cond = blk[0][0]
            if cond != cur_cond:
                cur_cond = cond
                k.load("sp", G3, G3.t[:], d["mod"][l][cond][5 * D:6 * D].partition_broadcast(128))
                k.load("sp", gtmp, gtmp.t[:], d["norm_g"][l][3].partition_broadcast(128))
                k.tt("pool", G3.t[:], G3.t[:], gtmp.t[:], ALU.mult, r=[G3.b, gtmp.b], w=[G3.b])
            if pre_w is None:
                pre_w = block_prefetch(blk)
            my_w = pre_w
            pre_w = None
            for mb in range(32):
                if mb < len(my_w):
                    wt = my_w[mb]
                else:
                    wt = w1_load(mb)
                for mc in range(2):
                    ps = prot.next()
                    for kc in range(16):
                        k.mm(ps.t[:, 0:N].rearrange("p (a b) -> p a b", a=nt), wt.t[:, kc, mc * 128:(mc + 1) * 128],
                             hT.t[:, 0:nt, kc, :], kc == 0, kc == 15, r=[wt.b] + hTb[:nt], w=[ps.b])
                    rl = rlr.next()
                    k.act(rl.t[:, 0:N], ps.t[:, 0:N], AF.Relu, r=[ps.b], w=[rl.b])
                    m = mb * 2 + mc
                    k.tt("pool", uT.t[:, m, 0:N], rl.t[:, 0:N], rl.t[:, 0:N], ALU.mult, r=[rl.b], w=[ubuf[m]])
            sqs = [sqr.next() for _ in range(nt)]
            for cb in range(4):
                cs = slice(cb * 512, (cb + 1) * 512)
                pss = [prot.next() for _ in range(nt)]
                for g in range(16):
                    w2 = w2rot.next()
                    k.load("sp", w2, w2.t[:], d["w2b"][l][g * 512:(g + 1) * 512, cs].rearrange("(kc p) n -> p kc n", p=128),
                           r=[k.dbuf(("w2b", l, g))])
                    if cb == 3 and g == 12 and bi + 1 < len(blocks):
                        pre_w = block_prefetch(blocks[bi + 1])
                    for t in range(nt):
                        for j in range(4):
                            m = g * 4 + j
                            k.mm(pss[t].t[:], uT.t[:, m, t * 128:(t + 1) * 128], w2.t[:, j, :], m == 0, m == 63,
                                 r=[ubuf[m], w2.b], w=[pss[t].b])
                for t in range(nt):
                    k.tt("dve", fg.t[:, t, cs], pss[t].t[:], G3.t[:, cs], ALU.mult, r=[pss[t].b, G3.b], w=[fgb[t]])
                    jk = jk2.next()
                    k.act(jk.t[:], pss[t].t[:], AF.Square, r=[pss[t].b], w=[jk.b, sqs[t].b], accum=sqs[t].t[:, cb:cb + 1])
            for t, (_, yin_ap, xin_ap, xmid_ap, xout_ap, key) in enumerate(blk):
                sq = sqs[t]
                xt = xrot.next()
                k.load("sp", xt, xt.t[:], xmid_ap, r=[k.dbuf(("xmid", key))])
                P.op("dve", lambda e, sq=sq: e.tensor_reduce(out=sq.t[:, 8:9], in_=sq.t[:, 0:4], axis=AX.X, op=ALU.add),
                     r=[sq.b], w=[sq.b])
                k.act(sq.t[:, 9:10], sq.t[:, 8:9], AF.Sqrt, r=[sq.b], w=[sq.b], bias=EPS_T(k), scale=1.0 / D)
                k.recip(sq.t[:, 10:11], sq.t[:, 9:10], r=[sq.b], w=[sq.b])
                k.stt("dve", xt.t[:], fg.t[:, t, :], sq.t[:, 10:11], xt.t[:], ALU.mult, ALU.add, r=[fgb[t], sq.b, xt.b], w=[xt.b])
                k.store("sp", xt, xout_ap, xt.t[:])


def phase_post2(k, l, blocks, w_out_ap):
    tiles = [t for blk in blocks for t in blk]
    phase_P(k, l, tiles, w_out_ap)
    phase_M(k, l, blocks)


def post_blocks0(k):
    d = k.d
    def tile(bl, tt):
        rows = slice(tt * 128, (tt + 1) * 128)
        return (2 if tt < 2 else bl, d["yin"][bl][rows, :], token_src(k, 0, bl, tt), d["xmid"][bl][rows, :],
                d["xl1"][bl][rows, :], (bl, tt))
    blocks = [[tile(bl, tt) for bl in range(NB) for tt in range(2)]]
    for bl in range(NB):
        for b4 in range(4):
            blocks.append([tile(bl, 2 + b4 * 4 + i) for i in range(4)])
    return blocks


def build_hT(k, ph, l, bl, hT, hTb, ntr=2):
    A_c, B_c = mod_cols(k, ph, l, 2, 0, 0, 1, "c")
    A_l, B_l = mod_cols(k, ph, l, bl, 0, 0, 1, "l")
    xrot = Rot(ph.sb("x", [128, D], F32, n=3))
    pools = (Rot(ph.sb("junk", [128, D], BF16, n=1)), Rot(ph.sb("ss", [128, 4], F32, n=3)),
             Rot(ph.sb("xn", [128, D], BF16, n=3)), Rot(ph.ps("tr", [128, 4, 128], BF16, n=ntr)))
    prev = None
    for tt in range(NT + 1):
        cur = None
        if tt < NT:
            xt = xrot.next()
            k.load("sp", xt, xt.t[:], token_src(k, l, bl, tt))
            cur = (tt, norm_part(k, xt, pools))
        if prev is not None:
            pt_, xn = prev
            A, Bm = (A_c, B_c) if pt_ < 2 else (A_l, B_l)
            trans_part(k, xn, lambda kc, pt_=pt_: (hT.t[:, kc, pt_ * 128:(pt_ + 1) * 128], hTb[pt_]), A, Bm, k.ident, pools)
        prev = cur


def phase_inproj1(k, bl):
    d = k.d
    P = k.P
    with k.phase("ip1") as ph:
        hT = ph.sb("hT", [128, 16, NTOK], BF16)
        hTb = [Buf("hT%d" % i) for i in range(NT)]
        build_hT(k, ph, 1, bl, hT, hTb)
        cosT = ph.sb("cosT", [128, SEQ], F32)
        sinT = ph.sb("sinT", [128, SEQ], F32)
        ang = ph.sb("ang", [128, SEQ], F32)
        tmpT = ph.sb("tmpT", [128, SEQ], F32)
        pcol = ph.sb("pcol", [128, 4], F32)
        Lm = ph.sb("Lm", [128, 128], BF16)
        Rm = ph.sb("Rm", [128, 32], F32)
        Jm = ph.sb("Jm", [128, 32], F32)
        k.memset("pool", Rm.t[:], 0.0, w=[Rm.b])
        for i in range(4):
            P.op("pool", lambda e, i=i: e.affine_select(out=Rm.t[:], in_=Rm.t[:], pattern=[[-1, 32]], compare_op=ALU.not_equal,
                                                         fill=1.0, base=-32 * i, channel_multiplier=1), r=[Rm.b], w=[Rm.b])
        P.op("pool", lambda e: e.iota(Jm.t[:], [[1, 32]], base=0, channel_multiplier=0,
                                      allow_small_or_imprecise_dtypes=True), w=[Jm.b])
        k.tt("dve", Rm.t[:], Rm.t[:], Jm.t[:], ALU.mult, r=[Rm.b, Jm.b], w=[Rm.b])
        P.op("dve", lambda e: e.tensor_reduce(out=pcol.t[:, 1:2], in_=Rm.t[:], axis=AX.X, op=ALU.add), r=[Rm.b], w=[pcol.b])
        k.act(pcol.t[:, 2:3], pcol.t[:, 1:2], AF.Exp, r=[pcol.b], w=[pcol.b], scale=-math.log(10000.0) / 32.0)
        P.op("pool", lambda e: e.iota(ang.t[0:64, :].rearrange("p (a b) -> p a b", a=32), [[1, 32], [0, 64]], base=0,
                                      channel_multiplier=0, allow_small_or_imprecise_dtypes=True), w=[ang.b])
        P.op("pool", lambda e: e.iota(ang.t[64:128, :].rearrange("p (a b) -> p a b", a=32), [[0, 32], [1, 64]], base=0,
                                      channel_multiplier=0, allow_small_or_imprecise_dtypes=True), w=[ang.b])
        k.ts("dve", ang.t[:], ang.t[:], pcol.t[:, 2:3], None, ALU.mult, None, r=[ang.b, pcol.b], w=[ang.b])
        k.act(tmpT.t[:], ang.t[:], AF.Sin, r=[ang.b], w=[tmpT.b], scale=1.0 / 64)
        k.act(sinT.t[:], ang.t[:], AF.Sin, r=[ang.b], w=[sinT.b], scale=1.0 / 32)
        k.tt("dve", tmpT.t[:], tmpT.t[:], tmpT.t[:], ALU.mult, r=[tmpT.b], w=[tmpT.b])
        k.ts("dve", cosT.t[:], tmpT.t[:], -2.0, 1.0, ALU.mult, ALU.add, r=[tmpT.b], w=[cosT.b])
        for it in range(5):
            k.tt("dve", tmpT.t[:], sinT.t[:], sinT.t[:], ALU.mult, r=[sinT.b], w=[tmpT.b])
            k.stt("dve", sinT.t[:], sinT.t[:], 2.0, cosT.t[:], ALU.mult, ALU.mult, r=[sinT.b, cosT.b], w=[sinT.b])
            k.ts("dve", cosT.t[:], tmpT.t[:], -2.0, 1.0, ALU.mult, ALU.add, r=[tmpT.b], w=[cosT.b])
        k.memset("pool", Lm.t[:], 0.0, w=[Lm.b])
        for a in range(2):
            for half, val in ((0, -1.0), (1, 1.0)):
                c0 = a * 64 + half * 32
                off = 32 if half == 0 else -32
                P.op("pool", lambda e, c0=c0, off=off, val=val: e.affine_select(
                    out=Lm.t[:, c0:c0 + 32], in_=Lm.t[:, c0:c0 + 32], pattern=[[-1, 32]], compare_op=ALU.not_equal,
                    fill=val, base=-c0 - off, channel_multiplier=1), r=[Lm.b], w=[Lm.b])
        wrot = Rot(ph.sb("w", [128, 16, 512], BF16, n=2))
        prot = Rot(ph.ps("ps", [128, 512], F32, n=4))
        rrot = Rot(ph.ps("rp", [128, 512], F32, n=2))
        qrr = Rot(ph.sb("qr", [128, 512], BF16, n=3))
        t1r = Rot(ph.sb("t1", [128, 512], F32, n=2))
        t2r = Rot(ph.sb("t2", [128, 512], F32, n=2))
        orot = Rot(ph.sb("o", [128, 512], BF16, n=4))
        pending = [None]
        for which in ("q", "k"):
            for wb in range(4):
                wt = wrot.next()
                c0 = (0 if which == "q" else D) + wb * 512
                k.load("pool", wt, wt.t[:], d["da_w_in"][0][:, c0:c0 + 512].rearrange("(kc p) n -> p kc n", p=128))
                for j in range(4):
                    mc = wb * 4 + j
                    groups = [(256 + g * 512, 512, True) for g in range(4)]
                    if which == "k":
                        groups = [(0, 256, False)] + groups
                    for (t0, n, rope) in groups:
                        ps = prot.next()
                        for kc in range(16):
                            k.mm(ps.t[:, 0:n], wt.t[:, kc, j * 128:(j + 1) * 128], hT.t[:, kc, t0:t0 + n], kc == 0, kc == 15,
                                 r=[wt.b] + hTb, w=[ps.b])
                        ot = orot.next()
                        if not rope:
                            k.copy("act", ot.t[:, 0:n], ps.t[:, 0:n], r=[ps.b], w=[ot.b])
                            k.store("sp", ot, d["kT1"][bl][mc][:, t0:t0 + n], ot.t[:, 0:n])
                            continue
                        qr = qrr.next()
                        l0 = t0 - 256
                        k.copy("act", qr.t[:], ps.t[:], r=[ps.b], w=[qr.b])
                        if pending[0] is not None:
                            pending[0]()

                        def fin(ps=ps, qr=qr, ot=ot, l0=l0, t0=t0, mc=mc, which=which):
                            t1 = t1r.next(); t2 = t2r.next(); rp = rrot.next()
                            k.mm(rp.t[:], Lm.t[:], qr.t[:], True, True, r=[Lm.b, qr.b], w=[rp.b])
                            k.tt("dve", t1.t[:], ps.t[:], cosT.t[:, l0:l0 + 512], ALU.mult, r=[ps.b, cosT.b], w=[t1.b])
                            k.tt("dve", t2.t[:], rp.t[:], sinT.t[:, l0:l0 + 512], ALU.mult, r=[rp.b, sinT.b], w=[t2.b])
                            k.tt("pool", ot.t[:], t1.t[:], t2.t[:], ALU.add, r=[t1.b, t2.b], w=[ot.b])
                            if which == "q":
                                k.store("sp", ot, d["qT1"][bl][mc][:, l0:l0 + 512], ot.t[:])
                            else:
                                k.store("sp", ot, d["kT1"][bl][mc][:, t0:t0 + 512], ot.t[:])
                        pending[0] = fin
        if pending[0] is not None:
            pending[0]()
            pending[0] = None
        for wb in range(4):
            wt = wrot.next()
            c0 = 2 * D + wb * 512
            k.load("pool", wt, wt.t[:], d["da_w_in"][0][:, c0:c0 + 512].rearrange("(kc p) n -> p kc n", p=128))
            for tt in range(NT):
                ps = prot.next()
                for kc in range(16):
                    k.mm(ps.t[:], hT.t[:, kc, tt * 128:(tt + 1) * 128], wt.t[:, kc, :], kc == 0, kc == 15,
                         r=[hTb[tt], wt.b], w=[ps.b])
                ot = orot.next()
                k.copy(k.evac_eng(), ot.t[:], ps.t[:], r=[ps.b], w=[ot.b])
                k.store("sp", ot, d["v1"][bl][tt * 128:(tt + 1) * 128, wb * 512:(wb + 1) * 512], ot.t[:])


LAM_INIT1 = 0.8 - 0.6 * math.exp(-0.3 * 1)


def phase_attn1(k, bl):
    d = k.d
    P = k.P
    with k.phase("attn") as ph:
        lam = ph.sb("lam", [128, 8], F32)
        lq = ph.sb("lq", [128, 4, 128], F32)
        lj = ph.sb("lj", [128, 128], F32)
        k.load("sp", lq, lq.t[:], d["da_lambda"][0].rearrange("a b -> (a b)").partition_broadcast(128).rearrange("p (a b) -> p a b", a=4))
        for i in range(2):
            k.tt("dve", lj.t[:], lq.t[:, 2 * i, :], lq.t[:, 2 * i + 1, :], ALU.mult, r=[lq.b], w=[lj.b])
            P.op("dve", lambda e, i=i: e.tensor_reduce(out=lam.t[:, i:i + 1], in_=lj.t[:], axis=AX.X, op=ALU.add),
                 r=[lj.b], w=[lam.b])
        k.act(lam.t[:, 2:4], lam.t[:, 0:2], AF.Exp, r=[lam.b], w=[lam.b])
        k.tt("dve", lam.t[:, 4:5], lam.t[:, 2:3], lam.t[:, 3:4], ALU.subtract, r=[lam.b], w=[lam.b])
        k.ts("dve", lam.t[:, 5:6], lam.t[:, 4:5], LAM_INIT1, None, ALU.add, None, r=[lam.b], w=[lam.b])
        gs = ph.sb("gs", [128, 256], F32)
        k.load("sp", gs, gs.t[:], d["da_sub_g"][0].partition_broadcast(128))
        k.ts("dve", gs.t[:], gs.t[:], 1.0 - LAM_INIT1, None, ALU.mult, None, r=[gs.b], w=[gs.b])
        qtr = Rot(ph.sb("QT", [128, 2, SEQ], BF16, n=2))
        ktr = Rot(ph.sb("KT", [128, 2, NTOK], BF16, n=2))
        var = ph.sb("VA", [128, NT, 257], BF16, n=2)
        for v in var:
            k.memset("pool", v.t[:, :, 256:257], 1.0, w=[v.b])
        var = Rot(var)
        ptr_ = Rot(ph.sb("PT", [128, 512], BF16, n=4))
        t0r = Rot(ph.sb("t0", [128, 4, 256], F32, n=2))
        orr = Rot(ph.sb("oo", [128, 256], F32, n=8))
        yir = Rot(ph.sb("yi", [128, 256], BF16, n=4))
        rlr = Rot(ph.sb("rl", [128, 8], F32, n=10))
        jkr = Rot(ph.sb("jk", [128, 256], F32, n=2))
        st_ps = Rot(ph.ps("st", [128, 512], F32, n=4))
        o_ps = Rot(ph.ps("o", [128, 257], F32, n=4))
        scale = 128.0 ** -0.5
        def head_loads(h):
            QT = qtr.next(); KT = ktr.next(); VA = var.next()
            for c in range(2):
                k.load("sp", QT, QT.t[:, c, :], d["qT1"][bl][2 * h + c])
                k.load("sp", KT, KT.t[:, c, :], d["kT1"][bl][2 * h + c])
            k.load("sp", VA, VA.t[:, :, 0:256], d["v1"][bl][:, h * 256:(h + 1) * 256].rearrange("(kc p) e -> p kc e", p=128))
            return QT, KT, VA
        nxt_h = head_loads(0)
        for h in range(H):
            QT, KT, VA = nxt_h
            if h + 1 < H:
                nxt_h = head_loads(h + 1)
            for qb in range(4):
                t0 = t0r.next()
                for c in range(2):
                    os_ = [o_ps.next() for _ in range(4)]
                    sts = {}

                    def issue_s(kc, c=c, qb=qb):
                        st = st_ps.next()
                        k.mm(st.t[:], KT.t[:, c, kc * 128:(kc + 1) * 128], QT.t[:, c, qb * 512:(qb + 1) * 512], True, True,
                             r=[KT.b, QT.b], w=[st.b])
                        sts[kc] = st
                    issue_s(0)
                    issue_s(1)
                    for kc in range(NT):
                        if kc + 2 < NT:
                            issue_s(kc + 2)
                        st = sts.pop(kc)
                        pt = ptr_.next()
                        k.act(pt.t[:], st.t[:], AF.Exp, r=[st.b], w=[pt.b], scale=scale)
                        for qt in range(4):
                            k.mm(os_[qt].t[:], pt.t[:, qt * 128:(qt + 1) * 128], VA.t[:, kc, :], kc == 0, kc == NT - 1,
                                 r=[pt.b, VA.b], w=[os_[qt].b])
                    if c == 0:
                        for qt in range(4):
                            o = os_[qt]
                            rl = rlr.next()
                            k.recip(rl.t[:, 0:1], o.t[:, 256:257], r=[o.b], w=[rl.b])
                            k.ts("dve", t0.t[:, qt, :], o.t[:, 0:256], rl.t[:, 0:1], None, ALU.mult, None, r=[o.b, rl.b], w=[t0.b])
                    else:
                        ss4 = rlr.next()
                        oos = []
                        for qt in range(4):
                            o = os_[qt]
                            rl = rlr.next(); oo = orr.next(); jk = jkr.next()
                            oos.append(oo)
                            k.recip(rl.t[:, 0:1], o.t[:, 256:257], r=[o.b], w=[rl.b])
                            k.ts("dve", rl.t[:, 1:2], rl.t[:, 0:1], lam.t[:, 5:6], -1.0, ALU.mult, ALU.mult, r=[rl.b, lam.b], w=[rl.b])
                            k.stt("dve", oo.t[:], o.t[:, 0:256], rl.t[:, 1:2], t0.t[:, qt, :], ALU.mult, ALU.add,
                                  r=[o.b, rl.b, t0.b], w=[oo.b])
                            k.tt("pool", jk.t[:], oo.t[:], oo.t[:], ALU.mult, r=[oo.b], w=[jk.b])
                            P.op("dve", lambda e, jk=jk, ss4=ss4, qt=qt: e.tensor_reduce(out=ss4.t[:, qt:qt + 1], in_=jk.t[:], axis=AX.X, op=ALU.add),
                                 r=[jk.b], w=[ss4.b])
                        k.act(ss4.t[:, 4:8], ss4.t[:, 0:4], AF.Sqrt, r=[ss4.b], w=[ss4.b], bias=EPS_T(k), scale=1.0 / 256)
                        k.recip(ss4.t[:, 0:4], ss4.t[:, 4:8], r=[ss4.b], w=[ss4.b])
                        for qt in range(4):
                            yi = yir.next()
                            k.stt("dve", yi.t[:], oos[qt].t[:], ss4.t[:, qt:qt + 1], gs.t[:], ALU.mult, ALU.mult,
                                  r=[oos[qt].b, ss4.b, gs.b], w=[yi.b])
                            r0 = 256 + qb * 512 + qt * 128
                            k.store("sp", yi, d["yin"][bl][r0:r0 + 128, h * 256:(h + 1) * 256], yi.t[:])


def post_blocks1(k):
    d = k.d
    blocks = []
    for bl in range(NB):
        def tile(tt, bl=bl):
            rows = slice(tt * 128, (tt + 1) * 128)
            orow = slice((tt - 2) * 128, (tt - 1) * 128)
            return (bl, d["yin"][bl][rows, :], d["xl1"][bl][rows, :], d["xmid"][bl][rows, :],
                    d["out"][bl][orow, :], (bl, tt))
        for b4 in range(4):
            blocks.append([tile(2 + b4 * 4 + i) for i in range(4)])
    return blocks


def declare_dram(k):
    nc = k.nc
    need = getattr(k, "need_inputs", None)
    inp = [("x", [NB, SEQ, D]), ("ctx", [NB, CTX, D]), ("c", [NB, D]), ("c_ctx", [D]),
           ("ada_w", [2, D, 6 * D]), ("ada_b", [2, 6 * D]), ("norm_g", [2, 4, D]),
           ("mlp_w1", [2, D, DFF]), ("mlp_w2", [2, DFF, D]), ("ml_w_in", [1, D, ML_IN]),
           ("ml_b_gates", [1, 32]), ("ml_head_g", [1, D]), ("ml_w_out", [1, D, D]),
           ("da_w_in", [1, D, 3 * D]), ("da_lambda", [1, 4, 128]), ("da_sub_g", [1, 256]), ("da_w_out", [1, D, D])]
    for name, shape in inp:
        k.dram(name, shape, F32, kind="ExternalInput" if (need is None or name in need) else "Internal")
    k.dram("out", [NB, SEQ, D], F32, kind="ExternalOutput")
    k.dram("mod", [2, 3, 6 * D], F32)
    k.dram("w1b", [2, D, DFF], BF16)
    k.dram("w2b", [2, DFF, D], BF16)
    k.dram("qk0", [NB, NTOK, D], BF16)
    k.dram("v0", [NB, NTOK, D], BF16)
    k.dram("og0", [NB, NTOK, D], BF16)
    k.dram("gt0", [NB, NTOK, 32], F32)
    k.dram("hf0", [NB, NTOK, D], F32)
    k.dram("hb0", [NB, NTOK, D], F32)
    k.dram("yin", [NB, NTOK, D], BF16)
    k.dram("xmid", [NB, NTOK, D], F32)
    k.dram("hTd", [NB, NT, 128, 16 * 128], BF16)
    k.dram("xl1", [NB, NTOK, D], F32)
    k.dram("qT1", [NB, 16, 128, SEQ], BF16)
    k.dram("kT1", [NB, 16, 128, NTOK], BF16)
    k.dram("v1", [NB, NTOK, D], BF16)


def build(phases, dbg=(), ext_in=(), need_inputs=None):
    nc = bass.Bass("TRN2", target_bir_lowering=False)
    k = K(nc, dbg, ext_in)
    k.need_inputs = need_inputs
    declare_dram(k)
    setup_consts(k)
    for ph in phases:
        ph(k)
    k.P.emit()
    return nc, k


def all_phases(k):
    phase_ada(k)
    for bl in range(NB):
        phase_inproj0(k, bl)
        phase_scan0(k, bl, cast_thunks(k, bl))
    phase_post2(k, 0, post_blocks0(k), k.d["ml_w_out"][0])
    for bl in range(NB):
        phase_inproj1(k, bl)
        phase_attn1(k, bl)
    phase_post2(k, 1, post_blocks1(k), k.d["da_w_out"][0])


_NC_CACHE = {}


def kernel(**inputs):
    n = 8
    if "nc" not in _NC_CACHE:
        _NC_CACHE["nc"] = build([all_phases])[0]
    nc = _NC_CACHE["nc"]
    shared = {kk: np.ascontiguousarray(v, dtype=np.float32) for kk, v in inputs.items() if kk not in ("x", "c", "ctx")}
    in_maps = []
    for i in range(n):
        m = dict(shared)
        for kk in ("x", "c", "ctx"):
            m[kk] = np.ascontiguousarray(inputs[kk][i * NB:(i + 1) * NB], dtype=np.float32)
        in_maps.append(m)
    res = run_bass_kernel_spmd(nc, in_maps, core_ids=list(range(n)))
    return np.concatenate([r["out"] for r in res.results], axis=0).astype(np.float32)
```
